# Optimizing a Trainium2 kernel written in Bass

```python
import math
import jax, jax.numpy as jnp
from jax import lax
import numpy as np

D_MODEL = 1024
BATCH = 16
SEQ = 256
DEPTH = 1
DEC_BATCH = 8
DEC_SEQ = 2048
PAST_LEN = 512

GRID_W = 64
N_HEADS_A = 4
HEAD_K = 128
HEAD_V = 128
D_A = N_HEADS_A * HEAD_V
CONV_K = 5
CHUNK = 64
N_GROUPS_B = 4
POOL_WINDOWS = (2, 4, 8, 16)
D_B = 512
GROUP_B = D_B // N_GROUPS_B
D_FF = 2816
N_MOD = 6
EPS = 1e-6
D_IN = 4 * D_A + 4 * N_HEADS_A + D_B + 2 * D_MODEL

kernel_name = "hybrid_deltanet_pool_diffusion_step"


def _rmsnorm(x, g):
    xf = x.astype(jnp.float32)
    y = xf * lax.rsqrt(jnp.mean(xf * xf, axis=-1, keepdims=True) + EPS)
    return (y * g.astype(jnp.float32)).astype(x.dtype)


def _l2norm(x):
    xf = x.astype(jnp.float32)
    return xf * lax.rsqrt(jnp.sum(xf * xf, axis=-1, keepdims=True) + EPS)


def _short_conv(u, w):
    y = lax.conv_general_dilated(u, w[:, None, :].astype(u.dtype), window_strides=(1,),
                                 padding=[(CONV_K // 2, CONV_K // 2)],
                                 dimension_numbers=('NWC', 'WIO', 'NWC'),
                                 feature_group_count=u.shape[-1])
    return jax.nn.silu(y)


def _split_cols(proj):
    sizes = (3 * D_A, D_A, 2 * N_HEADS_A, 2 * N_HEADS_A, D_B, D_MODEL, D_MODEL)
    pts, acc = [], 0
    for s in sizes[:-1]:
        acc += s
        pts.append(acc)
    return jnp.split(proj, pts, axis=-1)


def _gated_delta_chunked(q, k, v, g, beta, s0):
    B, T, H, K = q.shape
    V = v.shape[-1]
    N = T // CHUNK

    def to_chunks(a):
        a = a.reshape((B, N, CHUNK) + a.shape[2:])
        return jnp.moveaxis(a, (1, 2), (0, 3))

    qc = to_chunks(q) * (K ** -0.5)
    kc = to_chunks(k)
    vc = to_chunks(v)
    bc = to_chunks(beta)
    gc = jnp.cumsum(to_chunks(g), axis=-1)
    idx = jnp.arange(CHUNK)
    causal = idx[:, None] >= idx[None, :]
    strict = idx[:, None] > idx[None, :]
    diff = gc[..., :, None] - gc[..., None, :]
    decay = jnp.where(causal, jnp.exp(jnp.where(causal, diff, 0.0)), 0.0)
    kb = kc * bc[..., None]
    lower = jnp.where(strict, jnp.einsum('nbhik,nbhjk->nbhij', kb, kc) * decay, 0.0)
    eye = jnp.eye(CHUNK, dtype=jnp.float32)
    t_inv = lax.linalg.triangular_solve(eye + lower, jnp.broadcast_to(eye, lower.shape),
                                        left_side=True, lower=True, unit_diagonal=True)
    u = jnp.einsum('nbhij,nbhjv->nbhiv', t_inv, vc * bc[..., None])
    w = jnp.einsum('nbhij,nbhjk->nbhik', t_inv, kb * jnp.exp(gc)[..., None])
    intra = jnp.where(causal, jnp.einsum('nbhik,nbhjk->nbhij', qc, kc) * decay, 0.0)
    q_dec = qc * jnp.exp(gc)[..., None]
    g_last = gc[..., -1]
    k_dec = kc * jnp.exp(g_last[..., None] - gc)[..., None]

    def step(S, inp):
        u_i, w_i, intra_i, qd_i, kd_i, gl_i = inp
        v_new = u_i - jnp.einsum('bhck,bhkv->bhcv', w_i, S)
        o_i = jnp.einsum('bhck,bhkv->bhcv', qd_i, S) + jnp.einsum('bhij,bhjv->bhiv', intra_i, v_new)
        S = S * jnp.exp(gl_i)[..., None, None] + jnp.einsum('bhck,bhcv->bhkv', kd_i, v_new)
        return S, o_i

    s_fin, o = lax.scan(step, s0, (u, w, intra, q_dec, k_dec, g_last))
    o = jnp.moveaxis(o, (0, 3), (1, 2)).reshape(B, T, H, V)
    return o, s_fin


def _delta_branch(qkv, z, a, b, s0, a_log, dt_bias, g_onorm):
    B, T, _ = qkv.shape
    q, k, v = jnp.split(qkv, 3, axis=-1)
    qh = _l2norm(q.reshape(B, T, N_HEADS_A, HEAD_K))
    kh = _l2norm(k.reshape(B, T, N_HEADS_A, HEAD_K))
    vh = v.reshape(B, T, N_HEADS_A, HEAD_V).astype(jnp.float32)
    a4 = a.reshape(B, T, 2, N_HEADS_A).astype(jnp.float32)
    b4 = b.reshape(B, T, 2, N_HEADS_A).astype(jnp.float32)
    g = -jnp.exp(a_log.astype(jnp.float32)) * jax.nn.softplus(a4 + dt_bias.astype(jnp.float32))
    beta = jax.nn.sigmoid(b4)
    s0f = s0.astype(jnp.float32)
    o_f, s_f = _gated_delta_chunked(qh, kh, vh, g[:, :, 0], beta[:, :, 0], s0f[:, 0])
    o_b, s_b = _gated_delta_chunked(qh[:, ::-1], kh[:, ::-1], vh[:, ::-1],
                                    g[:, ::-1, 1], beta[:, ::-1, 1], s0f[:, 1])
    o = o_f + o_b[:, ::-1]
    o = _rmsnorm(o, g_onorm) * jax.nn.silu(z.reshape(B, T, N_HEADS_A, HEAD_V).astype(jnp.float32))
    return o.reshape(B, T, D_A).astype(qkv.dtype), jnp.stack([s_f, s_b], axis=1)


def _window_mean(u, w, axis):
    n = u.shape[axis]
    cs = jnp.cumsum(u, axis=axis)
    cs = jnp.concatenate([jnp.zeros_like(lax.slice_in_dim(cs, 0, 1, axis=axis)), cs], axis=axis)
    t = jnp.arange(n)
    lo = jnp.clip(t - w // 2, 0, n)
    hi = jnp.clip(t + (w - w // 2), 0, n)
    total = jnp.take(cs, hi, axis=axis) - jnp.take(cs, lo, axis=axis)
    cnt_shape = [1] * u.ndim
    cnt_shape[axis] = n
    return total / (hi - lo).astype(u.dtype).reshape(cnt_shape)


def _pool_branch(u, w_pool, pool_scale, grid_w):
    B, T, _ = u.shape
    uf = u.astype(jnp.float32).reshape(B, T, N_GROUPS_B, GROUP_B)
    outs = []
    for gi, w in enumerate(POOL_WINDOWS):
        ug = uf[:, :, gi]
        if grid_w is None:
            m = _window_mean(ug, w, 1)
        else:
            rows = T // grid_w
            grid = ug.reshape(B, rows, grid_w, GROUP_B)
            m = _window_mean(_window_mean(grid, w, 2), w, 1).reshape(B, T, GROUP_B)
        outs.append(m - ug)
    d = jnp.stack(outs, axis=2)
    y = jnp.einsum('btgc,gcd->btgd', d, w_pool.astype(jnp.float32)).reshape(B, T, D_B)
    return (y * pool_scale.astype(jnp.float32)).astype(u.dtype)


def _layer(x, cvec, s0, grid_w, w_mod, b_mod, g_pre_mix, g_post_mix, g_pre_ffn, g_post_ffn,
           w_in, conv_w, a_log, dt_bias, g_onorm, w_a_proj, w_pool, pool_scale, w_b_proj, w_o,
           w_up, w_down):
    mod = (jax.nn.silu(cvec) @ w_mod + b_mod).reshape(cvec.shape[0], 1, N_MOD, D_MODEL)
    sh1, sc1, gt1, sh2, sc2, gt2 = (mod[:, :, i] for i in range(N_MOD))
    h = _rmsnorm(x, g_pre_mix) * (1.0 + sc1) + sh1
    qkv, z, a, b, u, gate_a, gate_b = _split_cols(h @ w_in)
    qkv = _short_conv(qkv, conv_w)
    y_a, s_fin = _delta_branch(qkv, z, a, b, s0, a_log, dt_bias, g_onorm)
    y_b = _pool_branch(u, w_pool, pool_scale, grid_w)
    merged = jax.nn.sigmoid(gate_a) * (y_a @ w_a_proj) + jax.nn.sigmoid(gate_b) * (y_b @ w_b_proj)
    x = x + gt1 * _rmsnorm(merged @ w_o, g_post_mix)
    h = _rmsnorm(x, g_pre_ffn) * (1.0 + sc2) + sh2
    gt, up = jnp.split(h @ w_up, 2, axis=-1)
    x = x + gt2 * _rmsnorm((jax.nn.silu(gt) * up) @ w_down, g_post_ffn)
    return x, s_fin


def setup_inputs(seed: int = 0) -> dict:
    key = jax.random.key(seed)
    ks = jax.random.split(key, 24)
    f32 = jnp.float32

    def nrm(k, shape, scale):
        return jax.random.normal(k, shape, f32) * scale

    dt = jnp.exp(jax.random.uniform(ks[14], (DEPTH, 2, N_HEADS_A), f32, math.log(1e-3), math.log(1e-1)))
    return {
        'x_prompt': nrm(ks[0], (BATCH, SEQ, D_MODEL), 1.0),
        'x_sample': nrm(ks[1], (DEC_BATCH, DEC_SEQ, D_MODEL), 1.0),
        'state_delta': nrm(ks[2], (DEC_BATCH, DEPTH, 2, N_HEADS_A, HEAD_K, HEAD_V), HEAD_K ** -0.5),
        'c': nrm(ks[3], (DEC_BATCH, D_MODEL), 1.0),
        'c_ctx': nrm(ks[4], (D_MODEL,), 1.0),
        'w_mod': nrm(ks[5], (DEPTH, D_MODEL, N_MOD * D_MODEL), 0.5 * D_MODEL ** -0.5),
        'b_mod': nrm(ks[6], (DEPTH, N_MOD * D_MODEL), 0.01),
        'g_pre_mix': 1.0 + nrm(ks[7], (DEPTH, D_MODEL), 0.02),
        'g_post_mix': 1.0 + nrm(ks[8], (DEPTH, D_MODEL), 0.02),
        'g_pre_ffn': 1.0 + nrm(ks[9], (DEPTH, D_MODEL), 0.02),
        'g_post_ffn': 1.0 + nrm(ks[10], (DEPTH, D_MODEL), 0.02),
        'w_in': nrm(ks[11], (DEPTH, D_MODEL, D_IN), D_MODEL ** -0.5),
        'conv_w': nrm(ks[12], (DEPTH, CONV_K, 3 * D_A), CONV_K ** -0.5),
        'a_log': jnp.log(jax.random.uniform(ks[13], (DEPTH, 2, N_HEADS_A), f32, 1.0, 16.0)),
        'dt_bias': dt + jnp.log(-jnp.expm1(-dt)),
        'g_onorm': 1.0 + nrm(ks[15], (DEPTH, HEAD_V), 0.02),
        'w_a_proj': nrm(ks[16], (DEPTH, D_A, D_MODEL), D_A ** -0.5),
        'w_pool': nrm(ks[17], (DEPTH, N_GROUPS_B, GROUP_B, GROUP_B), GROUP_B ** -0.5),
        'pool_scale': 1.0 + nrm(ks[18], (DEPTH, D_B), 0.02),
        'w_b_proj': nrm(ks[19], (DEPTH, D_B, D_MODEL), D_B ** -0.5),
        'w_o': nrm(ks[20], (DEPTH, D_MODEL, D_MODEL), D_MODEL ** -0.5),
        'w_up': nrm(ks[21], (DEPTH, D_MODEL, 2 * D_FF), D_MODEL ** -0.5),
        'w_down': nrm(ks[22], (DEPTH, D_FF, D_MODEL), D_FF ** -0.5),
    }


def reference(x_prompt, x_sample, state_delta, c, c_ctx, w_mod, b_mod, g_pre_mix, g_post_mix,
              g_pre_ffn, g_post_ffn, w_in, conv_w, a_log, dt_bias, g_onorm, w_a_proj, w_pool,
              pool_scale, w_b_proj, w_o, w_up, w_down):
    s_zero = jnp.zeros((x_prompt.shape[0], 2, N_HEADS_A, HEAD_K, HEAD_V), jnp.float32)
    y_prompt = x_prompt
    y_sample = x_sample
    ctx_states = []
    for l in range(DEPTH):
        params = (w_mod[l], b_mod[l], g_pre_mix[l], g_post_mix[l], g_pre_ffn[l], g_post_ffn[l],
                  w_in[l], conv_w[l], a_log[l], dt_bias[l], g_onorm[l], w_a_proj[l], w_pool[l],
                  pool_scale[l], w_b_proj[l], w_o[l], w_up[l], w_down[l])
        y_prompt, s_ctx = _layer(y_prompt, c_ctx[None, :], s_zero, None, *params)
        ctx_states.append(s_ctx)
        y_sample, _ = _layer(y_sample, c, state_delta[:, l], GRID_W, *params)
    new_state_delta = jnp.stack(ctx_states, axis=1).astype(x_prompt.dtype)
    return (y_prompt, y_sample, new_state_delta)
```

```python
import numpy as np
from contextlib import ExitStack
import concourse.bass as bass
import concourse.mybir as mybir
from concourse.bass_utils import run_bass_kernel_spmd

F32 = mybir.dt.float32
BF16 = mybir.dt.bfloat16
F32R = mybir.dt.float32r
AF = mybir.ActivationFunctionType
ALU = mybir.AluOpType

D = 1024
NH = 4
DA = 512
DFF = 2816
DIN = 4624
EPS = 1e-6
BIG = 30000.0
NQ = 8

ANNOTATE = False
INV_BF16_FROM = 5
NWP = 6
INV_DT = F32
SAME_ENGINE_RAW = True
POOL_ENG = "dve"
DEBUG_TAPS = None


class Op:
    __slots__ = ("eng", "fn", "deps", "signal", "sigval", "is_dma", "dsem", "dval", "idx", "tag")

    def __init__(self, eng, fn, is_dma, idx):
        self.eng = eng
        self.fn = fn
        self.deps = set()
        self.signal = False
        self.sigval = None
        self.is_dma = is_dma
        self.dsem = None
        self.dval = None
        self.idx = idx


class Buf:
    def __init__(self, name, off=0, size=0):
        self.name = name
        self.off = off
        self.size = size
        self.after = []
        self.keys = set()
        self.psum = off >= 10 ** 7

    def k(self, *sub):
        key = (self, sub)
        self.keys.add(key)
        return key


class Sched:
    def __init__(self, nc):
        self.nc = nc
        self.E = {"pe": nc.tensor, "act": nc.scalar, "dve": nc.vector, "pool": nc.gpsimd, "sp": nc.sync}
        self.ops = []
        self.last_w = {}
        self.readers = {}
        self.retired = []

    def _need(self, prod, eng, dma):
        return prod.is_dma or dma or prod.eng != eng

    def add(self, eng, fn, reads=(), writes=(), sreads=(), dma=False, after=()):
        pr = [k for k in reads if isinstance(k[0], Buf) and k[0].psum]
        if pr:
            reads = [k for k in reads if not (isinstance(k[0], Buf) and k[0].psum)]
            writes = list(writes) + pr
        op = Op(eng, fn, dma, len(self.ops))
        op.tag = None
        if ANNOTATE:
            import sys as _sys
            f = _sys._getframe(2)
            if f.f_code.co_name in ("act", "cp"):
                f = f.f_back
            op.tag = "%s:%d" % (f.f_code.co_name, f.f_lineno)
        deps = op.deps
        for k in reads:
            w = self.last_w.get(k)
            if w is not None and (self._need(w, eng, dma) or (SAME_ENGINE_RAW and eng != "pe")):
                deps.add(w)
        for k in sreads:
            w = self.last_w.get(k)
            if w is not None:
                deps.add(w)
        for k in writes:
            w = self.last_w.get(k)
            if w is not None and self._need(w, eng, dma):
                deps.add(w)
            rd = self.readers.get(k)
            if rd:
                for r in rd.values():
                    if self._need(r, eng, dma):
                        deps.add(r)
            b = k[0]
            if isinstance(b, Buf) and b.after:
                for r in b.after:
                    if self._need(r, eng, dma):
                        deps.add(r)
        for r in after:
            if self._need(r, eng, dma):
                deps.add(r)
        for d in deps:
            if not d.is_dma:
                d.signal = True
        rkey = ("dma", op.idx) if dma else eng
        for k in list(reads) + list(sreads):
            self.readers.setdefault(k, {})[rkey] = op
        for k in writes:
            self.last_w[k] = op
            self.readers[k] = {}
        self.ops.append(op)
        return op

    def collect(self, keys):
        best = {}
        out = []
        for k in keys:
            cands = []
            w = self.last_w.get(k)
            if w is not None:
                cands.append(w)
            rd = self.readers.get(k)
            if rd:
                cands.extend(rd.values())
            for o in cands:
                if o.is_dma:
                    out.append(o)
                else:
                    b = best.get(o.eng)
                    if b is None or o.idx > b.idx:
                        best[o.eng] = o
        return list(set(out)) + list(best.values())

    def alloc(self, name, off, size):
        b = Buf(name, off, size)
        for (o, s, tok) in self.retired:
            if o < off + size and off < o + s:
                b.after.extend(tok)
        return b

    def retire(self, buf):
        self.retired.append((buf.off, buf.size, self.collect(buf.keys)))

    def emit(self, stack):
        nc = self.nc
        sem = {e: stack.enter_context(nc.semaphore("s_" + e)) for e in self.E}
        dsem = {q: [stack.enter_context(nc.semaphore("d_%s%d" % (q, i))) for i in range(NQ)] for q in ("sp", "pool", "act")}
        dval = {q: [0] * NQ for q in dsem}
        dcount = {q: 0 for q in dsem}
        cnt = {e: 0 for e in self.E}
        waited = {}
        for op in self.ops:
            eo = self.E[op.eng]
            for d in sorted(op.deps, key=lambda o: o.idx):
                if d.is_dma:
                    s, v = d.dsem, d.dval
                else:
                    s, v = sem[d.eng], d.sigval
                wk = (op.eng, id(s))
                if waited.get(wk, 0) >= v:
                    continue
                eo.wait_ge(s, v)
                waited[wk] = v
            if op.is_dma:
                q = op.eng
                slot = dcount[q] % NQ
                dcount[q] += 1
                s = dsem[q][slot]
                prev = dval[q][slot]
                wk = (q, id(s))
                if prev > 0 and waited.get(wk, 0) < prev:
                    eo.wait_ge(s, prev)
                    waited[wk] = prev
                ins = op.fn()
                if op.tag:
                    ins.annotate(op.tag)
                ins.then_inc(s, 16)
                op.dsem = s
                op.dval = prev + 16
                dval[q][slot] = prev + 16
            else:
                if op.fn is None:
                    continue
                ins = op.fn()
                if op.tag:
                    ins.annotate(op.tag)
                if op.signal:
                    cnt[op.eng] += 1
                    ins.then_inc(sem[op.eng], 1)
                    op.sigval = cnt[op.eng]
        return cnt


POOL_W = (2, 4, 8, 16)


def _invcnt(n, w):
    t = np.arange(n)
    lo = np.clip(t - w // 2, 0, n)
    hi = np.clip(t + (w - w // 2), 0, n)
    return (1.0 / (hi - lo)).astype(np.float32)


def _build_consts():
    p = np.arange(128)[:, None]
    f = np.arange(128)[None, :]
    same = (p // 64) == (f // 64)
    cf = {}
    cf["IDENT"] = np.eye(128, dtype=np.float32)
    cf["TRI_F"] = (same & (p <= f)).astype(np.float32)
    cf["TRI_B"] = (same & (p >= f)).astype(np.float32)
    cf["STR_F"] = (same & (p > f)).astype(np.float32)
    cf["STR_B"] = (same & (p < f)).astype(np.float32)
    cf["IND0"] = np.broadcast_to((p < 64), (128, 128)).astype(np.float32)
    cf["IND1"] = np.broadcast_to((p >= 64), (128, 128)).astype(np.float32)
    cf["ONES"] = np.ones((128, 128), np.float32)
    cf["NEGONES"] = -np.ones((128, 128), np.float32)
    bm = np.zeros((128, 512), np.float32)
    for h in range(4):
        bm[h, h * 128:(h + 1) * 128] = 1.0
    cf["BM4"] = bm
    cf["NEG05"] = np.full((128, 1), -0.5, np.float32)
    for wi, w in enumerate(POOL_W):
        cf["IC64_%d" % wi] = np.broadcast_to(_invcnt(64, w)[None, :], (128, 64)).copy()
        cf["IC32_%d" % wi] = np.broadcast_to(_invcnt(32, w)[None, :], (128, 32)).copy()
        cf["IC256_%d" % wi] = np.broadcast_to(_invcnt(256, w)[None, :], (128, 256)).copy()
    cb = {}
    cb["IDENTB"] = np.eye(128, dtype=np.float32)
    cb["ONESB"] = np.ones((128, 128), np.float32)
    cb["NEG_SL"] = np.where(same & (p > f), 0.0, -BIG).astype(np.float32)
    cb["NEG_SU"] = np.where(same & (p < f), 0.0, -BIG).astype(np.float32)
    cb["NEG_IU"] = np.where(same & (p <= f), 0.0, -BIG).astype(np.float32)
    cb["NEG_IL"] = np.where(same & (p >= f), 0.0, -BIG).astype(np.float32)
    cb["C_F"] = (BIG * ((~(same & (p > f))).astype(np.float32) + (~(same & (p <= f))).astype(np.float32))).astype(np.float32)
    cb["C_B"] = (BIG * ((~(same & (p < f))).astype(np.float32) + (~(same & (p >= f))).astype(np.float32))).astype(np.float32)

    def pack(dct):
        offs = {}
        cols = []
        o = 0
        for k, v in dct.items():
            offs[k] = (o, v.shape[1])
            cols.append(v)
            o += v.shape[1]
        return offs, np.ascontiguousarray(np.concatenate(cols, axis=1))

    offf, arrf = pack(cf)
    offb, arrb = pack(cb)
    return offf, arrf, offb, arrb


CF_OFF, CF_ARR, CB_OFF, CB_ARR = _build_consts()


KB = 1024


class Prog:
    def __init__(self, stop_after=None, taps=None):
        self.stop_after = stop_after
        self.taps = taps
        self.nc = bass.Bass("TRN2", target_bir_lowering=False)
        self.s = Sched(self.nc)
        self.out_dmas = []

    def mm(self, out, lhsT, rhs, start, stop, reads, writes, after=(), skip=False):
        nc = self.nc
        if skip:
            return self.s.add("pe", lambda: nc.tensor.matmul(out, lhsT, rhs, start=start, stop=stop, skip_group_check=True), reads, writes, after=after)
        return self.s.add("pe", lambda: nc.tensor.matmul(out, lhsT, rhs, start=start, stop=stop), reads, writes, after=after)

    def tr(self, out, in_, ident, reads, writes):
        nc = self.nc
        return self.s.add("pe", lambda: nc.tensor.transpose(out, in_, ident), reads, writes)

    def act(self, out, in_, func, reads, writes, bias=None, scale=None, accum_out=None, sreads=()):
        nc = self.nc
        kw = {}
        if bias is not None:
            kw["bias"] = bias
        if scale is not None:
            kw["scale"] = scale
        if accum_out is not None:
            kw["accum_out"] = accum_out
        return self.s.add("act", lambda: nc.scalar.activation(out, in_, func, **kw), reads, writes, sreads)

    def _ve(self, eng):
        return self.nc.vector if eng == "dve" else self.nc.gpsimd

    def ts(self, eng, out, in0, s1, s2, op0, op1, reads, writes, sreads=()):
        e = self._ve(eng)
        if op1 is None:
            return self.s.add(eng, lambda: e.tensor_scalar(out, in0, s1, None, op0), reads, writes, sreads)
        return self.s.add(eng, lambda: e.tensor_scalar(out, in0, s1, s2, op0, op1), reads, writes, sreads)

    def tt(self, eng, out, in0, in1, op, reads, writes):
        e = self._ve(eng)
        return self.s.add(eng, lambda: e.tensor_tensor(out, in0, in1, op), reads, writes)

    def stt(self, eng, out, in0, scalar, in1, op0, op1, reads, writes, sreads=()):
        e = self._ve(eng)
        return self.s.add(eng, lambda: e.scalar_tensor_tensor(out, in0, scalar, in1, op0, op1), reads, writes, sreads)

    def cp(self, eng, out, in_, reads, writes):
        if eng == "act":
            return self.act(out, in_, AF.Copy, reads, writes)
        e = self._ve(eng)
        return self.s.add(eng, lambda: e.tensor_copy(out, in_), reads, writes)

    def memset(self, eng, out, val, writes):
        e = self._ve(eng)
        return self.s.add(eng, lambda: e.memset(out, val), (), writes)

    def dma(self, q, out, in_, reads, writes, noncontig=False):
        nc = self.nc
        eo = self.s.E[q]
        if noncontig:
            def fn():
                with nc.allow_non_contiguous_dma(reason="small strided"):
                    return eo.dma_start(out=out, in_=in_)
        else:
            def fn():
                return eo.dma_start(out=out, in_=in_)
        return self.s.add(q, fn, reads, writes, dma=True)

    def view(self, off_bytes, dtype, shape):
        nfree = 1
        for d in shape[1:]:
            nfree *= d
        esz = 4 if dtype in (F32, F32R) else 2
        nbytes = nfree * esz
        assert off_bytes % 4 == 0 and nbytes % 4 == 0, (off_bytes, nbytes)
        w0 = off_bytes // 4
        ap = self.arena[0:shape[0], w0:w0 + nbytes // 4]
        if dtype != F32:
            ap = ap.bitcast(dtype)
        if len(shape) == 2:
            return ap
        names = " ".join("d%d" % i for i in range(len(shape) - 1))
        kw = {"d%d" % i: shape[i + 1] for i in range(len(shape) - 1)}
        return ap.rearrange("p (%s) -> p %s" % (names, names), **kw)

    def tap(self, name, ap, shape, dtype=F32, reads=()):
        if self.taps is None or name in self.taps:
            return
        t = self.nc.dram_tensor("tap_" + name, list(shape), dtype, kind="ExternalOutput").ap()
        op = self.dma("sp", t, ap, reads, ())
        self.out_dmas.append(op)
        self.taps.append(name)

    def build(self):
        nc, s = self.nc, self.s
        stack = ExitStack()
        self.stack = stack

        def din(name, shape):
            return nc.dram_tensor(name, list(shape), F32, kind="ExternalInput").ap()

        def dout(name, shape):
            return nc.dram_tensor(name, list(shape), F32, kind="ExternalOutput").ap()

        I = {}
        I["xp"] = din("xp", [512, D])
        I["xs"] = din("xs", [2048, D])
        I["s0"] = din("s0", [8, 128, 128])
        I["cv"] = din("cv", [2, D])
        I["w_mod"] = din("w_mod", [D, 6 * D])
        I["b_mod"] = din("b_mod", [6 * D])
        for n in ("g_pre_mix", "g_post_mix", "g_pre_ffn", "g_post_ffn"):
            I[n] = din(n, [D])
        I["w_in"] = din("w_in", [D, DIN])
        I["conv_w"] = din("conv_w", [5, 1536])
        I["a_log"] = din("a_log", [8])
        I["dt_bias"] = din("dt_bias", [8])
        I["g_onorm"] = din("g_onorm", [128])
        I["w_a"] = din("w_a", [DA, D])
        I["w_pool"] = din("w_pool", [4, 128, 128])
        I["pool_scale"] = din("pool_scale", [512])
        I["w_b"] = din("w_b", [512, D])
        I["w_o"] = din("w_o", [D, D])
        I["w_up"] = din("w_up", [D, 2 * DFF])
        I["w_down"] = din("w_down", [DFF, D])
        I["constf"] = din("constf", list(CF_ARR.shape))
        I["constb"] = din("constb", list(CB_ARR.shape))
        O = {}
        O["yp"] = dout("yp", [512, D])
        O["ys"] = dout("ys", [2048, D])
        O["ns"] = dout("ns", [2, 8, 128, 128])
        self.I, self.O = I, O

        ARENA_KB = 195
        self.arena = stack.enter_context(nc.sbuf_tensor("arena", [128, ARENA_KB * 256], F32))
        self.FR = [stack.enter_context(nc.sbuf_tensor("fr%d" % i, [128, 1536], F32)) for i in range(2)]
        self.ps = [stack.enter_context(nc.psum_tensor("ps%d" % i, [128, 512], F32)) for i in range(8)]
        self.psb = [p[:, :].bitcast(BF16) for p in self.ps]
        self.PSB = [Buf("psbank%d" % i, 10 ** 7 + i, 1) for i in range(8)]
        self.bank_ctr = 0

        CB0 = 176 * KB
        o = CB0
        ncf = CF_ARR.shape[1]
        self.CFv = self.view(o, F32, [128, ncf]); o += ((ncf * 4 + 7) // 8) * 8
        ncb = CB_ARR.shape[1]
        self.CBv = self.view(o, BF16, [128, ncb]); o += ncb * 2
        self.VT = self.view(o, F32, [128, 12, 32]); o += 1536
        self.SCT = self.view(o, F32, [128, 8, 2]); o += 64
        self.MODT = self.view(o, F32, [128, 48, 2]); o += 384
        self.A1 = self.view(o, F32, [128, 8, 2]); o += 64
        self.A2 = self.view(o, F32, [128, 8, 2]); o += 64
        self.G1 = self.view(o, F32, [128, 8, 2]); o += 64
        self.G2 = self.view(o, F32, [128, 8, 2]); o += 64
        self.ALB = self.view(o, F32, [128, 8]); o += 32
        self.DTB = self.view(o, F32, [128, 8]); o += 32
        self.NEGA = self.view(o, F32, [128, 8]); o += 32
        self.WPOOL = self.view(o, BF16, [128, 4, 128]); o += 1024
        self.NWPOOL = self.view(o, BF16, [128, 4, 128]); o += 1024
        self.SS = self.view(o, F32, [128, 64]); o += 256
        assert o <= ARENA_KB * KB, o
        self.CONSTB = Buf("const", CB0, o - CB0)
        self.kC = self.CONSTB.k("c")
        self.ss_ctr = 0

        self.setup()
        groups = [
            dict(name="p", x=I["xp"], y=O["yp"], Tg=512, seqs=[(0, 256), (256, 256)], v=0, grid=False, s0=None, ns=[0, 1]),
            dict(name="s", x=I["xs"], y=O["ys"], Tg=2048, seqs=[(0, 2048)], v=1, grid=True, s0=I["s0"], ns=None),
        ]
        for g in groups:
            self.run_group(g)
            if self.stop_after is not None and self.stop_after[0] == g["name"]:
                break
        s.add("sp", None, after=self.out_dmas)
        s.emit(stack)
        stack.close()
        return nc

    def CF(self, name, rows=128):
        o, n = CF_OFF[name]
        return self.CFv[0:rows, o:o + n]

    def CBc(self, name, rows=128):
        o, n = CB_OFF[name]
        return self.CBv[0:rows, o:o + n]

    def next_bank(self, lo=0, hi=8):
        b = lo + (self.bank_ctr % (hi - lo))
        self.bank_ctr += 1
        return b

    def pk(self, b):
        return self.PSB[b].k("all")

    def ss_col(self):
        c = self.ss_ctr % 64
        self.ss_ctr += 1
        return c, self.CONSTB.k("ss", c)

    def setup(self):
        s, I = self.s, self.I
        kC = self.kC
        kCF, kCB = self.CONSTB.k("cf"), self.CONSTB.k("cb")
        self.dma("sp", self.CFv, I["constf"], (), [kCF])
        self.dma("pool", self.CBv, I["constb"], (), [kCB])
        nc_ = self.nc
        s.add("dve", lambda: nc_.vector.memset(self.SS[:, 63:64], 0.0), [kCF, kCB], [kC])
        self.dma("sp", self.ALB, I["a_log"].partition_broadcast(128), (), [self.CONSTB.k("alb")])
        self.dma("sp", self.DTB, I["dt_bias"].partition_broadcast(128), (), [self.CONSTB.k("dtb")])
        self.dma("pool", self.WPOOL, I["w_pool"].rearrange("g c d -> c g d"), (), [self.CONSTB.k("wpool")])
        self.ts("dve", self.NWPOOL, self.WPOOL, -1.0, None, ALU.mult, None, [self.CONSTB.k("wpool")], [self.CONSTB.k("nwpool")])
        SB = s.alloc("setup", 0, 64 * KB)
        kv = SB.k("vec")
        VEC = self.view(48 * KB, F32, [32, 1536])
        self.memset("dve", VEC, 0.0, [kv])
        rows = [
            (0, 2, 1024, I["cv"]),
            (2, 1, 1024, I["g_pre_mix"].rearrange("(o f) -> o f", o=1)),
            (3, 1, 1024, I["g_post_mix"].rearrange("(o f) -> o f", o=1)),
            (4, 1, 1024, I["g_pre_ffn"].rearrange("(o f) -> o f", o=1)),
            (5, 1, 1024, I["g_post_ffn"].rearrange("(o f) -> o f", o=1)),
            (6, 6, 1024, I["b_mod"].rearrange("(i f) -> i f", i=6)),
            (12, 5, 1536, I["conv_w"]),
            (17, 1, 512, I["pool_scale"].rearrange("(o f) -> o f", o=1)),
            (18, 1, 128, I["g_onorm"].rearrange("(o f) -> o f", o=1)),
        ]
        for (r0, nr, n, src) in rows:
            self.dma("sp", VEC[r0:r0 + nr, 0:n], src, (), [kv])
        b = self.next_bank()
        for c in range(12):
            self.tr(self.ps[b][:, c * 32:(c + 1) * 32], VEC[0:32, c * 128:(c + 1) * 128], self.CF("IDENT", 32)[:, 0:32], [kv, kC], [self.pk(b)])
        kVT = self.CONSTB.k("vt")
        self.kVT = kVT
        self.cp("dve", self.VT, self.ps[b][:, 0:384].rearrange("p (c r) -> p c r", c=12), [self.pk(b)], [kVT])
        kSCT = self.CONSTB.k("sct")
        self.act(self.SCT, self.VT[:, 0:8, 0:2], AF.Silu, [kVT], [kSCT])
        kNEGA = self.CONSTB.k("nega")
        self.act(self.NEGA, self.ALB, AF.Exp, [self.CONSTB.k("alb")], [kNEGA])
        self.ts("dve", self.NEGA, self.NEGA, -1.0, None, ALU.mult, None, [kNEGA], [kNEGA])
        SCB = self.view(54 * KB, BF16, [128, 8, 2])
        kSCB = SB.k("scb")
        self.cp("dve", SCB, self.SCT, [kSCT], [kSCB])
        MROW = self.view(55 * KB, F32, [2, 6 * D])
        kMR = SB.k("mrow")
        w_mod_v = I["w_mod"].rearrange("(k p) c -> p k c", p=128)
        WMP = [self.view(i * 8 * KB, BF16, [128, 8, 512]) for i in range(4)]
        for ct in range(12):
            kw = SB.k("wmp", ct % 4)
            self.dma("pool", WMP[ct % 4], w_mod_v[:, :, ct * 512:(ct + 1) * 512], (), [kw])
            b = self.next_bank()
            for kc in range(8):
                self.mm(self.ps[b][0:2, 0:512], SCB[:, kc, :], WMP[ct % 4][:, kc, :], kc == 0, kc == 7, [kw, kSCB], [self.pk(b)])
            self.cp("act" if ct % 2 else "dve", MROW[:, ct * 512:(ct + 1) * 512], self.ps[b][0:2, 0:512], [self.pk(b)], [kMR])
        bm = self.next_bank()
        for oc in range(48):
            self.tr(self.ps[bm][:, 2 * oc:2 * oc + 2], MROW[0:2, oc * 128:(oc + 1) * 128], self.CF("IDENT", 2)[:, 0:2], [kMR, kC], [self.pk(bm)])
        kMOD = self.CONSTB.k("modt")
        for i in range(6):
            self.tt("dve", self.MODT[:, i * 8:(i + 1) * 8, :], self.ps[bm][:, i * 16:(i + 1) * 16].rearrange("p (k v) -> p k v", v=2),
                    self.VT[:, 0:8, 6 + i:7 + i].to_broadcast([128, 8, 2]), ALU.add, [self.pk(bm), kVT], [kMOD])
        kD = self.CONSTB.k("derived")
        self.kD = kD

        def vcol(j):
            return self.VT[:, 0:8, j:j + 1].to_broadcast([128, 8, 2])

        self.stt("dve", self.A1, self.MODT[:, 8:16, :], 1.0, vcol(2), ALU.add, ALU.mult, [kMOD, kVT], [kD])
        self.stt("dve", self.A2, self.MODT[:, 32:40, :], 1.0, vcol(4), ALU.add, ALU.mult, [kMOD, kVT], [kD])
        self.tt("dve", self.G1, self.MODT[:, 16:24, :], vcol(3), ALU.mult, [kMOD, kVT], [kD])
        self.tt("dve", self.G2, self.MODT[:, 40:48, :], vcol(5), ALU.mult, [kMOD, kVT], [kD])
        s.retire(SB)
        self.tap("modt", self.MODT.rearrange("p a b -> p (a b)"), [128, 96], reads=[kMOD])
        self.tap("vt", self.VT.rearrange("p a b -> p (a b)"), [128, 384], reads=[kVT])

    def run_group(self, g):
        st = self.stop_after[1] if (self.stop_after is not None and self.stop_after[0] == g["name"]) else None
        self.phase_ht(g)
        if st == "ht":
            return
        self.phase_proj1(g)
        if st is not None and (st == "proj1" or st.startswith("p1_")):
            return
        self.phase_dn(g)
        if st == "dn":
            return
        self.phase_m1(g)
        if st == "m1":
            return
        self.phase_m2(g)
        if st == "m2":
            return
        self.phase_ffn(g)

    def rstd_ops(self, c, kss, n_inv, kout):
        SS = self.SS
        self.ts("pool", SS[:, c:c + 1], SS[:, c:c + 1], n_inv, EPS, ALU.mult, ALU.add, [kss], [kss])
        self.tt("pool", SS[:, c:c + 1], SS[:, c:c + 1], self.CF("NEG05"), ALU.pow, [kss, self.kC], [kout])

    def norm_stage_a(self, src, src_keys, XN, kXN):
        c, kss = self.ss_col()
        self.act(XN, src, AF.Square, src_keys, [kXN, kss], accum_out=self.SS[:, c:c + 1])
        self.rstd_ops(c, kss, 1.0 / D, kss)
        self.ts("dve", XN, src, self.SS[:, c:c + 1], None, ALU.mult, None, src_keys, [kXN], sreads=[kss])

    def norm_stage_b(self, XN, kXN, dst, kdst, A, B):
        b = self.next_bank()
        for k in range(8):
            self.tr(self.psb[b][:, k * 128:(k + 1) * 128], XN[:, k * 128:(k + 1) * 128], self.CBc("IDENTB"), [kXN, self.kC], [self.pk(b)])
        for k in range(8):
            if k % 2 == 0:
                self.act(dst[:, k, :], self.psb[b][:, k * 128:(k + 1) * 128], AF.Identity, [self.pk(b)], [kdst],
                         scale=A[:, k:k + 1], bias=B[:, k:k + 1], sreads=[self.kD])
            else:
                self.ts("dve", dst[:, k, :], self.psb[b][:, k * 128:(k + 1) * 128], A[:, k:k + 1], B[:, k:k + 1], ALU.mult, ALU.add,
                        [self.pk(b)], [kdst], sreads=[self.kD])

    def norm_transpose_block(self, src, src_keys, XN, kXN, dst, kdst, A, B, bufafter=()):
        self.norm_stage_a(src, src_keys, XN, kXN)
        self.norm_stage_b(XN, kXN, dst, kdst, A, B)

    def phase_ht(self, g):
        s = self.s
        Tg, v = g["Tg"], g["v"]
        NB = Tg // 128
        HTb = s.alloc("HT", 0, 32 * KB)
        HT = self.view(0, BF16, [128, 8, 2048])
        NS_ = 6
        LA_ = 5
        XLb = s.alloc("XL", 112 * KB, NS_ * 4 * KB)
        XNb = s.alloc("XN", 136 * KB, NS_ * 2 * KB)
        XL = [self.view(112 * KB + i * 4 * KB, F32, [128, D]) for i in range(NS_)]
        XN = [self.view(136 * KB + i * 2 * KB, BF16, [128, D]) for i in range(NS_)]
        A = self.A1[:, :, v]
        B = self.MODT[:, 0:8, v]

        def stage_a(blk):
            sl = blk % NS_
            kx = XLb.k(sl)
            self.dma("sp", XL[sl], g["x"][blk * 128:(blk + 1) * 128, :], (), [kx])
            self.norm_stage_a(XL[sl], [kx], XN[sl], XNb.k(sl))

        for blk in range(min(LA_, NB)):
            stage_a(blk)
        for blk in range(NB):
            if blk + LA_ < NB:
                stage_a(blk + LA_)
            sl = blk % NS_
            self.norm_stage_b(XN[sl], XNb.k(sl), HT[:, :, blk * 128:(blk + 1) * 128], HTb.k(blk), A, B)
        s.retire(XLb)
        s.retire(XNb)
        g["HTb"], g["HT"] = HTb, HT
        if self.taps is not None and g["name"] == "p":
            self.tap("ht_" + g["name"], HT[:, :, 0:Tg].rearrange("p a b -> p (a b)") if Tg == 2048 else HT[:, 0, 0:Tg], [128, 8 * Tg if Tg == 2048 else Tg],
                     dtype=BF16, reads=[HTb.k(b) for b in range(NB)])

    def wp_views(self):
        if not hasattr(self, "WPb"):
            self.WPb = Buf("WP", 152 * KB, 24 * KB)
            self.WPv = [self.view(152 * KB + i * 4 * KB, BF16, [128, 8, 256]) for i in range(NWP)]
            self.wp_ctr = 0
        return self.WPb, self.WPv

    def wp_load(self, parts):
        WPb, WPv = self.wp_views()
        sl = self.wp_ctr % NWP
        self.wp_ctr += 1
        for (kc0, nk, n, src) in parts:
            self.dma("pool", WPv[sl][:, kc0:kc0 + nk, 0:n], src, (), [WPb.k(sl)])
        return sl

    def seg_iter(self, g, t0, n):
        out = []
        for si, (so, T) in enumerate(g["seqs"]):
            a = max(t0, so)
            b = min(t0 + n, so + T)
            if a < b:
                out.append((si, a - so, a - t0, b - a))
        return out

    def phase_proj1(self, g):
        s, I = self.s, self.I
        Tg, v = g["Tg"], g["v"]
        NB = Tg // 128
        NT = Tg // 512
        HT, HTb = g["HT"], g["HTb"]
        seqs = g["seqs"]
        nseq, T = len(seqs), seqs[0][1]
        WPb, WPv = self.wp_views()
        w_in_v = I["w_in"].rearrange("(k p) c -> p k c", p=128)
        panels = [("q0", 0, 256), ("q1", 256, 256), ("k0", 512, 256), ("k1", 768, 256), ("v0", 1024, 256), ("v1", 1280, 256),
                  ("z0", 1536, 256), ("z1", 1792, 256), ("ab", 2048, 16), ("u0", 2064, 256), ("u1", 2320, 256)]
        issued = [0]
        slot_of = {}

        def issue_upto(i):
            while issued[0] <= min(i, len(panels) - 1):
                j = issued[0]
                _, c0, n = panels[j]
                slot_of[j] = self.wp_load([(0, 8, n, w_in_v[:, :, c0:c0 + n])])
                issued[0] += 1

        QKVb = s.alloc("QKV", 64 * KB, 48 * KB)
        QKV = self.view(64 * KB, BF16, [128, 12, 2048])
        ZSb = s.alloc("ZS", 32 * KB, 16 * KB)
        ZS = self.view(32 * KB, BF16, [128, 4, 2048])
        YBb = s.alloc("YB", 48 * KB, 16 * KB)
        YB = self.view(48 * KB, BF16, [128, 4, 2048])
        GTS = self.view(148 * KB, F32, [128, 16, 6, 8])
        EGL = self.view(151 * KB, F32, [128, 16, 16])
        g.update(QKVb=QKVb, QKV=QKV, ZSb=ZSb, ZS=ZS, YBb=YBb, YB=YB, GTS=GTS, EGL=EGL)
        RAWb = s.alloc("RAW", 124 * KB, 10 * KB)
        RAW = [self.view(124 * KB + i * 4608, BF16, [128, nseq, T + 4]) for i in range(2)]
        CDb = s.alloc("CONVD", 148 * KB, 2560)
        CD = [self.view(148 * KB + i * 1280, BF16, [128, 5, 128]) for i in range(2)]
        kC, kVT = self.kC, self.kVT
        for i in range(2):
            self.memset("pool", RAW[i], 0.0, [RAWb.k(i)])
        ht_keys = lambda t0, n: [HTb.k(b) for b in range(t0 // 128, (t0 + n) // 128)]

        def proj_tile(piece0, oc, t0, bank, m=128):
            sl = slot_of[piece0 + oc // 2]
            for kc in range(8):
                self.mm(self.ps[bank][0:m, 0:512], WPv[sl][:, kc, (oc % 2) * 128:(oc % 2) * 128 + m], HT[:, kc, t0:t0 + 512], kc == 0, kc == 7,
                        [WPb.k(sl)] + ht_keys(t0, 512), [self.pk(bank)])

        LOOK = NWP - 2
        issue_upto(LOOK)
        def qkv_proj(ch):
            pi, oc = ch // 4, ch % 4
            issue_upto(2 * pi + oc // 2 + LOOK)
            rs = ch % 2
            for j in range(5):
                self.ts("pool" if j % 2 else "dve", CD[rs][:, j, :], self.CBc("IDENTB"), self.VT[:, ch, 12 + j:13 + j], None, ALU.mult, None,
                        [kC], [CDb.k(rs)], sreads=[kVT])
            for tt in range(NT):
                t0 = tt * 512
                bank = self.next_bank()
                proj_tile(2 * pi, oc, t0, bank)
                for (si, ts_, ot, n) in self.seg_iter(g, t0, 512):
                    self.cp("act" if (tt + si) % 2 else "dve", RAW[rs][:, si, 2 + ts_:2 + ts_ + n], self.ps[bank][:, ot:ot + n],
                            [self.pk(bank)], [RAWb.k(rs)])

        def qkv_conv(ch):
            rs = ch % 2
            for tt in range(NT):
                t0 = tt * 512
                bank = self.next_bank()
                for (si, ts_, ot, n) in self.seg_iter(g, t0, 512):
                    for j in range(5):
                        self.mm(self.ps[bank][:, ot:ot + n], CD[rs][:, j, :], RAW[rs][:, si, ts_ + j:ts_ + j + n], j == 0, j == 4,
                                [CDb.k(rs), RAWb.k(rs)], [self.pk(bank)])
                self.act(QKV[:, ch, t0:t0 + 512], self.ps[bank][:, 0:512], AF.Silu, [self.pk(bank)],
                         [QKVb.k(ch, b) for b in range(t0 // 128, t0 // 128 + 4)])

        SQb = s.alloc("SQ", 112 * KB, 2 * KB)
        SQ = [self.view(112 * KB + i * KB, BF16, [128, 512]) for i in range(2)]
        RIb = s.alloc("RINV", 144 * KB, 4 * KB)
        RI = [self.view(144 * KB + i * 2 * KB, F32, [128, 512]) for i in range(2)]
        import math
        l2it = [0]

        def l2norm_chunk(ch):
            lnsc = math.log(128 ** -0.5) if ch < 4 else 0.0
            for tt in range(NT):
                t0 = tt * 512
                r = l2it[0] % 2
                l2it[0] += 1
                qk = [QKVb.k(ch, b) for b in range(t0 // 128, t0 // 128 + 4)]
                qv = QKV[:, ch, t0:t0 + 512]
                self.tt("dve", SQ[r], qv, qv, ALU.mult, qk, [SQb.k(r)])
                bank = self.next_bank()
                self.mm(self.ps[bank][:, 0:512], self.CBc("ONESB"), SQ[r], True, True, [SQb.k(r), kC], [self.pk(bank)])
                self.act(RI[r], self.ps[bank][:, 0:512], AF.Ln, [self.pk(bank)], [RIb.k(r)], bias=EPS)
                self.act(RI[r], RI[r], AF.Exp, [RIb.k(r)], [RIb.k(r)], scale=-0.5, bias=lnsc)
                self.tt("dve", qv, qv, RI[r], ALU.mult, qk + [RIb.k(r)], qk)

        qkv_proj(0)
        for ch in range(12):
            if ch + 1 < 12:
                qkv_proj(ch + 1)
            qkv_conv(ch)
            if 1 <= ch <= 8:
                l2norm_chunk(ch - 1)
        s.retire(SQb)
        s.retire(RIb)
        s.retire(RAWb)
        s.retire(CDb)
        sub = self.stop_after[1] if (self.stop_after is not None and self.stop_after[0] == g["name"]) else None
        if sub == "p1_qkv":
            return
        for oc in range(4):
            issue_upto(6 + oc // 2 + LOOK)
            for tt in range(NT):
                t0 = tt * 512
                bank = self.next_bank()
                proj_tile(6, oc, t0, bank)
                self.act(ZS[:, oc, t0:t0 + 512], self.ps[bank][:, 0:512], AF.Silu, [self.pk(bank)],
                         [ZSb.k(b) for b in range(t0 // 128, t0 // 128 + 4)])
        if sub == "p1_z":
            return
        issue_upto(10)
        sl = slot_of[8]
        GTSb = s.alloc("GTS", 148 * KB, 4 * KB)
        g["GTSb"] = GTSb
        GTb = s.alloc("GTMP", 114 * KB, 4 * KB)
        GT = self.view(114 * KB, F32, [128, 16, 48])[:, 0:NB, :]
        kalb, kdtb, knega = self.CONSTB.k("alb"), self.CONSTB.k("dtb"), self.CONSTB.k("nega")
        kg = GTb.k("a")
        kp6, kp7 = self.pk(6), self.pk(7)
        allg = [GTSb.k(b) for b in range(NB)]
        alle = [GTSb.k("egl", b) for b in range(NB)]
        for blk in range(NB):
            for kc in range(8):
                self.mm(self.ps[6][:, blk * 16:(blk + 1) * 16], HT[:, kc, blk * 128:(blk + 1) * 128], WPv[sl][:, kc, 0:16], kc == 0, kc == 7,
                        [WPb.k(sl), HTb.k(blk)], [kp6])
        pv = self.ps[6][:, 0:NB * 16].rearrange("p (b c) -> p b c", c=16)
        psA, psB = pv[:, :, 0:8], pv[:, :, 8:16]
        bc = lambda t_: t_.unsqueeze(1).to_broadcast([128, NB, 8])
        GS = lambda slot: GTS[:, 0:NB, slot, :]
        self.tt("dve", GT[:, :, 0:8], psA, bc(self.DTB), ALU.add, [kp6, kdtb], [kg])
        self.act(GT[:, :, 24:32], psB, AF.Exp, [kp6], [kg], scale=-1.0)
        self.act(GT[:, :, 8:16], GT[:, :, 0:8], AF.Exp, [kg], [kg])
        self.act(GT[:, :, 16:24], GT[:, :, 8:16], AF.Ln, [kg], [kg], bias=1.0)
        self.tt("dve", GS(5), GT[:, :, 16:24], bc(self.NEGA), ALU.mult, [kg, knega], allg)
        self.act(GT[:, :, 32:40], GT[:, :, 24:32], AF.Ln, [kg], [kg], bias=1.0)
        self.act(GS(0), GT[:, :, 32:40], AF.Exp, [kg], allg, scale=-1.0)
        for blk in range(NB):
            gtok = GTS[:, blk, 5, :]
            c0 = blk * 32
            for (cc, nm, lo) in ((0, "TRI_F", 0), (4, "TRI_B", 4), (8, "STR_F", 0), (12, "STR_B", 4)):
                self.mm(self.ps[7][:, c0 + cc:c0 + cc + 4], self.CF(nm), gtok[:, lo:lo + 4], True, True, [GTSb.k(blk), kC], [kp7])
            self.mm(self.ps[7][:, c0 + 16:c0 + 24], self.CF("IND0"), gtok, True, True, [GTSb.k(blk), kC], [kp7])
            self.mm(self.ps[7][:, c0 + 24:c0 + 32], self.CF("IND1"), gtok, True, True, [GTSb.k(blk), kC], [kp7])
        cv = self.ps[7][:, 0:NB * 32].rearrange("p (b c) -> p b c", c=32)
        self.tt("dve", GS(1), cv[:, :, 0:8], GT[:, :, 32:40], ALU.subtract, [kp7, kg], allg)
        self.ts("dve", GS(2), cv[:, :, 0:8], -1.0, None, ALU.mult, None, [kp7], allg)
        self.act(GT[:, :, 40:48], cv[:, :, 0:8], AF.Exp, [kp7], [kg])
        self.act(GS(4), cv[:, :, 8:16], AF.Exp, [kp7], allg)
        self.act(EGL[:, 0:NB, :], cv[:, :, 16:32], AF.Exp, [kp7], alle)
        self.tt("dve", GS(3), GT[:, :, 40:48], GS(0), ALU.mult, [kg] + allg, allg)
        s.retire(GTb)
        if sub == "p1_ab":
            return
        PE_ = POOL_ENG
        PAb = s.alloc("PA", 112 * KB, 12 * KB)
        PBb = s.alloc("PB", 124 * KB, 12 * KB)
        UBb = s.alloc("UB", 136 * KB, 4 * KB)
        MBb = s.alloc("MB", 140 * KB, 4 * KB)
        PAf = self.view(112 * KB, F32, [128, 3072])
        PBf = self.view(124 * KB, F32, [128, 3072])
        UB = self.view(136 * KB, BF16, [128, 2048])
        MB = self.view(140 * KB, BF16, [128, 2048])
        if g["grid"]:
            nrows, rowlen = 32, 64
            icn = "IC64_%d"
        else:
            nrows, rowlen = nseq, T
            icn = "IC256_%d"
        kPA, kPB = PAb.k("a"), PBb.k("a")
        for gi in range(4):
            w = POOL_W[gi]
            L1 = nrows * (rowlen + w)
            pa3 = PAf[:, 0:L1].rearrange("p (r c) -> p r c", r=nrows)
            self.memset(PE_, PAf[:, 0:L1], 0.0, [kPA])
            for tt in range(NT):
                t0 = tt * 512
                bank = self.next_bank()
                proj_tile(9, gi, t0, bank)
                if sub == "p1_ua":
                    return
                r0 = t0 // rowlen
                nr = 512 // rowlen
                self.cp("act", pa3[:, r0:r0 + nr, w // 2:w // 2 + rowlen], self.ps[bank][:, 0:512].rearrange("p (r c) -> p r c", r=nr),
                        [self.pk(bank)], [kPA])
                if sub == "p1_ub":
                    return
                self.cp("dve", UB[:, t0:t0 + 512], self.ps[bank][:, 0:512], [self.pk(bank)], [UBb.k(tt)])
                if sub == "p1_uc":
                    return
            cur, ck, oth, ok = PAf, kPA, PBf, kPB
            kk = 1
            while kk < w:
                no = L1 - (2 * kk - 1)
                self.tt(PE_, oth[:, 0:no], cur[:, 0:no], cur[:, kk:kk + no], ALU.add, [ck], [ok])
                cur, ck, oth, ok = oth, ok, cur, ck
                kk *= 2
            if sub == "p1_ud":
                return
            cur3 = cur[:, 0:L1].rearrange("p (r c) -> p r c", r=nrows)
            ic1 = self.CF(icn % gi)
            if not g["grid"]:
                self.tt(PE_, MB[:, 0:Tg].rearrange("p (r c) -> p r c", r=nrows), cur3[:, :, 0:rowlen],
                        ic1.unsqueeze(1).to_broadcast([128, nrows, rowlen]), ALU.mult, [ck, kC], [MBb.k("a")])
            else:
                L2 = (nrows + w) * rowlen
                self.memset(PE_, oth[:, 0:L2], 0.0, [ok])
                o3 = oth[:, 0:L2].rearrange("p (r c) -> p r c", c=rowlen)
                self.tt(PE_, o3[:, w // 2:w // 2 + nrows, :], cur3[:, :, 0:rowlen],
                        ic1.unsqueeze(1).to_broadcast([128, nrows, rowlen]), ALU.mult, [ck, kC], [ok])
                cur, ck, oth, ok = oth, ok, cur, ck
                kk = 1
                while kk < w:
                    sh = kk * rowlen
                    no = L2 - (2 * kk - 1) * rowlen
                    self.tt(PE_, oth[:, 0:no], cur[:, 0:no], cur[:, sh:sh + no], ALU.add, [ck], [ok])
                    cur, ck, oth, ok = oth, ok, cur, ck
                    kk *= 2
                c3 = cur[:, 0:L2].rearrange("p (r c) -> p r c", c=rowlen)
                ic2 = self.CF("IC32_%d" % gi)
                self.tt(PE_, MB[:, 0:Tg].rearrange("p (r c) -> p r c", c=rowlen), c3[:, 0:nrows, :],
                        ic2.unsqueeze(2).to_broadcast([128, nrows, rowlen]), ALU.mult, [ck, kC], [MBb.k("a")])
            if sub == "p1_u1":
                return
            for tt in range(NT):
                t0 = tt * 512
                bank = self.next_bank()
                self.mm(self.ps[bank][:, 0:512], self.WPOOL[:, gi, :], MB[:, t0:t0 + 512], True, False,
                        [MBb.k("a"), self.CONSTB.k("wpool")], [self.pk(bank)])
                self.mm(self.ps[bank][:, 0:512], self.NWPOOL[:, gi, :], UB[:, t0:t0 + 512], False, True,
                        [UBb.k(tt), self.CONSTB.k("nwpool")], [self.pk(bank)])
                self.ts("dve", YB[:, gi, t0:t0 + 512], self.ps[bank][:, 0:512], self.VT[:, gi, 17:18], None, ALU.mult, None,
                        [self.pk(bank)], [YBb.k(b) for b in range(t0 // 128, t0 // 128 + 4)], sreads=[kVT])
        for b_ in (PAb, PBb, UBb, MBb):
            s.retire(b_)
        s.retire(HTb)
        if self.taps is not None:
            nm = g["name"]
            allq = [QKVb.k(ch, b) for ch in range(12) for b in range(NB)]
            for ch in (0, 5, 9):
                self.tap("qkv%d_%s" % (ch, nm), QKV[:, ch, 0:Tg], [128, Tg], dtype=BF16, reads=allq)
            self.tap("zs1_" + nm, ZS[:, 1, 0:Tg], [128, Tg], dtype=BF16, reads=[ZSb.k(b) for b in range(NB)])
            for gi in (0, 3):
                self.tap("yb%d_%s" % (gi, nm), YB[:, gi, 0:Tg], [128, Tg], dtype=BF16, reads=[YBb.k(b) for b in range(NB)])
            self.tap("gts_" + nm, GTS[:, 0:NB, :, :].rearrange("p a b c -> p (a b c)"), [128, NB * 48], reads=[GTSb.k(b) for b in range(NB)])
            self.tap("egl_" + nm, EGL[:, 0:NB, :].rearrange("p a b -> p (a b)"), [128, NB * 16], reads=[GTSb.k("egl", b) for b in range(NB)])

    def phase_dn(self, g):
        s, I = self.s, self.I
        Tg = g["Tg"]
        QKV, QKVb, ZS, ZSb, GTS, GTSb, EGL = g["QKV"], g["QKVb"], g["ZS"], g["ZSb"], g["GTS"], g["GTSb"], g["EGL"]
        kC, kVT = self.kC, self.kVT
        OTb = s.alloc("OTOK", 0, 32 * KB)
        OT = self.view(0, F32, [128, 16, 512])
        WPb, _ = self.wp_views()
        DNb = s.alloc("DN", 112 * KB, 36 * KB)
        DN2b = s.alloc("DN2", 152 * KB, 19 * KB)
        DN2b.after.extend(s.collect(WPb.keys))
        dbase = [112 * KB, 152 * KB]
        xbase = [131 * KB, 140 * KB]
        dbuf = [DNb, DN2b]

        def dv(d, off, dtype, shape=(128, 4, 128)):
            return self.view(dbase[d] + off, dtype, list(shape))

        def frv(d, w0, n, dtype, pat, **kw):
            ap = self.FR[d][:, w0:w0 + n]
            if dtype != F32:
                ap = ap.bitcast(dtype)
            return ap.rearrange(pat, **kw)

        T_ = {}
        for d in range(2):
            t = dict(
                vb32=dv(d, 0, F32), E1=dv(d, 0, BF16), E3=dv(d, 1024, BF16),
                L=frv(d, 0, 512, F32, "p (h c) -> p h c", h=4), Lr=frv(d, 0, 512, F32R, "p (h c) -> p h c", h=4),
                RP=frv(d, 512, 1024, F32, "p (h a c) -> p h a c", h=4, a=2), RPr=frv(d, 512, 1024, F32R, "p (h a c) -> p h a c", h=4, a=2),
                EG=self.view(xbase[d] + 2048, BF16, [128, 4, 128]), BS=self.view(xbase[d], F32, [128, 4, 128]),
                kbg32=dv(d, 2048, F32), vnew=dv(d, 4096, BF16),
            )
            for par in range(2):
                o = 5120 + par * 7168
                t["wT32", par] = dv(d, o, F32)
                t["u", par] = dv(d, o + 2048, F32)
                t["IT", par] = dv(d, o + 4096, BF16)
                t["qdT", par] = dv(d, o + 5120, BF16)
                t["kdec", par] = dv(d, o + 6144, BF16)
            T_[d] = t
        S = self.view(134 * KB, F32, [128, 8, 128])
        Sb = self.view(138 * KB, BF16, [128, 8, 128])
        IDB = self.CBc("IDENTB")
        IDF = self.CF("IDENT")
        ID4 = self.CBc("IDENTB").unsqueeze(1).to_broadcast([128, 4, 128])
        BM4 = self.CF("BM4", 4).rearrange("p (h c) -> p h c", h=4)
        visited = set()

        def K(d, name, par=None):
            if name in ("BS", "EG"):
                return None
            return dbuf[d].k(name, par)

        def KS(d, name):
            return [dbuf[d].k(name, None)]

        def bcs(gb, slot, hd0):
            return GTS[:, gb, slot, hd0:hd0 + 4].unsqueeze(2).to_broadcast([128, 4, 128])

        def p4(bank):
            return self.ps[bank][:, 0:512].rearrange("p (h c) -> p h c", h=4)

        def pb4(bank, half=0):
            return self.psb[bank][:, half * 512:half * 512 + 512].rearrange("p (h c) -> p h c", h=4)

        def pre(d, gb, par):
            t = T_[d]
            hd0 = 4 * d
            X1, X2, X3 = 4 * d + 1, 4 * d + 2, 4 * d + 3
            t0 = gb * 128
            kgs = GTSb.k(gb)
            qk = lambda ch: QKVb.k(ch, gb)
            for h in range(4):
                self.tr(self.psb[X1][:, h * 128:(h + 1) * 128], QKV[:, 4 + h, t0:t0 + 128], IDB, [qk(4 + h), kC], [self.pk(X1)])
            yield
            self.tt("dve", t["kbg32"], pb4(X1, 0), bcs(gb, 3, hd0), ALU.mult, [self.pk(X1), kgs], [K(d, "kbg")])
            self.tt("dve", t["kdec", par], pb4(X1, 0), bcs(gb, 4, hd0), ALU.mult, [self.pk(X1), kgs], [K(d, "kdec", par)])
            yield
            for h in range(4):
                kT = QKV[:, 4 + h, t0:t0 + 128]
                self.mm(self.ps[X2][:, h * 128:(h + 1) * 128], kT, kT, True, True, [qk(4 + h)], [self.pk(X2)])
            for h in range(4):
                self.mm(self.ps[X3][:, h * 128:(h + 1) * 128], QKV[:, 4 + h, t0:t0 + 128], QKV[:, h, t0:t0 + 128], True, True,
                        [qk(4 + h), qk(h)], [self.pk(X3)])
            yield
            self.mm(self.ps[X1][0:4, 0:128], GTS[:, gb, 5, hd0:hd0 + 4], self.CF("TRI_F" if d == 0 else "TRI_B"), True, True,
                    [kgs, kC], [self.pk(X1)])
            self.tt("dve", t["BS"][0:4], self.ps[X1][0:4, 0:128].unsqueeze(1).to_broadcast([4, 4, 128]), BM4, ALU.mult,
                    [self.pk(X1), kC], KS(d, "BS"))
            bs2 = t["BS"][0:4].rearrange("p h c -> p (h c)")
            mc = self.CBc("C_F" if d == 0 else "C_B").unsqueeze(1).to_broadcast([128, 4, 128])
            m3 = self.CBc("NEG_IU" if d == 0 else "NEG_IL").unsqueeze(1).to_broadcast([128, 4, 128])
            yield
            self.mm(self.ps[X1][:, 0:512], self.CF("ONES", 4), bs2, True, False, KS(d, "BS") + [kC], [self.pk(X1)], skip=True)
            self.act(t["EG"], p4(X1), AF.Exp, [self.pk(X1)], KS(d, "EG"))
            yield
            self.mm(p4(X1), IDB, m3, False, False, [kC], [self.pk(X1)], skip=True)
            for h in range(4):
                self.act(t["E3"][:, h, :], self.ps[X1][:, h * 128:(h + 1) * 128], AF.Exp, [self.pk(X1)], [K(d, "E3")],
                         bias=GTS[:, gb, 2, hd0 + h:hd0 + h + 1], sreads=[kgs])
            yield
            self.mm(p4(X1), IDB, mc, False, True, [kC], [self.pk(X1)], skip=True)
            for h in range(4):
                self.act(t["E1"][:, h, :], self.ps[X1][:, h * 128:(h + 1) * 128], AF.Exp, [self.pk(X1)], [K(d, "E1")],
                         scale=-1.0, bias=GTS[:, gb, 1, hd0 + h:hd0 + h + 1], sreads=[kgs])
            yield
            self.tt("dve", t["Lr"], p4(X2), t["E1"], ALU.mult, [self.pk(X2), K(d, "E1")], [K(d, "L")])
            self.tt("dve", t["IT", par], p4(X3), t["E3"], ALU.mult, [self.pk(X3), K(d, "E3")], [K(d, "IT", par)])
            self.tt("pool", t["qdT", par], QKV[:, 0:4, t0:t0 + 128], t["EG"], ALU.mult, [qk(0), qk(1), qk(2), qk(3)] + KS(d, "EG"), [K(d, "qdT", par)])
            yield
            for h in range(4):
                self.tr(self.ps[X2][:, h * 128:(h + 1) * 128], t["L"][:, h, :], IDF, [K(d, "L"), kC], [self.pk(X2)])
            yield
            self.cp("act", t["RPr"][:, :, 0, :], p4(X2), [self.pk(X2)], [K(d, "R")])
            self.tt("dve", t["RPr"][:, :, 1, :], ID4, p4(X2), ALU.subtract, [self.pk(X2), kC], [K(d, "P")])
            yield
            Lr, RPr = t["Lr"], t["RPr"]
            XA = (X1, X2)
            for lev in range(6):
                for h in range(4):
                    bk = XA[h // 2]
                    c0 = (h % 2) * 256
                    self.mm(self.ps[bk][:, c0:c0 + 256], Lr[:, h, :], RPr[:, h, :, :].rearrange("p a c -> p (a c)"), True, True,
                            [K(d, "L"), K(d, "R"), K(d, "P")], [self.pk(bk)])
                if lev < 5:
                    for h in range(3):
                        self.mm(self.ps[X3][:, h * 128:h * 128 + 256], RPr[:, h, 0, :], Lr[:, h:h + 2, :].rearrange("p a c -> p (a c)"), True, True,
                                [K(d, "R"), K(d, "L")], [self.pk(X3)])
                    self.mm(self.ps[X3][:, 384:512], RPr[:, 3, 0, :], Lr[:, 3, :], True, True, [K(d, "R"), K(d, "L")], [self.pk(X3)])
                yield
                if lev < 5:
                    self.cp("act", Lr, p4(X3), [self.pk(X3)], [K(d, "L")])
                for i, bk in enumerate(XA):
                    pv = self.ps[bk][:, 0:512].rearrange("p (h a c) -> p h a c", h=2, a=2)
                    if lev < 5:
                        self.cp("act" if i else "dve", RPr[:, 2 * i:2 * i + 2, 0, :], pv[:, :, 0, :], [self.pk(bk)], [K(d, "R")])
                    if lev >= 1:
                        self.tt("dve", RPr[:, 2 * i:2 * i + 2, 1, :], t["RP"][:, 2 * i:2 * i + 2, 1, :], pv[:, :, 1, :], ALU.add,
                                [K(d, "P"), self.pk(bk)], [K(d, "P")])
                    yield
            for h in range(4):
                self.tr(self.psb[X2][:, h * 128:(h + 1) * 128], QKV[:, 8 + h, t0:t0 + 128], IDB, [qk(8 + h), kC], [self.pk(X2)])
            yield
            self.tt("dve", t["vb32"], pb4(X2, 0), bcs(gb, 0, hd0), ALU.mult, [self.pk(X2), kgs], [K(d, "E1"), K(d, "E3")])
            yield
            for h in range(4):
                self.mm(self.ps[X3][:, h * 128:(h + 1) * 128], t["RP"][:, h, 1, :], t["vb32"][:, h, :], True, True,
                        [K(d, "P"), K(d, "E1"), K(d, "E3")], [self.pk(X3)])
            for h in range(4):
                self.mm(self.ps[X1][:, h * 128:(h + 1) * 128], t["kbg32"][:, h, :], t["RP"][:, h, 1, :], True, True,
                        [K(d, "P"), K(d, "kbg")], [self.pk(X1)])
            yield
            self.cp("act", t["u", par], p4(X3), [self.pk(X3)], [K(d, "u", par)])
            self.cp("dve", t["wT32", par], p4(X1), [self.pk(X1)], [K(d, "wT32", par)])
            yield

        def scan(d, gb, par, first_o):
            t = T_[d]
            hd0 = 4 * d
            X0 = 4 * d
            kS = DNb.k("S", d)
            kSb = DNb.k("Sb", d)
            wT, u, IT, qdT, kdec = t["wT32", par], t["u", par], t["IT", par], t["qdT", par], t["kdec", par]
            for c in ((0, 1) if d == 0 else (1, 0)):
                r0, r1 = c * 64, (c + 1) * 64
                for h in range(4):
                    self.mm(self.ps[X0][r0:r1, h * 128:(h + 1) * 128], wT[:, h, r0:r1], S[:, hd0 + h, :], True, True,
                            [K(d, "wT32", par), kS], [self.pk(X0)])
                yield
                self.tt("dve", t["vnew"][r0:r1].rearrange("p h c -> p (h c)"), u[r0:r1].rearrange("p h c -> p (h c)"),
                        self.ps[X0][r0:r1, 0:512], ALU.subtract, [K(d, "u", par), self.pk(X0)], [K(d, "vnew")])
                yield
                for h in range(4):
                    self.mm(self.ps[X0][r0:r1, h * 128:(h + 1) * 128], qdT[:, h, r0:r1], Sb[:, hd0 + h, :], h == 0, False,
                            [K(d, "qdT", par), kSb], [self.pk(X0)], skip=True)
                for h in range(4):
                    self.mm(self.ps[X0][r0:r1, h * 128:(h + 1) * 128], IT[r0:r1, h, r0:r1], t["vnew"][r0:r1, h, :], False, True,
                            [K(d, "IT", par), K(d, "vnew")], [self.pk(X0)], skip=True)
                yield
                ko = OTb.k(gb, c)
                if first_o:
                    self.cp("act", OT[r0:r1, gb, :], self.ps[X0][r0:r1, 0:512], [self.pk(X0)], [ko])
                else:
                    self.tt("dve", OT[r0:r1, gb, :], OT[r0:r1, gb, :], self.ps[X0][r0:r1, 0:512], ALU.add, [ko, self.pk(X0)], [ko])
                yield
                for h in range(4):
                    self.mm(self.ps[X0][:, h * 128:(h + 1) * 128], kdec[r0:r1, h, :], t["vnew"][r0:r1, h, :], True, True,
                            [K(d, "kdec", par), K(d, "vnew")], [self.pk(X0)])
                yield
                for h in range(4):
                    hd = hd0 + h
                    self.stt("dve", S[:, hd, :], S[:, hd, :], EGL[:, gb, c * 8 + hd:c * 8 + hd + 1], self.ps[X0][:, h * 128:(h + 1) * 128],
                             ALU.mult, ALU.add, [kS, self.pk(X0)], [kS], sreads=[GTSb.k("egl", gb)])
                self.cp("act", Sb[:, hd0:hd0 + 4, :], S[:, hd0:hd0 + 4, :], [kS], [kSb])
                yield

        import itertools

        def run_il(gens):
            for _ in itertools.zip_longest(*gens):
                pass

        steps = []
        for si, (so, T) in enumerate(g["seqs"]):
            nblk = T // 128
            for n in range(nblk):
                steps.append((si, so // 128, nblk, n))
        blk_of = lambda st, d: st[1] + (st[3] if d == 0 else st[2] - 1 - st[3])

        def init_state(si):
            for d in range(2):
                kS, kSb = DNb.k("S", d), DNb.k("Sb", d)
                if g["s0"] is None:
                    self.memset("pool", S[:, 4 * d:4 * d + 4, :], 0.0, [kS])
                    self.memset("pool", Sb[:, 4 * d:4 * d + 4, :], 0.0, [kSb])
                else:
                    self.dma("sp", S[:, 4 * d:4 * d + 4, :], g["s0"][4 * d:4 * d + 4].rearrange("h k v -> k h v"), (), [kS])
                    self.cp("act", Sb[:, 4 * d:4 * d + 4, :], S[:, 4 * d:4 * d + 4, :], [kS], [kSb])

        run_il([pre(d, blk_of(steps[0], d), 0) for d in range(2)])
        for i, st in enumerate(steps):
            si, gb0, nblk, n = st
            if n == 0:
                init_state(si)
            gl = []
            for d in range(2):
                gb = blk_of(st, d)
                gl.append(scan(d, gb, i % 2, gb not in visited))
                visited.add(gb)
            if i + 1 < len(steps):
                gl += [pre(d, blk_of(steps[i + 1], d), (i + 1) % 2) for d in range(2)]
            run_il(gl)
            if n == nblk - 1 and g["ns"] is not None:
                op = self.dma("sp", self.O["ns"][g["ns"][si]].rearrange("h k v -> k h v"), S, [DNb.k("S", 0), DNb.k("S", 1)], ())
                self.out_dmas.append(op)
        s.retire(DNb)
        WPb.after = s.collect(DN2b.keys)
        base = 112 * KB
        ONb = s.alloc("ON", base, 4 * KB)
        ON = [self.view(base + i * 1024, BF16, [128, 4, 128]) for i in range(2)]
        JK = self.view(base + 2048, BF16, [128, 128])
        for gb in range(Tg // 128):
            t0 = gb * 128
            r = gb % 2
            ko = [OTb.k(gb, 0), OTb.k(gb, 1)]
            c0 = (self.ss_ctr // 4 * 4 + 4) % 64
            self.ss_ctr = c0 + 4
            kss = self.CONSTB.k("ss4", c0)
            for h in range(4):
                self.act(JK, OT[:, gb, h * 128:(h + 1) * 128], AF.Square, ko, [ONb.k("jk"), kss], accum_out=self.SS[:, c0 + h:c0 + h + 1])
            self.ts("pool", self.SS[:, c0:c0 + 4], self.SS[:, c0:c0 + 4], 1.0 / 128, EPS, ALU.mult, ALU.add, [kss], [kss])
            self.tt("pool", self.SS[:, c0:c0 + 4], self.SS[:, c0:c0 + 4], self.CF("NEG05").to_broadcast([128, 4]), ALU.pow, [kss, kC], [kss])
            self.tt("dve", ON[r], OT[:, gb, :].rearrange("p (h c) -> p h c", h=4),
                    self.SS[:, c0:c0 + 4].unsqueeze(2).to_broadcast([128, 4, 128]), ALU.mult, ko + [kss], [ONb.k(r)])
            bank = self.next_bank()
            for h in range(4):
                self.tr(self.psb[bank][:, h * 128:(h + 1) * 128], ON[r][:, h, :], IDB, [ONb.k(r), kC], [self.pk(bank)])
            self.stt("dve", ZS[:, 0:4, t0:t0 + 128], pb4(bank, 0), self.VT[:, 0, 18:19], ZS[:, 0:4, t0:t0 + 128], ALU.mult, ALU.mult,
                     [self.pk(bank), ZSb.k(gb)], [ZSb.k(gb)], sreads=[kVT])
        s.retire(ONb)
        s.retire(OTb)
        s.retire(QKVb)
        s.retire(GTSb)
        g["YA"], g["YAb"] = ZS, ZSb
        if self.taps is not None:
            nm = g["name"]
            for h in (0, 2):
                self.tap("ya%d_%s" % (h, nm), ZS[:, h, 0:Tg], [128, Tg], dtype=BF16, reads=[ZSb.k(b) for b in range(Tg // 128)])

    def build_grow(self, Gt, v, dst_off, buf, tmp_off, tmpbuf, dgkey=None):
        dst = self.view(dst_off, F32, [128, D])
        DG = [self.view(tmp_off + i * 512, F32, [128, 128]) for i in range(2)]
        banks = [self.next_bank(), self.next_bank()]
        for k in range(8):
            r = k % 2
            kdg = dgkey if dgkey is not None else tmpbuf.k("dg", r)
            self.ts("dve", DG[r], self.CF("IDENT"), Gt[:, k, v:v + 1], None, ALU.mult, None, [self.kC], [kdg], sreads=[self.kD])
            b = banks[k // 4]
            self.mm(self.ps[b][:, (k % 4) * 128:(k % 4 + 1) * 128], self.CF("ONES"), DG[r], True, True, [kdg, self.kC], [self.pk(b)])
        for hf in range(2):
            self.cp("act" if hf else "dve", dst[:, hf * 512:(hf + 1) * 512], self.ps[banks[hf]][:, 0:512], [self.pk(banks[hf])], [buf.k("grow")])
        return dst

    def norm_residual(self, banks, resid, resid_keys, grow, kgrow, dst, dst_keys, T1, kT1):
        c0 = (self.ss_ctr // 2 * 2 + 2) % 64
        self.ss_ctr = c0 + 2
        kss = self.CONSTB.k("ss2", c0)
        SS = self.SS
        for hf in range(2):
            self.act(T1[hf], self.ps[banks[hf]][:, 0:512], AF.Square, [self.pk(banks[hf])], [kT1[hf], kss], accum_out=SS[:, c0 + hf:c0 + hf + 1])
        self.tt("pool", SS[:, c0:c0 + 1], SS[:, c0:c0 + 1], SS[:, c0 + 1:c0 + 2], ALU.add, [kss], [kss])
        self.rstd_ops(c0, kss, 1.0 / D, kss)
        for hf in range(2):
            self.stt("dve", T1[hf], self.ps[banks[hf]][:, 0:512], SS[:, c0:c0 + 1], grow[:, hf * 512:(hf + 1) * 512], ALU.mult, ALU.mult,
                     [self.pk(banks[hf]), kgrow], [kT1[hf]], sreads=[kss])
            self.tt("pool", dst[:, hf * 512:(hf + 1) * 512], T1[hf], resid[:, hf * 512:(hf + 1) * 512], ALU.add,
                    [kT1[hf]] + list(resid_keys), list(dst_keys))

    def phase_m1(self, g):
        s, I = self.s, self.I
        Tg = g["Tg"]
        NT = Tg // 512
        self.phase_ht(g)
        HT, HTb, YA, YAb, YB, YBb = g["HT"], g["HTb"], g["YA"], g["YAb"], g["YB"], g["YBb"]
        MTb = s.alloc("MT", 64 * KB, 32 * KB)
        MT = self.view(64 * KB, BF16, [128, 8, 2048])
        SCb = s.alloc("M1S", 96 * KB, 12 * KB)
        SA = [self.view(96 * KB + i * KB, BF16, [128, 512]) for i in range(4)]
        MM = [self.view(100 * KB + i * 2 * KB, F32, [128, 512]) for i in range(4)]
        WPb, WPv = self.wp_views()
        w_in_v = I["w_in"].rearrange("(k p) c -> p k c", p=128)
        w_a_v = I["w_a"].rearrange("(k p) c -> p k c", p=128)
        w_b_v = I["w_b"].rearrange("(k p) c -> p k c", p=128)
        it = 0

        def m1_load(pp):
            c = 256 * pp
            return (self.wp_load([(0, 8, 256, w_in_v[:, :, 2576 + c:2576 + c + 256])]),
                    self.wp_load([(0, 8, 256, w_in_v[:, :, 3600 + c:3600 + c + 256])]),
                    self.wp_load([(0, 4, 256, w_a_v[:, :, c:c + 256]), (4, 4, 256, w_b_v[:, :, c:c + 256])]))

        nxt = m1_load(0)
        for pp in range(4):
            sl = nxt
            if pp + 1 < 4:
                nxt = m1_load(pp + 1)
            for tt in range(NT):
                t0 = tt * 512
                hk = [HTb.k(b) for b in range(t0 // 128, t0 // 128 + 4)]
                yak = [YAb.k(b) for b in range(t0 // 128, t0 // 128 + 4)]
                ybk = [YBb.k(b) for b in range(t0 // 128, t0 // 128 + 4)]
                for oc2 in range(2):
                    oc = pp * 2 + oc2
                    cs = slice(oc2 * 128, (oc2 + 1) * 128)
                    bGA, bGB, bPA, bPB = [self.next_bank() for _ in range(4)]
                    for kc in range(8):
                        self.mm(self.ps[bGA][:, 0:512], WPv[sl[0]][:, kc, cs], HT[:, kc, t0:t0 + 512], kc == 0, kc == 7, [WPb.k(sl[0])] + hk, [self.pk(bGA)])
                    for kc in range(8):
                        self.mm(self.ps[bGB][:, 0:512], WPv[sl[1]][:, kc, cs], HT[:, kc, t0:t0 + 512], kc == 0, kc == 7, [WPb.k(sl[1])] + hk, [self.pk(bGB)])
                    for kc in range(4):
                        self.mm(self.ps[bPA][:, 0:512], WPv[sl[2]][:, kc, cs], YA[:, kc, t0:t0 + 512], kc == 0, kc == 3, [WPb.k(sl[2])] + yak, [self.pk(bPA)])
                    for kc in range(4):
                        self.mm(self.ps[bPB][:, 0:512], WPv[sl[2]][:, 4 + kc, cs], YB[:, kc, t0:t0 + 512], kc == 0, kc == 3, [WPb.k(sl[2])] + ybk, [self.pk(bPB)])
                    r = (it % 2) * 2
                    it += 1
                    self.act(SA[r], self.ps[bGA][:, 0:512], AF.Sigmoid, [self.pk(bGA)], [SCb.k("sa", r)])
                    self.act(SA[r + 1], self.ps[bGB][:, 0:512], AF.Sigmoid, [self.pk(bGB)], [SCb.k("sa", r + 1)])
                    self.tt("dve", MM[r], SA[r], self.ps[bPA][:, 0:512], ALU.mult, [SCb.k("sa", r), self.pk(bPA)], [SCb.k("mm", r)])
                    self.tt("dve", MM[r + 1], SA[r + 1], self.ps[bPB][:, 0:512], ALU.mult, [SCb.k("sa", r + 1), self.pk(bPB)], [SCb.k("mm", r + 1)])
                    self.tt("pool", MT[:, oc, t0:t0 + 512], MM[r], MM[r + 1], ALU.add, [SCb.k("mm", r), SCb.k("mm", r + 1)],
                            [MTb.k(b) for b in range(t0 // 128, t0 // 128 + 4)])
        for b_ in (SCb, HTb, YAb, YBb):
            s.retire(b_)
        g["MT"], g["MTb"] = MT, MTb
        if self.taps is not None:
            nm = g["name"]
            for oc in (0, 5):
                self.tap("mt%d_%s" % (oc, nm), MT[:, oc, 0:Tg], [128, Tg], dtype=BF16, reads=[MTb.k(b) for b in range(Tg // 128)])

    def phase_m2(self, g):
        s, I = self.s, self.I
        Tg, v = g["Tg"], g["v"]
        NB = Tg // 128
        MT, MTb = g["MT"], g["MTb"]
        WOb = s.alloc("WO", 96 * KB, 16 * KB)
        WO = self.view(96 * KB, BF16, [128, 8, 1024])
        w_o_v = I["w_o"].rearrange("(k p) c -> p k c", p=128)
        for hf in range(2):
            self.dma("pool", WO[:, :, hf * 512:(hf + 1) * 512], w_o_v[:, :, hf * 512:(hf + 1) * 512], (), [WOb.k(hf)])
        X1b = s.alloc("X1", 0, 64 * KB)
        X1 = self.view(0, F32, [128, 16, D])
        Mb = s.alloc("M2S", 112 * KB, 20 * KB)
        XL = [self.view(112 * KB + i * 4 * KB, F32, [128, D]) for i in range(2)]
        G1row = self.build_grow(self.G1, v, 120 * KB, Mb, 124 * KB, Mb)
        T1 = [[self.view(126 * KB + (i * 2 + hf) * 2 * KB, F32, [128, 512]) for hf in range(2)] for i in range(2)]
        for blk in range(NB):
            r = blk % 2
            kx = Mb.k("xl", r)
            self.dma("sp", XL[r], g["x"][blk * 128:(blk + 1) * 128, :], (), [kx])
            banks = [self.next_bank(), self.next_bank()]
            for hf in range(2):
                for kc in range(8):
                    self.mm(self.ps[banks[hf]][:, 0:512], MT[:, kc, blk * 128:(blk + 1) * 128], WO[:, kc, hf * 512:(hf + 1) * 512], kc == 0, kc == 7,
                            [MTb.k(blk), WOb.k(hf)], [self.pk(banks[hf])])
            self.norm_residual(banks, XL[r], [kx], G1row, Mb.k("grow"), X1[:, blk, :], [X1b.k(blk)], T1[r], [Mb.k("t1", r, 0), Mb.k("t1", r, 1)])
        for b_ in (Mb, MTb, WOb):
            s.retire(b_)
        g["X1"], g["X1b"] = X1, X1b
        if self.taps is not None:
            nm = g["name"]
            self.tap("x1_" + nm, X1[:, 0:NB, :].rearrange("p a b -> p (a b)"), [128, NB * D], reads=[X1b.k(b) for b in range(NB)])

    def phase_ffn(self, g):
        s, I = self.s, self.I
        Tg, v = g["Tg"], g["v"]
        X1, X1b = g["X1"], g["X1b"]
        WPb, WPv = self.wp_views()
        WDb = s.alloc("WD", 64 * KB, 44 * KB)
        WD = self.view(64 * KB, BF16, [128, 22, 1024])
        w_d_v = I["w_down"].rearrange("(k p) c -> p k c", p=128)
        for j0 in range(0, 22, 4):
            j1 = min(j0 + 4, 22)
            self.dma("pool", WD[:, j0:j1, :], w_d_v[:, j0:j1, :], (), [WDb.k(j) for j in range(j0, j1)])
        ATb = s.alloc("ACTT", 108 * KB, 22 * KB)
        ACTT = self.view(108 * KB, BF16, [128, 22, 512])
        H2b = s.alloc("H2T", 130 * KB, 8 * KB)
        H2T = self.view(130 * KB, BF16, [128, 8, 512])
        Fb = s.alloc("FSCR", 138 * KB, 14 * KB)
        XN = self.view(138 * KB, BF16, [128, D])
        YO = [self.view(140 * KB + i * 4 * KB, F32, [128, D]) for i in range(2)]
        Gb = Fb
        G2row = self.build_grow(self.G2, v, 148 * KB, Fb, 138 * KB, Fb, dgkey=Fb.k("xn"))
        SG = [YO[0][:, 0:512], YO[0][:, 512:1024]]
        T1 = SG
        w_up_v = I["w_up"].rearrange("(k p) c -> p k c", p=128)
        A = self.A2[:, :, v]
        B = self.MODT[:, 24:32, v]
        it = 0
        NU = Tg // 512

        def h2t_block(un, bi):
            blk = un * 4 + bi
            self.norm_transpose_block(X1[:, blk, :], [X1b.k(blk)], XN, Fb.k("xn"), H2T[:, :, bi * 128:(bi + 1) * 128], H2b.k(bi), A, B)

        for bi in range(4):
            h2t_block(0, bi)
        def up_load(jp):
            return (self.wp_load([(0, 8, 256, w_up_v[:, :, 256 * jp:256 * jp + 256])]),
                    self.wp_load([(0, 8, 256, w_up_v[:, :, DFF + 256 * jp:DFF + 256 * jp + 256])]))

        q = [up_load(0), up_load(1)]
        for un in range(NU):
            hk = [H2b.k(bi) for bi in range(4)]
            for jp in range(11):
                sg_, su_ = q.pop(0)
                if jp + 2 < 11:
                    q.append(up_load(jp + 2))
                for jj in range(2):
                    j = jp * 2 + jj
                    cs = slice(jj * 128, (jj + 1) * 128)
                    bG, bU = self.next_bank(), self.next_bank()
                    for kc in range(8):
                        self.mm(self.ps[bG][:, 0:512], WPv[sg_][:, kc, cs], H2T[:, kc, :], kc == 0, kc == 7, [WPb.k(sg_)] + hk, [self.pk(bG)])
                    for kc in range(8):
                        self.mm(self.ps[bU][:, 0:512], WPv[su_][:, kc, cs], H2T[:, kc, :], kc == 0, kc == 7, [WPb.k(su_)] + hk, [self.pk(bU)])
                    r = it % 2
                    it += 1
                    self.act(SG[r], self.ps[bG][:, 0:512], AF.Silu, [self.pk(bG)], [Gb.k("sg", r)])
                    self.tt("dve", ACTT[:, j, :], SG[r], self.ps[bU][:, 0:512], ALU.mult, [Gb.k("sg", r), self.pk(bU)], [ATb.k(j)])
            if un + 1 < NU:
                q = [up_load(0), up_load(1)]
            for bi in range(4):
                blk = un * 4 + bi
                if un + 1 < NU:
                    nblk_ = (un + 1) * 4 + bi
                    self.norm_stage_a(X1[:, nblk_, :], [X1b.k(nblk_)], XN, Fb.k("xn"))
                banks = [self.next_bank(), self.next_bank()]
                for hf in range(2):
                    for j in range(22):
                        self.mm(self.ps[banks[hf]][:, 0:512], ACTT[:, j, bi * 128:(bi + 1) * 128], WD[:, j, hf * 512:(hf + 1) * 512], j == 0, j == 21,
                                [ATb.k(j), WDb.k(j)], [self.pk(banks[hf])])
                if un + 1 < NU:
                    self.norm_stage_b(XN, Fb.k("xn"), H2T[:, :, bi * 128:(bi + 1) * 128], H2b.k(bi), A, B)
                yo = YO[1]
                kyo = Fb.k("yo", 1)
                self.norm_residual(banks, X1[:, blk, :], [X1b.k(blk)], G2row, Fb.k("grow"), yo, [kyo], T1, [Gb.k("sg", 0), Gb.k("sg", 1)])
                op = self.dma("sp", g["y"][blk * 128:(blk + 1) * 128, :], yo, [kyo], ())
                self.out_dmas.append(op)
        for b_ in (WDb, ATb, H2b, Fb, X1b):
            s.retire(b_)


_PROG_CACHE = {}


def _get_prog(stop_after=None, taps=False):
    key = (stop_after, taps)
    if key not in _PROG_CACHE:
        p = Prog(stop_after=stop_after, taps=[] if taps else None)
        p.build()
        _PROG_CACHE[key] = p
    return _PROG_CACHE[key]


def _in_maps(inputs):
    f = lambda a: np.ascontiguousarray(np.asarray(a, dtype=np.float32))
    common = {
        "w_mod": f(inputs["w_mod"][0]), "b_mod": f(inputs["b_mod"][0]),
        "g_pre_mix": f(inputs["g_pre_mix"][0]), "g_post_mix": f(inputs["g_post_mix"][0]),
        "g_pre_ffn": f(inputs["g_pre_ffn"][0]), "g_post_ffn": f(inputs["g_post_ffn"][0]),
        "w_in": f(inputs["w_in"][0]), "conv_w": f(inputs["conv_w"][0]),
        "a_log": f(inputs["a_log"][0]).reshape(8), "dt_bias": f(inputs["dt_bias"][0]).reshape(8),
        "g_onorm": f(inputs["g_onorm"][0]), "w_a": f(inputs["w_a_proj"][0]), "w_pool": f(inputs["w_pool"][0]),
        "pool_scale": f(inputs["pool_scale"][0]), "w_b": f(inputs["w_b_proj"][0]), "w_o": f(inputs["w_o"][0]),
        "w_up": f(inputs["w_up"][0]), "w_down": f(inputs["w_down"][0]),
        "constf": CF_ARR, "constb": CB_ARR,
    }
    xp = f(inputs["x_prompt"])
    xs = f(inputs["x_sample"])
    sd = f(inputs["state_delta"])
    c = f(inputs["c"])
    cc = f(inputs["c_ctx"])
    maps = []
    for i in range(8):
        m = dict(common)
        m["xp"] = np.ascontiguousarray(xp[2 * i:2 * i + 2].reshape(512, D))
        m["xs"] = np.ascontiguousarray(xs[i])
        m["s0"] = np.ascontiguousarray(sd[i, 0].reshape(8, 128, 128))
        m["cv"] = np.ascontiguousarray(np.stack([cc, c[i]], axis=0))
        maps.append(m)
    return maps


def kernel(**inputs):
    prog = _get_prog()
    res = run_bass_kernel_spmd(prog.nc, _in_maps(inputs), core_ids=list(range(8)))
    r = res.results
    y_prompt = np.concatenate([r[i]["yp"].reshape(2, 256, D) for i in range(8)], axis=0).astype(np.float32)
    y_sample = np.stack([r[i]["ys"] for i in range(8)], axis=0).astype(np.float32)
    ns = np.concatenate([r[i]["ns"].reshape(2, 1, 2, 4, 128, 128) for i in range(8)], axis=0).astype(np.float32)
    return (y_prompt, y_sample, ns)
```

```python
import numpy as np
from contextlib import ExitStack
import concourse.bass as bass
import concourse.mybir as mybir
from concourse.bass_utils import run_bass_kernel_spmd

F32 = mybir.dt.float32
BF16 = mybir.dt.bfloat16
F32R = mybir.dt.float32r
AF = mybir.ActivationFunctionType
ALU = mybir.AluOpType

D = 1024
NH = 4
DA = 512
DFF = 2816
DIN = 4624
EPS = 1e-6
BIG = 30000.0
NQ = 8

ANNOTATE = False
INV_BF16_FROM = 5
NWP = 6
INV_DT = F32
SAME_ENGINE_RAW = True
POOL_ENG = "dve"
DEBUG_TAPS = None


class Op:
    __slots__ = ("eng", "fn", "deps", "signal", "sigval", "is_dma", "dsem", "dval", "idx", "tag")

    def __init__(self, eng, fn, is_dma, idx):
        self.eng = eng
        self.fn = fn
        self.deps = set()
        self.signal = False
        self.sigval = None
        self.is_dma = is_dma
        self.dsem = None
        self.dval = None
        self.idx = idx


class Buf:
    def __init__(self, name, off=0, size=0):
        self.name = name
        self.off = off
        self.size = size
        self.after = []
        self.keys = set()
        self.psum = off >= 10 ** 7

    def k(self, *sub):
        key = (self, sub)
        self.keys.add(key)
        return key


class Sched:
    def __init__(self, nc):
        self.nc = nc
        self.E = {"pe": nc.tensor, "act": nc.scalar, "dve": nc.vector, "pool": nc.gpsimd, "sp": nc.sync}
        self.ops = []
        self.last_w = {}
        self.readers = {}
        self.retired = []

    def _need(self, prod, eng, dma):
        return prod.is_dma or dma or prod.eng != eng

    def add(self, eng, fn, reads=(), writes=(), sreads=(), dma=False, after=()):
        pr = [k for k in reads if isinstance(k[0], Buf) and k[0].psum]
        if pr:
            reads = [k for k in reads if not (isinstance(k[0], Buf) and k[0].psum)]
            writes = list(writes) + pr
        op = Op(eng, fn, dma, len(self.ops))
        op.tag = None
        if ANNOTATE:
            import sys as _sys
            f = _sys._getframe(2)
            if f.f_code.co_name in ("act", "cp"):
                f = f.f_back
            op.tag = "%s:%d" % (f.f_code.co_name, f.f_lineno)
        deps = op.deps
        for k in reads:
            w = self.last_w.get(k)
            if w is not None and (self._need(w, eng, dma) or (SAME_ENGINE_RAW and eng != "pe")):
                deps.add(w)
        for k in sreads:
            w = self.last_w.get(k)
            if w is not None:
                deps.add(w)
        for k in writes:
            w = self.last_w.get(k)
            if w is not None and self._need(w, eng, dma):
                deps.add(w)
            rd = self.readers.get(k)
            if rd:
                for r in rd.values():
                    if self._need(r, eng, dma):
                        deps.add(r)
            b = k[0]
            if isinstance(b, Buf) and b.after:
                for r in b.after:
                    if self._need(r, eng, dma):
                        deps.add(r)
        for r in after:
            if self._need(r, eng, dma):
                deps.add(r)
        for d in deps:
            if not d.is_dma:
                d.signal = True
        rkey = ("dma", op.idx) if dma else eng
        for k in list(reads) + list(sreads):
            self.readers.setdefault(k, {})[rkey] = op
        for k in writes:
            self.last_w[k] = op
            self.readers[k] = {}
        self.ops.append(op)
        return op

    def collect(self, keys):
        best = {}
        out = []
        for k in keys:
            cands = []
            w = self.last_w.get(k)
            if w is not None:
                cands.append(w)
            rd = self.readers.get(k)
            if rd:
                cands.extend(rd.values())
            for o in cands:
                if o.is_dma:
                    out.append(o)
                else:
                    b = best.get(o.eng)
                    if b is None or o.idx > b.idx:
                        best[o.eng] = o
        return list(set(out)) + list(best.values())

    def alloc(self, name, off, size):
        b = Buf(name, off, size)
        for (o, s, tok) in self.retired:
            if o < off + size and off < o + s:
                b.after.extend(tok)
        return b

    def retire(self, buf):
        self.retired.append((buf.off, buf.size, self.collect(buf.keys)))

    def emit(self, stack):
        nc = self.nc
        sem = {e: stack.enter_context(nc.semaphore("s_" + e)) for e in self.E}
        dsem = {q: [stack.enter_context(nc.semaphore("d_%s%d" % (q, i))) for i in range(NQ)] for q in ("sp", "pool", "act")}
        dval = {q: [0] * NQ for q in dsem}
        dcount = {q: 0 for q in dsem}
        cnt = {e: 0 for e in self.E}
        waited = {}
        for op in self.ops:
            eo = self.E[op.eng]
            for d in sorted(op.deps, key=lambda o: o.idx):
                if d.is_dma:
                    s, v = d.dsem, d.dval
                else:
                    s, v = sem[d.eng], d.sigval
                wk = (op.eng, id(s))
                if waited.get(wk, 0) >= v:
                    continue
                eo.wait_ge(s, v)
                waited[wk] = v
            if op.is_dma:
                q = op.eng
                slot = dcount[q] % NQ
                dcount[q] += 1
                s = dsem[q][slot]
                prev = dval[q][slot]
                wk = (q, id(s))
                if prev > 0 and waited.get(wk, 0) < prev:
                    eo.wait_ge(s, prev)
                    waited[wk] = prev
                ins = op.fn()
                if op.tag:
                    ins.annotate(op.tag)
                ins.then_inc(s, 16)
                op.dsem = s
                op.dval = prev + 16
                dval[q][slot] = prev + 16
            else:
                if op.fn is None:
                    continue
                ins = op.fn()
                if op.tag:
                    ins.annotate(op.tag)
                if op.signal:
                    cnt[op.eng] += 1
                    ins.then_inc(sem[op.eng], 1)
                    op.sigval = cnt[op.eng]
        return cnt


POOL_W = (2, 4, 8, 16)


def _invcnt(n, w):
    t = np.arange(n)
    lo = np.clip(t - w // 2, 0, n)
    hi = np.clip(t + (w - w // 2), 0, n)
    return (1.0 / (hi - lo)).astype(np.float32)


def _build_consts():
    p = np.arange(128)[:, None]
    f = np.arange(128)[None, :]
    same = (p // 64) == (f // 64)
    cf = {}
    cf["IDENT"] = np.eye(128, dtype=np.float32)
    cf["TRI_F"] = (same & (p <= f)).astype(np.float32)
    cf["TRI_B"] = (same & (p >= f)).astype(np.float32)
    cf["STR_F"] = (same & (p > f)).astype(np.float32)
    cf["STR_B"] = (same & (p < f)).astype(np.float32)
    cf["IND0"] = np.broadcast_to((p < 64), (128, 128)).astype(np.float32)
    cf["IND1"] = np.broadcast_to((p >= 64), (128, 128)).astype(np.float32)
    cf["ONES"] = np.ones((128, 128), np.float32)
    cf["NEGONES"] = -np.ones((128, 128), np.float32)
    bm = np.zeros((128, 512), np.float32)
    for h in range(4):
        bm[h, h * 128:(h + 1) * 128] = 1.0
    cf["BM4"] = bm
    cf["NEG05"] = np.full((128, 1), -0.5, np.float32)
    for wi, w in enumerate(POOL_W):
        cf["IC64_%d" % wi] = np.broadcast_to(_invcnt(64, w)[None, :], (128, 64)).copy()
        cf["IC32_%d" % wi] = np.broadcast_to(_invcnt(32, w)[None, :], (128, 32)).copy()
        cf["IC256_%d" % wi] = np.broadcast_to(_invcnt(256, w)[None, :], (128, 256)).copy()
    cb = {}
    cb["IDENTB"] = np.eye(128, dtype=np.float32)
    cb["ONESB"] = np.ones((128, 128), np.float32)
    cb["NEG_SL"] = np.where(same & (p > f), 0.0, -BIG).astype(np.float32)
    cb["NEG_SU"] = np.where(same & (p < f), 0.0, -BIG).astype(np.float32)
    cb["NEG_IU"] = np.where(same & (p <= f), 0.0, -BIG).astype(np.float32)
    cb["NEG_IL"] = np.where(same & (p >= f), 0.0, -BIG).astype(np.float32)
    cb["C_F"] = (BIG * ((~(same & (p > f))).astype(np.float32) + (~(same & (p <= f))).astype(np.float32))).astype(np.float32)
    cb["C_B"] = (BIG * ((~(same & (p < f))).astype(np.float32) + (~(same & (p >= f))).astype(np.float32))).astype(np.float32)

    def pack(dct):
        offs = {}
        cols = []
        o = 0
        for k, v in dct.items():
            offs[k] = (o, v.shape[1])
            cols.append(v)
            o += v.shape[1]
        return offs, np.ascontiguousarray(np.concatenate(cols, axis=1))

    offf, arrf = pack(cf)
    offb, arrb = pack(cb)
    return offf, arrf, offb, arrb


CF_OFF, CF_ARR, CB_OFF, CB_ARR = _build_consts()


KB = 1024


class Prog:
    def __init__(self, stop_after=None, taps=None):
        self.stop_after = stop_after
        self.taps = taps
        self.nc = bass.Bass("TRN2", target_bir_lowering=False)
        self.s = Sched(self.nc)
        self.out_dmas = []

    def mm(self, out, lhsT, rhs, start, stop, reads, writes, after=(), skip=False):
        nc = self.nc
        if skip:
            return self.s.add("pe", lambda: nc.tensor.matmul(out, lhsT, rhs, start=start, stop=stop, skip_group_check=True), reads, writes, after=after)
        return self.s.add("pe", lambda: nc.tensor.matmul(out, lhsT, rhs, start=start, stop=stop), reads, writes, after=after)

    def tr(self, out, in_, ident, reads, writes):
        nc = self.nc
        return self.s.add("pe", lambda: nc.tensor.transpose(out, in_, ident), reads, writes)

    def act(self, out, in_, func, reads, writes, bias=None, scale=None, accum_out=None, sreads=()):
        nc = self.nc
        kw = {}
        if bias is not None:
            kw["bias"] = bias
        if scale is not None:
            kw["scale"] = scale
        if accum_out is not None:
            kw["accum_out"] = accum_out
        return self.s.add("act", lambda: nc.scalar.activation(out, in_, func, **kw), reads, writes, sreads)

    def _ve(self, eng):
        return self.nc.vector if eng == "dve" else self.nc.gpsimd

    def ts(self, eng, out, in0, s1, s2, op0, op1, reads, writes, sreads=()):
        e = self._ve(eng)
        if op1 is None:
            return self.s.add(eng, lambda: e.tensor_scalar(out, in0, s1, None, op0), reads, writes, sreads)
        return self.s.add(eng, lambda: e.tensor_scalar(out, in0, s1, s2, op0, op1), reads, writes, sreads)

    def tt(self, eng, out, in0, in1, op, reads, writes):
        e = self._ve(eng)
        return self.s.add(eng, lambda: e.tensor_tensor(out, in0, in1, op), reads, writes)

    def stt(self, eng, out, in0, scalar, in1, op0, op1, reads, writes, sreads=()):
        e = self._ve(eng)
        return self.s.add(eng, lambda: e.scalar_tensor_tensor(out, in0, scalar, in1, op0, op1), reads, writes, sreads)

    def cp(self, eng, out, in_, reads, writes):
        if eng == "act":
            return self.act(out, in_, AF.Copy, reads, writes)
        e = self._ve(eng)
        return self.s.add(eng, lambda: e.tensor_copy(out, in_), reads, writes)

    def memset(self, eng, out, val, writes):
        e = self._ve(eng)
        return self.s.add(eng, lambda: e.memset(out, val), (), writes)

    def dma(self, q, out, in_, reads, writes, noncontig=False):
        nc = self.nc
        eo = self.s.E[q]
        if noncontig:
            def fn():
                with nc.allow_non_contiguous_dma(reason="small strided"):
                    return eo.dma_start(out=out, in_=in_)
        else:
            def fn():
                return eo.dma_start(out=out, in_=in_)
        return self.s.add(q, fn, reads, writes, dma=True)

    def view(self, off_bytes, dtype, shape):
        nfree = 1
        for d in shape[1:]:
            nfree *= d
        esz = 4 if dtype in (F32, F32R) else 2
        nbytes = nfree * esz
        assert off_bytes % 4 == 0 and nbytes % 4 == 0, (off_bytes, nbytes)
        w0 = off_bytes // 4
        ap = self.arena[0:shape[0], w0:w0 + nbytes // 4]
        if dtype != F32:
            ap = ap.bitcast(dtype)
        if len(shape) == 2:
            return ap
        names = " ".join("d%d" % i for i in range(len(shape) - 1))
        kw = {"d%d" % i: shape[i + 1] for i in range(len(shape) - 1)}
        return ap.rearrange("p (%s) -> p %s" % (names, names), **kw)

    def tap(self, name, ap, shape, dtype=F32, reads=()):
        if self.taps is None or name in self.taps:
            return
        t = self.nc.dram_tensor("tap_" + name, list(shape), dtype, kind="ExternalOutput").ap()
        op = self.dma("sp", t, ap, reads, ())
        self.out_dmas.append(op)
        self.taps.append(name)

    def build(self):
        nc, s = self.nc, self.s
        stack = ExitStack()
        self.stack = stack

        def din(name, shape):
            return nc.dram_tensor(name, list(shape), F32, kind="ExternalInput").ap()

        def dout(name, shape):
            return nc.dram_tensor(name, list(shape), F32, kind="ExternalOutput").ap()

        I = {}
        I["xp"] = din("xp", [512, D])
        I["xs"] = din("xs", [2048, D])
        I["s0"] = din("s0", [8, 128, 128])
        I["cv"] = din("cv", [2, D])
        I["w_mod"] = din("w_mod", [D, 6 * D])
        I["b_mod"] = din("b_mod", [6 * D])
        for n in ("g_pre_mix", "g_post_mix", "g_pre_ffn", "g_post_ffn"):
            I[n] = din(n, [D])
        I["w_in"] = din("w_in", [D, DIN])
        I["conv_w"] = din("conv_w", [5, 1536])
        I["a_log"] = din("a_log", [8])
        I["dt_bias"] = din("dt_bias", [8])
        I["g_onorm"] = din("g_onorm", [128])
        I["w_a"] = din("w_a", [DA, D])
        I["w_pool"] = din("w_pool", [4, 128, 128])
        I["pool_scale"] = din("pool_scale", [512])
        I["w_b"] = din("w_b", [512, D])
        I["w_o"] = din("w_o", [D, D])
        I["w_up"] = din("w_up", [D, 2 * DFF])
        I["w_down"] = din("w_down", [DFF, D])
        I["constf"] = din("constf", list(CF_ARR.shape))
        I["constb"] = din("constb", list(CB_ARR.shape))
        O = {}
        O["yp"] = dout("yp", [512, D])
        O["ys"] = dout("ys", [2048, D])
        O["ns"] = dout("ns", [2, 8, 128, 128])
        self.I, self.O = I, O

        ARENA_KB = 195
        self.arena = stack.enter_context(nc.sbuf_tensor("arena", [128, ARENA_KB * 256], F32))
        self.FR = [stack.enter_context(nc.sbuf_tensor("fr%d" % i, [128, 1536], F32)) for i in range(2)]
        self.ps = [stack.enter_context(nc.psum_tensor("ps%d" % i, [128, 512], F32)) for i in range(8)]
        self.psb = [p[:, :].bitcast(BF16) for p in self.ps]
        self.PSB = [Buf("psbank%d" % i, 10 ** 7 + i, 1) for i in range(8)]
        self.bank_ctr = 0

        CB0 = 176 * KB
        o = CB0
        ncf = CF_ARR.shape[1]
        self.CFv = self.view(o, F32, [128, ncf]); o += ((ncf * 4 + 7) // 8) * 8
        ncb = CB_ARR.shape[1]
        self.CBv = self.view(o, BF16, [128, ncb]); o += ncb * 2
        self.VT = self.view(o, F32, [128, 12, 32]); o += 1536
        self.SCT = self.view(o, F32, [128, 8, 2]); o += 64
        self.MODT = self.view(o, F32, [128, 48, 2]); o += 384
        self.A1 = self.view(o, F32, [128, 8, 2]); o += 64
        self.A2 = self.view(o, F32, [128, 8, 2]); o += 64
        self.G1 = self.view(o, F32, [128, 8, 2]); o += 64
        self.G2 = self.view(o, F32, [128, 8, 2]); o += 64
        self.ALB = self.view(o, F32, [128, 8]); o += 32
        self.DTB = self.view(o, F32, [128, 8]); o += 32
        self.NEGA = self.view(o, F32, [128, 8]); o += 32
        self.WPOOL = self.view(o, BF16, [128, 4, 128]); o += 1024
        self.NWPOOL = self.view(o, BF16, [128, 4, 128]); o += 1024
        self.SS = self.view(o, F32, [128, 64]); o += 256
        assert o <= ARENA_KB * KB, o
        self.CONSTB = Buf("const", CB0, o - CB0)
        self.kC = self.CONSTB.k("c")
        self.ss_ctr = 0

        self.setup()
        groups = [
            dict(name="p", x=I["xp"], y=O["yp"], Tg=512, seqs=[(0, 256), (256, 256)], v=0, grid=False, s0=None, ns=[0, 1]),
            dict(name="s", x=I["xs"], y=O["ys"], Tg=2048, seqs=[(0, 2048)], v=1, grid=True, s0=I["s0"], ns=None),
        ]
        for g in groups:
            self.run_group(g)
            if self.stop_after is not None and self.stop_after[0] == g["name"]:
                break
        s.add("sp", None, after=self.out_dmas)
        s.emit(stack)
        stack.close()
        return nc

    def CF(self, name, rows=128):
        o, n = CF_OFF[name]
        return self.CFv[0:rows, o:o + n]

    def CBc(self, name, rows=128):
        o, n = CB_OFF[name]
        return self.CBv[0:rows, o:o + n]

    def next_bank(self, lo=0, hi=8):
        b = lo + (self.bank_ctr % (hi - lo))
        self.bank_ctr += 1
        return b

    def pk(self, b):
        return self.PSB[b].k("all")

    def ss_col(self):
        c = self.ss_ctr % 64
        self.ss_ctr += 1
        return c, self.CONSTB.k("ss", c)

    def setup(self):
        s, I = self.s, self.I
        kC = self.kC
        kCF, kCB = self.CONSTB.k("cf"), self.CONSTB.k("cb")
        self.dma("sp", self.CFv, I["constf"], (), [kCF])
        self.dma("pool", self.CBv, I["constb"], (), [kCB])
        nc_ = self.nc
        s.add("dve", lambda: nc_.vector.memset(self.SS[:, 63:64], 0.0), [kCF, kCB], [kC])
        self.dma("sp", self.ALB, I["a_log"].partition_broadcast(128), (), [self.CONSTB.k("alb")])
        self.dma("sp", self.DTB, I["dt_bias"].partition_broadcast(128), (), [self.CONSTB.k("dtb")])
        self.dma("pool", self.WPOOL, I["w_pool"].rearrange("g c d -> c g d"), (), [self.CONSTB.k("wpool")])
        self.ts("dve", self.NWPOOL, self.WPOOL, -1.0, None, ALU.mult, None, [self.CONSTB.k("wpool")], [self.CONSTB.k("nwpool")])
        SB = s.alloc("setup", 0, 64 * KB)
        kv = SB.k("vec")
        VEC = self.view(48 * KB, F32, [32, 1536])
        self.memset("dve", VEC, 0.0, [kv])
        rows = [
            (0, 2, 1024, I["cv"]),
            (2, 1, 1024, I["g_pre_mix"].rearrange("(o f) -> o f", o=1)),
            (3, 1, 1024, I["g_post_mix"].rearrange("(o f) -> o f", o=1)),
            (4, 1, 1024, I["g_pre_ffn"].rearrange("(o f) -> o f", o=1)),
            (5, 1, 1024, I["g_post_ffn"].rearrange("(o f) -> o f", o=1)),
            (6, 6, 1024, I["b_mod"].rearrange("(i f) -> i f", i=6)),
            (12, 5, 1536, I["conv_w"]),
            (17, 1, 512, I["pool_scale"].rearrange("(o f) -> o f", o=1)),
            (18, 1, 128, I["g_onorm"].rearrange("(o f) -> o f", o=1)),
        ]
        for (r0, nr, n, src) in rows:
            self.dma("sp", VEC[r0:r0 + nr, 0:n], src, (), [kv])
        b = self.next_bank()
        for c in range(12):
            self.tr(self.ps[b][:, c * 32:(c + 1) * 32], VEC[0:32, c * 128:(c + 1) * 128], self.CF("IDENT", 32)[:, 0:32], [kv, kC], [self.pk(b)])
        kVT = self.CONSTB.k("vt")
        self.kVT = kVT
        self.cp("dve", self.VT, self.ps[b][:, 0:384].rearrange("p (c r) -> p c r", c=12), [self.pk(b)], [kVT])
        kSCT = self.CONSTB.k("sct")
        self.act(self.SCT, self.VT[:, 0:8, 0:2], AF.Silu, [kVT], [kSCT])
        kNEGA = self.CONSTB.k("nega")
        self.act(self.NEGA, self.ALB, AF.Exp, [self.CONSTB.k("alb")], [kNEGA])
        self.ts("dve", self.NEGA, self.NEGA, -1.0, None, ALU.mult, None, [kNEGA], [kNEGA])
        SCB = self.view(54 * KB, BF16, [128, 8, 2])
        kSCB = SB.k("scb")
        self.cp("dve", SCB, self.SCT, [kSCT], [kSCB])
        MROW = self.view(55 * KB, F32, [2, 6 * D])
        kMR = SB.k("mrow")
        w_mod_v = I["w_mod"].rearrange("(k p) c -> p k c", p=128)
        WMP = [self.view(i * 8 * KB, BF16, [128, 8, 512]) for i in range(4)]
        for ct in range(12):
            kw = SB.k("wmp", ct % 4)
            self.dma("pool", WMP[ct % 4], w_mod_v[:, :, ct * 512:(ct + 1) * 512], (), [kw])
            b = self.next_bank()
            for kc in range(8):
                self.mm(self.ps[b][0:2, 0:512], SCB[:, kc, :], WMP[ct % 4][:, kc, :], kc == 0, kc == 7, [kw, kSCB], [self.pk(b)])
            self.cp("act" if ct % 2 else "dve", MROW[:, ct * 512:(ct + 1) * 512], self.ps[b][0:2, 0:512], [self.pk(b)], [kMR])
        bm = self.next_bank()
        for oc in range(48):
            self.tr(self.ps[bm][:, 2 * oc:2 * oc + 2], MROW[0:2, oc * 128:(oc + 1) * 128], self.CF("IDENT", 2)[:, 0:2], [kMR, kC], [self.pk(bm)])
        kMOD = self.CONSTB.k("modt")
        for i in range(6):
            self.tt("dve", self.MODT[:, i * 8:(i + 1) * 8, :], self.ps[bm][:, i * 16:(i + 1) * 16].rearrange("p (k v) -> p k v", v=2),
                    self.VT[:, 0:8, 6 + i:7 + i].to_broadcast([128, 8, 2]), ALU.add, [self.pk(bm), kVT], [kMOD])
        kD = self.CONSTB.k("derived")
        self.kD = kD

        def vcol(j):
            return self.VT[:, 0:8, j:j + 1].to_broadcast([128, 8, 2])

        self.stt("dve", self.A1, self.MODT[:, 8:16, :], 1.0, vcol(2), ALU.add, ALU.mult, [kMOD, kVT], [kD])
        self.stt("dve", self.A2, self.MODT[:, 32:40, :], 1.0, vcol(4), ALU.add, ALU.mult, [kMOD, kVT], [kD])
        self.tt("dve", self.G1, self.MODT[:, 16:24, :], vcol(3), ALU.mult, [kMOD, kVT], [kD])
        self.tt("dve", self.G2, self.MODT[:, 40:48, :], vcol(5), ALU.mult, [kMOD, kVT], [kD])
        s.retire(SB)
        self.tap("modt", self.MODT.rearrange("p a b -> p (a b)"), [128, 96], reads=[kMOD])
        self.tap("vt", self.VT.rearrange("p a b -> p (a b)"), [128, 384], reads=[kVT])

    def run_group(self, g):
        st = self.stop_after[1] if (self.stop_after is not None and self.stop_after[0] == g["name"]) else None
        self.phase_ht(g)
        if st == "ht":
            return
        self.phase_proj1(g)
        if st is not None and (st == "proj1" or st.startswith("p1_")):
            return
        self.phase_dn(g)
        if st == "dn":
            return
        self.phase_m1(g)
        if st == "m1":
            return
        self.phase_m2(g)
        if st == "m2":
            return
        self.phase_ffn(g)

    def rstd_ops(self, c, kss, n_inv, kout):
        SS = self.SS
        self.ts("pool", SS[:, c:c + 1], SS[:, c:c + 1], n_inv, EPS, ALU.mult, ALU.add, [kss], [kss])
        self.tt("pool", SS[:, c:c + 1], SS[:, c:c + 1], self.CF("NEG05"), ALU.pow, [kss, self.kC], [kout])

    def norm_stage_a(self, src, src_keys, XN, kXN):
        c, kss = self.ss_col()
        self.act(XN, src, AF.Square, src_keys, [kXN, kss], accum_out=self.SS[:, c:c + 1])
        self.rstd_ops(c, kss, 1.0 / D, kss)
        self.ts("dve", XN, src, self.SS[:, c:c + 1], None, ALU.mult, None, src_keys, [kXN], sreads=[kss])

    def norm_stage_b(self, XN, kXN, dst, kdst, A, B):
        b = self.next_bank()
        for k in range(8):
            self.tr(self.psb[b][:, k * 128:(k + 1) * 128], XN[:, k * 128:(k + 1) * 128], self.CBc("IDENTB"), [kXN, self.kC], [self.pk(b)])
        for k in range(8):
            if k % 2 == 0:
                self.act(dst[:, k, :], self.psb[b][:, k * 128:(k + 1) * 128], AF.Identity, [self.pk(b)], [kdst],
                         scale=A[:, k:k + 1], bias=B[:, k:k + 1], sreads=[self.kD])
            else:
                self.ts("dve", dst[:, k, :], self.psb[b][:, k * 128:(k + 1) * 128], A[:, k:k + 1], B[:, k:k + 1], ALU.mult, ALU.add,
                        [self.pk(b)], [kdst], sreads=[self.kD])

    def norm_transpose_block(self, src, src_keys, XN, kXN, dst, kdst, A, B, bufafter=()):
        self.norm_stage_a(src, src_keys, XN, kXN)
        self.norm_stage_b(XN, kXN, dst, kdst, A, B)

    def phase_ht(self, g):
        s = self.s
        Tg, v = g["Tg"], g["v"]
        NB = Tg // 128
        HTb = s.alloc("HT", 0, 32 * KB)
        HT = self.view(0, BF16, [128, 8, 2048])
        NS_ = 6
        LA_ = 5
        XLb = s.alloc("XL", 112 * KB, NS_ * 4 * KB)
        XNb = s.alloc("XN", 136 * KB, NS_ * 2 * KB)
        XL = [self.view(112 * KB + i * 4 * KB, F32, [128, D]) for i in range(NS_)]
        XN = [self.view(136 * KB + i * 2 * KB, BF16, [128, D]) for i in range(NS_)]
        A = self.A1[:, :, v]
        B = self.MODT[:, 0:8, v]

        def stage_a(blk):
            sl = blk % NS_
            kx = XLb.k(sl)
            self.dma("sp", XL[sl], g["x"][blk * 128:(blk + 1) * 128, :], (), [kx])
            self.norm_stage_a(XL[sl], [kx], XN[sl], XNb.k(sl))

        for blk in range(min(LA_, NB)):
            stage_a(blk)
        for blk in range(NB):
            if blk + LA_ < NB:
                stage_a(blk + LA_)
            sl = blk % NS_
            self.norm_stage_b(XN[sl], XNb.k(sl), HT[:, :, blk * 128:(blk + 1) * 128], HTb.k(blk), A, B)
        s.retire(XLb)
        s.retire(XNb)
        g["HTb"], g["HT"] = HTb, HT
        if self.taps is not None and g["name"] == "p":
            self.tap("ht_" + g["name"], HT[:, :, 0:Tg].rearrange("p a b -> p (a b)") if Tg == 2048 else HT[:, 0, 0:Tg], [128, 8 * Tg if Tg == 2048 else Tg],
                     dtype=BF16, reads=[HTb.k(b) for b in range(NB)])

    def wp_views(self):
        if not hasattr(self, "WPb"):
            self.WPb = Buf("WP", 152 * KB, 24 * KB)
            self.WPv = [self.view(152 * KB + i * 4 * KB, BF16, [128, 8, 256]) for i in range(NWP)]
            self.wp_ctr = 0
        return self.WPb, self.WPv

    def wp_load(self, parts):
        WPb, WPv = self.wp_views()
        sl = self.wp_ctr % NWP
        self.wp_ctr += 1
        for (kc0, nk, n, src) in parts:
            self.dma("pool", WPv[sl][:, kc0:kc0 + nk, 0:n], src, (), [WPb.k(sl)])
        return sl

    def seg_iter(self, g, t0, n):
        out = []
        for si, (so, T) in enumerate(g["seqs"]):
            a = max(t0, so)
            b = min(t0 + n, so + T)
            if a < b:
                out.append((si, a - so, a - t0, b - a))
        return out

    def phase_proj1(self, g):
        s, I = self.s, self.I
        Tg, v = g["Tg"], g["v"]
        NB = Tg // 128
        NT = Tg // 512
        HT, HTb = g["HT"], g["HTb"]
        seqs = g["seqs"]
        nseq, T = len(seqs), seqs[0][1]
        WPb, WPv = self.wp_views()
        w_in_v = I["w_in"].rearrange("(k p) c -> p k c", p=128)
        panels = [("q0", 0, 256), ("q1", 256, 256), ("k0", 512, 256), ("k1", 768, 256), ("v0", 1024, 256), ("v1", 1280, 256),
                  ("z0", 1536, 256), ("z1", 1792, 256), ("ab", 2048, 16), ("u0", 2064, 256), ("u1", 2320, 256)]
        issued = [0]
        slot_of = {}

        def issue_upto(i):
            while issued[0] <= min(i, len(panels) - 1):
                j = issued[0]
                _, c0, n = panels[j]
                slot_of[j] = self.wp_load([(0, 8, n, w_in_v[:, :, c0:c0 + n])])
                issued[0] += 1

        QKVb = s.alloc("QKV", 64 * KB, 48 * KB)
        QKV = self.view(64 * KB, BF16, [128, 12, 2048])
        ZSb = s.alloc("ZS", 32 * KB, 16 * KB)
        ZS = self.view(32 * KB, BF16, [128, 4, 2048])
        YBb = s.alloc("YB", 48 * KB, 16 * KB)
        YB = self.view(48 * KB, BF16, [128, 4, 2048])
        GTS = self.view(148 * KB, F32, [128, 16, 6, 8])
        EGL = self.view(151 * KB, F32, [128, 16, 16])
        g.update(QKVb=QKVb, QKV=QKV, ZSb=ZSb, ZS=ZS, YBb=YBb, YB=YB, GTS=GTS, EGL=EGL)
        RAWb = s.alloc("RAW", 124 * KB, 10 * KB)
        RAW = [self.view(124 * KB + i * 4608, BF16, [128, nseq, T + 4]) for i in range(2)]
        CDb = s.alloc("CONVD", 148 * KB, 2560)
        CD = [self.view(148 * KB + i * 1280, BF16, [128, 5, 128]) for i in range(2)]
        kC, kVT = self.kC, self.kVT
        for i in range(2):
            self.memset("pool", RAW[i], 0.0, [RAWb.k(i)])
        ht_keys = lambda t0, n: [HTb.k(b) for b in range(t0 // 128, (t0 + n) // 128)]

        def proj_tile(piece0, oc, t0, bank, m=128):
            sl = slot_of[piece0 + oc // 2]
            for kc in range(8):
                self.mm(self.ps[bank][0:m, 0:512], WPv[sl][:, kc, (oc % 2) * 128:(oc % 2) * 128 + m], HT[:, kc, t0:t0 + 512], kc == 0, kc == 7,
                        [WPb.k(sl)] + ht_keys(t0, 512), [self.pk(bank)])

        LOOK = NWP - 2
        issue_upto(LOOK)
        def qkv_proj(ch):
            pi, oc = ch // 4, ch % 4
            issue_upto(2 * pi + oc // 2 + LOOK)
            rs = ch % 2
            for j in range(5):
                self.ts("pool" if j % 2 else "dve", CD[rs][:, j, :], self.CBc("IDENTB"), self.VT[:, ch, 12 + j:13 + j], None, ALU.mult, None,
                        [kC], [CDb.k(rs)], sreads=[kVT])
            for tt in range(NT):
                t0 = tt * 512
                bank = self.next_bank()
                proj_tile(2 * pi, oc, t0, bank)
                for (si, ts_, ot, n) in self.seg_iter(g, t0, 512):
                    self.cp("act" if (tt + si) % 2 else "dve", RAW[rs][:, si, 2 + ts_:2 + ts_ + n], self.ps[bank][:, ot:ot + n],
                            [self.pk(bank)], [RAWb.k(rs)])

        def qkv_conv(ch):
            rs = ch % 2
            for tt in range(NT):
                t0 = tt * 512
                bank = self.next_bank()
                for (si, ts_, ot, n) in self.seg_iter(g, t0, 512):
                    for j in range(5):
                        self.mm(self.ps[bank][:, ot:ot + n], CD[rs][:, j, :], RAW[rs][:, si, ts_ + j:ts_ + j + n], j == 0, j == 4,
                                [CDb.k(rs), RAWb.k(rs)], [self.pk(bank)])
                self.act(QKV[:, ch, t0:t0 + 512], self.ps[bank][:, 0:512], AF.Silu, [self.pk(bank)],
                         [QKVb.k(ch, b) for b in range(t0 // 128, t0 // 128 + 4)])

        SQb = s.alloc("SQ", 112 * KB, 2 * KB)
        SQ = [self.view(112 * KB + i * KB, BF16, [128, 512]) for i in range(2)]
        RIb = s.alloc("RINV", 144 * KB, 4 * KB)
        RI = [self.view(144 * KB + i * 2 * KB, F32, [128, 512]) for i in range(2)]
        import math
        l2it = [0]

        def l2norm_chunk(ch):
            lnsc = math.log(128 ** -0.5) if ch < 4 else 0.0
            for tt in range(NT):
                t0 = tt * 512
                r = l2it[0] % 2
                l2it[0] += 1
                qk = [QKVb.k(ch, b) for b in range(t0 // 128, t0 // 128 + 4)]
                qv = QKV[:, ch, t0:t0 + 512]
                self.tt("dve", SQ[r], qv, qv, ALU.mult, qk, [SQb.k(r)])
                bank = self.next_bank()
                self.mm(self.ps[bank][:, 0:512], self.CBc("ONESB"), SQ[r], True, True, [SQb.k(r), kC], [self.pk(bank)])
                self.act(RI[r], self.ps[bank][:, 0:512], AF.Ln, [self.pk(bank)], [RIb.k(r)], bias=EPS)
                self.act(RI[r], RI[r], AF.Exp, [RIb.k(r)], [RIb.k(r)], scale=-0.5, bias=lnsc)
                self.tt("dve", qv, qv, RI[r], ALU.mult, qk + [RIb.k(r)], qk)

        qkv_proj(0)
        for ch in range(12):
            if ch + 1 < 12:
                qkv_proj(ch + 1)
            qkv_conv(ch)
            if 1 <= ch <= 8:
                l2norm_chunk(ch - 1)
        s.retire(SQb)
        s.retire(RIb)
        s.retire(RAWb)
        s.retire(CDb)
        sub = self.stop_after[1] if (self.stop_after is not None and self.stop_after[0] == g["name"]) else None
        if sub == "p1_qkv":
            return
        for oc in range(4):
            issue_upto(6 + oc // 2 + LOOK)
            for tt in range(NT):
                t0 = tt * 512
                bank = self.next_bank()
                proj_tile(6, oc, t0, bank)
                self.act(ZS[:, oc, t0:t0 + 512], self.ps[bank][:, 0:512], AF.Silu, [self.pk(bank)],
                         [ZSb.k(b) for b in range(t0 // 128, t0 // 128 + 4)])
        if sub == "p1_z":
            return
        issue_upto(10)
        sl = slot_of[8]
        GTSb = s.alloc("GTS", 148 * KB, 4 * KB)
        g["GTSb"] = GTSb
        GTb = s.alloc("GTMP", 114 * KB, 4 * KB)
        GT = self.view(114 * KB, F32, [128, 16, 48])[:, 0:NB, :]
        kalb, kdtb, knega = self.CONSTB.k("alb"), self.CONSTB.k("dtb"), self.CONSTB.k("nega")
        kg = GTb.k("a")
        kp6, kp7 = self.pk(6), self.pk(7)
        allg = [GTSb.k(b) for b in range(NB)]
        alle = [GTSb.k("egl", b) for b in range(NB)]
        for blk in range(NB):
            for kc in range(8):
                self.mm(self.ps[6][:, blk * 16:(blk + 1) * 16], HT[:, kc, blk * 128:(blk + 1) * 128], WPv[sl][:, kc, 0:16], kc == 0, kc == 7,
                        [WPb.k(sl), HTb.k(blk)], [kp6])
        pv = self.ps[6][:, 0:NB * 16].rearrange("p (b c) -> p b c", c=16)
        psA, psB = pv[:, :, 0:8], pv[:, :, 8:16]
        bc = lambda t_: t_.unsqueeze(1).to_broadcast([128, NB, 8])
        GS = lambda slot: GTS[:, 0:NB, slot, :]
        self.tt("dve", GT[:, :, 0:8], psA, bc(self.DTB), ALU.add, [kp6, kdtb], [kg])
        self.act(GT[:, :, 24:32], psB, AF.Exp, [kp6], [kg], scale=-1.0)
        self.act(GT[:, :, 8:16], GT[:, :, 0:8], AF.Exp, [kg], [kg])
        self.act(GT[:, :, 16:24], GT[:, :, 8:16], AF.Ln, [kg], [kg], bias=1.0)
        self.tt("dve", GS(5), GT[:, :, 16:24], bc(self.NEGA), ALU.mult, [kg, knega], allg)
        self.act(GT[:, :, 32:40], GT[:, :, 24:32], AF.Ln, [kg], [kg], bias=1.0)
        self.act(GS(0), GT[:, :, 32:40], AF.Exp, [kg], allg, scale=-1.0)
        for blk in range(NB):
            gtok = GTS[:, blk, 5, :]
            c0 = blk * 32
            for (cc, nm, lo) in ((0, "TRI_F", 0), (4, "TRI_B", 4), (8, "STR_F", 0), (12, "STR_B", 4)):
                self.mm(self.ps[7][:, c0 + cc:c0 + cc + 4], self.CF(nm), gtok[:, lo:lo + 4], True, True, [GTSb.k(blk), kC], [kp7])
            self.mm(self.ps[7][:, c0 + 16:c0 + 24], self.CF("IND0"), gtok, True, True, [GTSb.k(blk), kC], [kp7])
            self.mm(self.ps[7][:, c0 + 24:c0 + 32], self.CF("IND1"), gtok, True, True, [GTSb.k(blk), kC], [kp7])
        cv = self.ps[7][:, 0:NB * 32].rearrange("p (b c) -> p b c", c=32)
        self.tt("dve", GS(1), cv[:, :, 0:8], GT[:, :, 32:40], ALU.subtract, [kp7, kg], allg)
        self.ts("dve", GS(2), cv[:, :, 0:8], -1.0, None, ALU.mult, None, [kp7], allg)
        self.act(GT[:, :, 40:48], cv[:, :, 0:8], AF.Exp, [kp7], [kg])
        self.act(GS(4), cv[:, :, 8:16], AF.Exp, [kp7], allg)
        self.act(EGL[:, 0:NB, :], cv[:, :, 16:32], AF.Exp, [kp7], alle)
        self.tt("dve", GS(3), GT[:, :, 40:48], GS(0), ALU.mult, [kg] + allg, allg)
        s.retire(GTb)
        if sub == "p1_ab":
            return
        PE_ = POOL_ENG
        PAb = s.alloc("PA", 112 * KB, 12 * KB)
        PBb = s.alloc("PB", 124 * KB, 12 * KB)
        UBb = s.alloc("UB", 136 * KB, 4 * KB)
        MBb = s.alloc("MB", 140 * KB, 4 * KB)
        PAf = self.view(112 * KB, F32, [128, 3072])
        PBf = self.view(124 * KB, F32, [128, 3072])
        UB = self.view(136 * KB, BF16, [128, 2048])
        MB = self.view(140 * KB, BF16, [128, 2048])
        if g["grid"]:
            nrows, rowlen = 32, 64
            icn = "IC64_%d"
        else:
            nrows, rowlen = nseq, T
            icn = "IC256_%d"
        kPA, kPB = PAb.k("a"), PBb.k("a")
        for gi in range(4):
            w = POOL_W[gi]
            L1 = nrows * (rowlen + w)
            pa3 = PAf[:, 0:L1].rearrange("p (r c) -> p r c", r=nrows)
            self.memset(PE_, PAf[:, 0:L1], 0.0, [kPA])
            for tt in range(NT):
                t0 = tt * 512
                bank = self.next_bank()
                proj_tile(9, gi, t0, bank)
                if sub == "p1_ua":
                    return
                r0 = t0 // rowlen
                nr = 512 // rowlen
                self.cp("act", pa3[:, r0:r0 + nr, w // 2:w // 2 + rowlen], self.ps[bank][:, 0:512].rearrange("p (r c) -> p r c", r=nr),
                        [self.pk(bank)], [kPA])
                if sub == "p1_ub":
                    return
                self.cp("dve", UB[:, t0:t0 + 512], self.ps[bank][:, 0:512], [self.pk(bank)], [UBb.k(tt)])
                if sub == "p1_uc":
                    return
            cur, ck, oth, ok = PAf, kPA, PBf, kPB
            kk = 1
            while kk < w:
                no = L1 - (2 * kk - 1)
                self.tt(PE_, oth[:, 0:no], cur[:, 0:no], cur[:, kk:kk + no], ALU.add, [ck], [ok])
                cur, ck, oth, ok = oth, ok, cur, ck
                kk *= 2
            if sub == "p1_ud":
                return
            cur3 = cur[:, 0:L1].rearrange("p (r c) -> p r c", r=nrows)
            ic1 = self.CF(icn % gi)
            if not g["grid"]:
                self.tt(PE_, MB[:, 0:Tg].rearrange("p (r c) -> p r c", r=nrows), cur3[:, :, 0:rowlen],
                        ic1.unsqueeze(1).to_broadcast([128, nrows, rowlen]), ALU.mult, [ck, kC], [MBb.k("a")])
            else:
                L2 = (nrows + w) * rowlen
                self.memset(PE_, oth[:, 0:L2], 0.0, [ok])
                o3 = oth[:, 0:L2].rearrange("p (r c) -> p r c", c=rowlen)
                self.tt(PE_, o3[:, w // 2:w // 2 + nrows, :], cur3[:, :, 0:rowlen],
                        ic1.unsqueeze(1).to_broadcast([128, nrows, rowlen]), ALU.mult, [ck, kC], [ok])
                cur, ck, oth, ok = oth, ok, cur, ck
                kk = 1
                while kk < w:
                    sh = kk * rowlen
                    no = L2 - (2 * kk - 1) * rowlen
                    self.tt(PE_, oth[:, 0:no], cur[:, 0:no], cur[:, sh:sh + no], ALU.add, [ck], [ok])
                    cur, ck, oth, ok = oth, ok, cur, ck
                    kk *= 2
                c3 = cur[:, 0:L2].rearrange("p (r c) -> p r c", c=rowlen)
                ic2 = self.CF("IC32_%d" % gi)
                self.tt(PE_, MB[:, 0:Tg].rearrange("p (r c) -> p r c", c=rowlen), c3[:, 0:nrows, :],
                        ic2.unsqueeze(2).to_broadcast([128, nrows, rowlen]), ALU.mult, [ck, kC], [MBb.k("a")])
            if sub == "p1_u1":
                return
            for tt in range(NT):
                t0 = tt * 512
                bank = self.next_bank()
                self.mm(self.ps[bank][:, 0:512], self.WPOOL[:, gi, :], MB[:, t0:t0 + 512], True, False,
                        [MBb.k("a"), self.CONSTB.k("wpool")], [self.pk(bank)])
                self.mm(self.ps[bank][:, 0:512], self.NWPOOL[:, gi, :], UB[:, t0:t0 + 512], False, True,
                        [UBb.k(tt), self.CONSTB.k("nwpool")], [self.pk(bank)])
                self.ts("dve", YB[:, gi, t0:t0 + 512], self.ps[bank][:, 0:512], self.VT[:, gi, 17:18], None, ALU.mult, None,
                        [self.pk(bank)], [YBb.k(b) for b in range(t0 // 128, t0 // 128 + 4)], sreads=[kVT])
        for b_ in (PAb, PBb, UBb, MBb):
            s.retire(b_)
        s.retire(HTb)
        if self.taps is not None:
            nm = g["name"]
            allq = [QKVb.k(ch, b) for ch in range(12) for b in range(NB)]
            for ch in (0, 5, 9):
                self.tap("qkv%d_%s" % (ch, nm), QKV[:, ch, 0:Tg], [128, Tg], dtype=BF16, reads=allq)
            self.tap("zs1_" + nm, ZS[:, 1, 0:Tg], [128, Tg], dtype=BF16, reads=[ZSb.k(b) for b in range(NB)])
            for gi in (0, 3):
                self.tap("yb%d_%s" % (gi, nm), YB[:, gi, 0:Tg], [128, Tg], dtype=BF16, reads=[YBb.k(b) for b in range(NB)])
            self.tap("gts_" + nm, GTS[:, 0:NB, :, :].rearrange("p a b c -> p (a b c)"), [128, NB * 48], reads=[GTSb.k(b) for b in range(NB)])
            self.tap("egl_" + nm, EGL[:, 0:NB, :].rearrange("p a b -> p (a b)"), [128, NB * 16], reads=[GTSb.k("egl", b) for b in range(NB)])

    def phase_dn(self, g):
        s, I = self.s, self.I
        Tg = g["Tg"]
        QKV, QKVb, ZS, ZSb, GTS, GTSb, EGL = g["QKV"], g["QKVb"], g["ZS"], g["ZSb"], g["GTS"], g["GTSb"], g["EGL"]
        kC, kVT = self.kC, self.kVT
        OTb = s.alloc("OTOK", 0, 32 * KB)
        OT = self.view(0, F32, [128, 16, 512])
        WPb, _ = self.wp_views()
        DNb = s.alloc("DN", 112 * KB, 36 * KB)
        DN2b = s.alloc("DN2", 152 * KB, 19 * KB)
        DN2b.after.extend(s.collect(WPb.keys))
        dbase = [112 * KB, 152 * KB]
        xbase = [131 * KB, 140 * KB]
        dbuf = [DNb, DN2b]

        def dv(d, off, dtype, shape=(128, 4, 128)):
            return self.view(dbase[d] + off, dtype, list(shape))

        def frv(d, w0, n, dtype, pat, **kw):
            ap = self.FR[d][:, w0:w0 + n]
            if dtype != F32:
                ap = ap.bitcast(dtype)
            return ap.rearrange(pat, **kw)

        T_ = {}
        for d in range(2):
            t = dict(
                vb32=dv(d, 0, F32), E1=dv(d, 0, BF16), E3=dv(d, 1024, BF16),
                L=frv(d, 0, 512, F32, "p (h c) -> p h c", h=4), Lr=frv(d, 0, 512, F32R, "p (h c) -> p h c", h=4),
                RP=frv(d, 512, 1024, F32, "p (h a c) -> p h a c", h=4, a=2), RPr=frv(d, 512, 1024, F32R, "p (h a c) -> p h a c", h=4, a=2),
                EG=self.view(xbase[d] + 2048, BF16, [128, 4, 128]), BS=self.view(xbase[d], F32, [128, 4, 128]),
                kbg32=dv(d, 2048, F32), vnew=dv(d, 4096, BF16),
            )
            for par in range(2):
                o = 5120 + par * 7168
                t["wT32", par] = dv(d, o, F32)
                t["u", par] = dv(d, o + 2048, F32)
                t["IT", par] = dv(d, o + 4096, BF16)
                t["qdT", par] = dv(d, o + 5120, BF16)
                t["kdec", par] = dv(d, o + 6144, BF16)
            T_[d] = t
        S = self.view(134 * KB, F32, [128, 8, 128])
        Sb = self.view(138 * KB, BF16, [128, 8, 128])
        IDB = self.CBc("IDENTB")
        IDF = self.CF("IDENT")
        ID4 = self.CBc("IDENTB").unsqueeze(1).to_broadcast([128, 4, 128])
        BM4 = self.CF("BM4", 4).rearrange("p (h c) -> p h c", h=4)
        visited = set()

        def K(d, name, par=None):
            if name in ("BS", "EG"):
                return None
            return dbuf[d].k(name, par)

        def KS(d, name):
            return [dbuf[d].k(name, None)]

        def bcs(gb, slot, hd0):
            return GTS[:, gb, slot, hd0:hd0 + 4].unsqueeze(2).to_broadcast([128, 4, 128])

        def p4(bank):
            return self.ps[bank][:, 0:512].rearrange("p (h c) -> p h c", h=4)

        def pb4(bank, half=0):
            return self.psb[bank][:, half * 512:half * 512 + 512].rearrange("p (h c) -> p h c", h=4)

        def pre(d, gb, par):
            t = T_[d]
            hd0 = 4 * d
            X1, X2, X3 = 4 * d + 1, 4 * d + 2, 4 * d + 3
            t0 = gb * 128
            kgs = GTSb.k(gb)
            qk = lambda ch: QKVb.k(ch, gb)
            for h in range(4):
                self.tr(self.psb[X1][:, h * 128:(h + 1) * 128], QKV[:, 4 + h, t0:t0 + 128], IDB, [qk(4 + h), kC], [self.pk(X1)])
            yield
            self.tt("dve", t["kbg32"], pb4(X1, 0), bcs(gb, 3, hd0), ALU.mult, [self.pk(X1), kgs], [K(d, "kbg")])
            self.tt("dve", t["kdec", par], pb4(X1, 0), bcs(gb, 4, hd0), ALU.mult, [self.pk(X1), kgs], [K(d, "kdec", par)])
            yield
            for h in range(4):
                kT = QKV[:, 4 + h, t0:t0 + 128]
                self.mm(self.ps[X2][:, h * 128:(h + 1) * 128], kT, kT, True, True, [qk(4 + h)], [self.pk(X2)])
            for h in range(4):
                self.mm(self.ps[X3][:, h * 128:(h + 1) * 128], QKV[:, 4 + h, t0:t0 + 128], QKV[:, h, t0:t0 + 128], True, True,
                        [qk(4 + h), qk(h)], [self.pk(X3)])
            yield
            self.mm(self.ps[X1][0:4, 0:128], GTS[:, gb, 5, hd0:hd0 + 4], self.CF("TRI_F" if d == 0 else "TRI_B"), True, True,
                    [kgs, kC], [self.pk(X1)])
            self.tt("dve", t["BS"][0:4], self.ps[X1][0:4, 0:128].unsqueeze(1).to_broadcast([4, 4, 128]), BM4, ALU.mult,
                    [self.pk(X1), kC], KS(d, "BS"))
            bs2 = t["BS"][0:4].rearrange("p h c -> p (h c)")
            mc = self.CBc("C_F" if d == 0 else "C_B").unsqueeze(1).to_broadcast([128, 4, 128])
            m3 = self.CBc("NEG_IU" if d == 0 else "NEG_IL").unsqueeze(1).to_broadcast([128, 4, 128])
            yield
            self.mm(self.ps[X1][:, 0:512], self.CF("ONES", 4), bs2, True, False, KS(d, "BS") + [kC], [self.pk(X1)], skip=True)
            self.act(t["EG"], p4(X1), AF.Exp, [self.pk(X1)], KS(d, "EG"))
            yield
            self.mm(p4(X1), IDB, m3, False, False, [kC], [self.pk(X1)], skip=True)
            for h in range(4):
                self.act(t["E3"][:, h, :], self.ps[X1][:, h * 128:(h + 1) * 128], AF.Exp, [self.pk(X1)], [K(d, "E3")],
                         bias=GTS[:, gb, 2, hd0 + h:hd0 + h + 1], sreads=[kgs])
            yield
            self.mm(p4(X1), IDB, mc, False, True, [kC], [self.pk(X1)], skip=True)
            for h in range(4):
                self.act(t["E1"][:, h, :], self.ps[X1][:, h * 128:(h + 1) * 128], AF.Exp, [self.pk(X1)], [K(d, "E1")],
                         scale=-1.0, bias=GTS[:, gb, 1, hd0 + h:hd0 + h + 1], sreads=[kgs])
            yield
            self.tt("dve", t["Lr"], p4(X2), t["E1"], ALU.mult, [self.pk(X2), K(d, "E1")], [K(d, "L")])
            self.tt("dve", t["IT", par], p4(X3), t["E3"], ALU.mult, [self.pk(X3), K(d, "E3")], [K(d, "IT", par)])
            self.tt("pool", t["qdT", par], QKV[:, 0:4, t0:t0 + 128], t["EG"], ALU.mult, [qk(0), qk(1), qk(2), qk(3)] + KS(d, "EG"), [K(d, "qdT", par)])
            yield
            for h in range(4):
                self.tr(self.ps[X2][:, h * 128:(h + 1) * 128], t["L"][:, h, :], IDF, [K(d, "L"), kC], [self.pk(X2)])
            yield
            self.cp("act", t["RPr"][:, :, 0, :], p4(X2), [self.pk(X2)], [K(d, "R")])
            self.tt("dve", t["RPr"][:, :, 1, :], ID4, p4(X2), ALU.subtract, [self.pk(X2), kC], [K(d, "P")])
            yield
            Lr, RPr = t["Lr"], t["RPr"]
            XA = (X1, X2)
            for lev in range(6):
                for h in range(4):
                    bk = XA[h // 2]
                    c0 = (h % 2) * 256
                    self.mm(self.ps[bk][:, c0:c0 + 256], Lr[:, h, :], RPr[:, h, :, :].rearrange("p a c -> p (a c)"), True, True,
                            [K(d, "L"), K(d, "R"), K(d, "P")], [self.pk(bk)])
                if lev < 5:
                    for h in range(3):
                        self.mm(self.ps[X3][:, h * 128:h * 128 + 256], RPr[:, h, 0, :], Lr[:, h:h + 2, :].rearrange("p a c -> p (a c)"), True, True,
                                [K(d, "R"), K(d, "L")], [self.pk(X3)])
                    self.mm(self.ps[X3][:, 384:512], RPr[:, 3, 0, :], Lr[:, 3, :], True, True, [K(d, "R"), K(d, "L")], [self.pk(X3)])
                yield
                if lev < 5:
                    self.cp("act", Lr, p4(X3), [self.pk(X3)], [K(d, "L")])
                for i, bk in enumerate(XA):
                    pv = self.ps[bk][:, 0:512].rearrange("p (h a c) -> p h a c", h=2, a=2)
                    if lev < 5:
                        self.cp("act" if i else "dve", RPr[:, 2 * i:2 * i + 2, 0, :], pv[:, :, 0, :], [self.pk(bk)], [K(d, "R")])
                    if lev >= 1:
                        self.tt("dve", RPr[:, 2 * i:2 * i + 2, 1, :], t["RP"][:, 2 * i:2 * i + 2, 1, :], pv[:, :, 1, :], ALU.add,
                                [K(d, "P"), self.pk(bk)], [K(d, "P")])
                    yield
            for h in range(4):
                self.tr(self.psb[X2][:, h * 128:(h + 1) * 128], QKV[:, 8 + h, t0:t0 + 128], IDB, [qk(8 + h), kC], [self.pk(X2)])
            yield
            self.tt("dve", t["vb32"], pb4(X2, 0), bcs(gb, 0, hd0), ALU.mult, [self.pk(X2), kgs], [K(d, "E1"), K(d, "E3")])
            yield
            for h in range(4):
                self.mm(self.ps[X3][:, h * 128:(h + 1) * 128], t["RP"][:, h, 1, :], t["vb32"][:, h, :], True, True,
                        [K(d, "P"), K(d, "E1"), K(d, "E3")], [self.pk(X3)])
            for h in range(4):
                self.mm(self.ps[X1][:, h * 128:(h + 1) * 128], t["kbg32"][:, h, :], t["RP"][:, h, 1, :], True, True,
                        [K(d, "P"), K(d, "kbg")], [self.pk(X1)])
            yield
            self.cp("act", t["u", par], p4(X3), [self.pk(X3)], [K(d, "u", par)])
            self.cp("dve", t["wT32", par], p4(X1), [self.pk(X1)], [K(d, "wT32", par)])
            yield

        def scan(d, gb, par, first_o, tail=()):
            t = T_[d]
            hd0 = 4 * d
            X0 = 4 * d
            kS = DNb.k("S", d)
            kSb = DNb.k("Sb", d)
            wT, u, IT, qdT, kdec = t["wT32", par], t["u", par], t["IT", par], t["qdT", par], t["kdec", par]
            for c in ((0, 1) if d == 0 else (1, 0)):
                r0, r1 = c * 64, (c + 1) * 64
                for h in range(4):
                    self.mm(self.ps[X0][r0:r1, h * 128:(h + 1) * 128], wT[:, h, r0:r1], S[:, hd0 + h, :], True, True,
                            [K(d, "wT32", par), kS], [self.pk(X0)])
                yield
                self.tt("dve", t["vnew"][r0:r1].rearrange("p h c -> p (h c)"), u[r0:r1].rearrange("p h c -> p (h c)"),
                        self.ps[X0][r0:r1, 0:512], ALU.subtract, [K(d, "u", par), self.pk(X0)], [K(d, "vnew")])
                yield
                for h in range(4):
                    self.mm(self.ps[X0][r0:r1, h * 128:(h + 1) * 128], qdT[:, h, r0:r1], Sb[:, hd0 + h, :], h == 0, False,
                            [K(d, "qdT", par), kSb], [self.pk(X0)], skip=True)
                for h in range(4):
                    self.mm(self.ps[X0][r0:r1, h * 128:(h + 1) * 128], IT[r0:r1, h, r0:r1], t["vnew"][r0:r1, h, :], False, True,
                            [K(d, "IT", par), K(d, "vnew")], [self.pk(X0)], skip=True)
                yield
                ko = OTb.k(gb, c)
                if first_o:
                    self.cp("act", OT[r0:r1, gb, :], self.ps[X0][r0:r1, 0:512], [self.pk(X0)], [ko])
                else:
                    self.tt("dve", OT[r0:r1, gb, :], OT[r0:r1, gb, :], self.ps[X0][r0:r1, 0:512], ALU.add, [ko, self.pk(X0)], [ko])
                yield
                for h in range(4):
                    self.mm(self.ps[X0][:, h * 128:(h + 1) * 128], kdec[r0:r1, h, :], t["vnew"][r0:r1, h, :], True, True,
                            [K(d, "kdec", par), K(d, "vnew")], [self.pk(X0)])
                yield
                for h in range(4):
                    hd = hd0 + h
                    self.stt("dve", S[:, hd, :], S[:, hd, :], EGL[:, gb, c * 8 + hd:c * 8 + hd + 1], self.ps[X0][:, h * 128:(h + 1) * 128],
                             ALU.mult, ALU.add, [kS, self.pk(X0)], [kS], sreads=[GTSb.k("egl", gb)])
                self.cp("act", Sb[:, hd0:hd0 + 4, :], S[:, hd0:hd0 + 4, :], [kS], [kSb])
                yield
            for (pgb, slot) in tail:
                post_b(pgb, slot, X0)
                yield

        import itertools

        def run_il(gens):
            for _ in itertools.zip_longest(*gens):
                pass

        steps = []
        for si, (so, T) in enumerate(g["seqs"]):
            nblk = T // 128
            for n in range(nblk):
                steps.append((si, so // 128, nblk, n))
        blk_of = lambda st, d: st[1] + (st[3] if d == 0 else st[2] - 1 - st[3])

        def init_state(si):
            for d in range(2):
                kS, kSb = DNb.k("S", d), DNb.k("Sb", d)
                if g["s0"] is None:
                    self.memset("pool", S[:, 4 * d:4 * d + 4, :], 0.0, [kS])
                    self.memset("pool", Sb[:, 4 * d:4 * d + 4, :], 0.0, [kSb])
                else:
                    self.dma("sp", S[:, 4 * d:4 * d + 4, :], g["s0"][4 * d:4 * d + 4].rearrange("h k v -> k h v"), (), [kS])
                    self.cp("act", Sb[:, 4 * d:4 * d + 4, :], S[:, 4 * d:4 * d + 4, :], [kS], [kSb])

        ONb = s.alloc("ON", 143 * KB, 3 * KB)
        ON = [self.view(143 * KB + i * 1024, BF16, [128, 4, 128]) for i in range(2)]
        JK = self.view(145 * KB, BF16, [128, 128])

        def post_a(gb, slot):
            ko = [OTb.k(gb, 0), OTb.k(gb, 1)]
            c0 = (self.ss_ctr // 4 * 4 + 4) % 64
            self.ss_ctr = c0 + 4
            kss = self.CONSTB.k("ss4", c0)
            for h in range(4):
                self.act(JK, OT[:, gb, h * 128:(h + 1) * 128], AF.Square, ko, [ONb.k("jk"), kss], accum_out=self.SS[:, c0 + h:c0 + h + 1])
            yield
            self.ts("pool", self.SS[:, c0:c0 + 4], self.SS[:, c0:c0 + 4], 1.0 / 128, EPS, ALU.mult, ALU.add, [kss], [kss])
            self.tt("pool", self.SS[:, c0:c0 + 4], self.SS[:, c0:c0 + 4], self.CF("NEG05").to_broadcast([128, 4]), ALU.pow, [kss, kC], [kss])
            yield
            self.tt("dve", ON[slot], OT[:, gb, :].rearrange("p (h c) -> p h c", h=4),
                    self.SS[:, c0:c0 + 4].unsqueeze(2).to_broadcast([128, 4, 128]), ALU.mult, ko + [kss], [ONb.k(slot)])
            yield

        def post_b(gb, slot, bank):
            t0 = gb * 128
            for h in range(4):
                self.tr(self.psb[bank][:, h * 128:(h + 1) * 128], ON[slot][:, h, :], IDB, [ONb.k(slot), kC], [self.pk(bank)])
            self.stt("dve", ZS[:, 0:4, t0:t0 + 128], pb4(bank, 0), self.VT[:, 0, 18:19], ZS[:, 0:4, t0:t0 + 128], ALU.mult, ALU.mult,
                     [self.pk(bank), ZSb.k(gb)], [ZSb.k(gb)], sreads=[kVT])

        vcount = {}
        pending = []
        run_il([pre(d, blk_of(steps[0], d), 0) for d in range(2)])
        for i, st in enumerate(steps):
            si, gb0, nblk, n = st
            if n == 0:
                init_state(si)
            gl = []
            done_now = []
            for d in range(2):
                gb = blk_of(st, d)
                tail = [(pending[d], d)] if d < len(pending) else []
                gl.append(scan(d, gb, i % 2, gb not in visited, tail))
                visited.add(gb)
                vcount[gb] = vcount.get(gb, 0) + 1
                if vcount[gb] == 2:
                    done_now.append(gb)
            if i + 1 < len(steps):
                gl += [pre(d, blk_of(steps[i + 1], d), (i + 1) % 2) for d in range(2)]
            gl += [post_a(pgb, j) for j, pgb in enumerate(pending)]
            pending = done_now
            run_il(gl)
            if n == nblk - 1 and g["ns"] is not None:
                op = self.dma("sp", self.O["ns"][g["ns"][si]].rearrange("h k v -> k h v"), S, [DNb.k("S", 0), DNb.k("S", 1)], ())
                self.out_dmas.append(op)
        run_il([post_a(pgb, j) for j, pgb in enumerate(pending)])
        for j, pgb in enumerate(pending):
            post_b(pgb, j, 4 * j)
        s.retire(ONb)
        s.retire(DNb)
        WPb.after = s.collect(DN2b.keys)
        base = 112 * KB
        s.retire(OTb)
        s.retire(QKVb)
        s.retire(GTSb)
        g["YA"], g["YAb"] = ZS, ZSb
        if self.taps is not None:
            nm = g["name"]
            for h in (0, 2):
                self.tap("ya%d_%s" % (h, nm), ZS[:, h, 0:Tg], [128, Tg], dtype=BF16, reads=[ZSb.k(b) for b in range(Tg // 128)])

    def build_grow(self, Gt, v, dst_off, buf, tmp_off, tmpbuf, dgkey=None):
        dst = self.view(dst_off, F32, [128, D])
        DG = [self.view(tmp_off + i * 512, F32, [128, 128]) for i in range(2)]
        banks = [self.next_bank(), self.next_bank()]
        for k in range(8):
            r = k % 2
            kdg = dgkey if dgkey is not None else tmpbuf.k("dg", r)
            self.ts("dve", DG[r], self.CF("IDENT"), Gt[:, k, v:v + 1], None, ALU.mult, None, [self.kC], [kdg], sreads=[self.kD])
            b = banks[k // 4]
            self.mm(self.ps[b][:, (k % 4) * 128:(k % 4 + 1) * 128], self.CF("ONES"), DG[r], True, True, [kdg, self.kC], [self.pk(b)])
        for hf in range(2):
            self.cp("act" if hf else "dve", dst[:, hf * 512:(hf + 1) * 512], self.ps[banks[hf]][:, 0:512], [self.pk(banks[hf])], [buf.k("grow")])
        return dst

    def norm_residual(self, banks, resid, resid_keys, grow, kgrow, dst, dst_keys, T1, kT1):
        c0 = (self.ss_ctr // 2 * 2 + 2) % 64
        self.ss_ctr = c0 + 2
        kss = self.CONSTB.k("ss2", c0)
        SS = self.SS
        for hf in range(2):
            self.act(T1[hf], self.ps[banks[hf]][:, 0:512], AF.Square, [self.pk(banks[hf])], [kT1[hf], kss], accum_out=SS[:, c0 + hf:c0 + hf + 1])
        self.tt("pool", SS[:, c0:c0 + 1], SS[:, c0:c0 + 1], SS[:, c0 + 1:c0 + 2], ALU.add, [kss], [kss])
        self.rstd_ops(c0, kss, 1.0 / D, kss)
        for hf in range(2):
            self.stt("dve", T1[hf], self.ps[banks[hf]][:, 0:512], SS[:, c0:c0 + 1], grow[:, hf * 512:(hf + 1) * 512], ALU.mult, ALU.mult,
                     [self.pk(banks[hf]), kgrow], [kT1[hf]], sreads=[kss])
            self.tt("pool", dst[:, hf * 512:(hf + 1) * 512], T1[hf], resid[:, hf * 512:(hf + 1) * 512], ALU.add,
                    [kT1[hf]] + list(resid_keys), list(dst_keys))

    def phase_m1(self, g):
        s, I = self.s, self.I
        Tg = g["Tg"]
        NT = Tg // 512
        self.phase_ht(g)
        HT, HTb, YA, YAb, YB, YBb = g["HT"], g["HTb"], g["YA"], g["YAb"], g["YB"], g["YBb"]
        MTb = s.alloc("MT", 64 * KB, 32 * KB)
        MT = self.view(64 * KB, BF16, [128, 8, 2048])
        SCb = s.alloc("M1S", 96 * KB, 12 * KB)
        SA = [self.view(96 * KB + i * KB, BF16, [128, 512]) for i in range(4)]
        MM = [self.view(100 * KB + i * 2 * KB, F32, [128, 512]) for i in range(4)]
        WPb, WPv = self.wp_views()
        w_in_v = I["w_in"].rearrange("(k p) c -> p k c", p=128)
        w_a_v = I["w_a"].rearrange("(k p) c -> p k c", p=128)
        w_b_v = I["w_b"].rearrange("(k p) c -> p k c", p=128)
        it = 0

        def m1_load(pp):
            c = 256 * pp
            return (self.wp_load([(0, 8, 256, w_in_v[:, :, 2576 + c:2576 + c + 256])]),
                    self.wp_load([(0, 8, 256, w_in_v[:, :, 3600 + c:3600 + c + 256])]),
                    self.wp_load([(0, 4, 256, w_a_v[:, :, c:c + 256]), (4, 4, 256, w_b_v[:, :, c:c + 256])]))

        nxt = m1_load(0)
        for pp in range(4):
            sl = nxt
            if pp + 1 < 4:
                nxt = m1_load(pp + 1)
            for tt in range(NT):
                t0 = tt * 512
                hk = [HTb.k(b) for b in range(t0 // 128, t0 // 128 + 4)]
                yak = [YAb.k(b) for b in range(t0 // 128, t0 // 128 + 4)]
                ybk = [YBb.k(b) for b in range(t0 // 128, t0 // 128 + 4)]
                for oc2 in range(2):
                    oc = pp * 2 + oc2
                    cs = slice(oc2 * 128, (oc2 + 1) * 128)
                    bGA, bGB, bPA, bPB = [self.next_bank() for _ in range(4)]
                    for kc in range(8):
                        self.mm(self.ps[bGA][:, 0:512], WPv[sl[0]][:, kc, cs], HT[:, kc, t0:t0 + 512], kc == 0, kc == 7, [WPb.k(sl[0])] + hk, [self.pk(bGA)])
                    for kc in range(8):
                        self.mm(self.ps[bGB][:, 0:512], WPv[sl[1]][:, kc, cs], HT[:, kc, t0:t0 + 512], kc == 0, kc == 7, [WPb.k(sl[1])] + hk, [self.pk(bGB)])
                    for kc in range(4):
                        self.mm(self.ps[bPA][:, 0:512], WPv[sl[2]][:, kc, cs], YA[:, kc, t0:t0 + 512], kc == 0, kc == 3, [WPb.k(sl[2])] + yak, [self.pk(bPA)])
                    for kc in range(4):
                        self.mm(self.ps[bPB][:, 0:512], WPv[sl[2]][:, 4 + kc, cs], YB[:, kc, t0:t0 + 512], kc == 0, kc == 3, [WPb.k(sl[2])] + ybk, [self.pk(bPB)])
                    r = (it % 2) * 2
                    it += 1
                    self.act(SA[r], self.ps[bGA][:, 0:512], AF.Sigmoid, [self.pk(bGA)], [SCb.k("sa", r)])
                    self.act(SA[r + 1], self.ps[bGB][:, 0:512], AF.Sigmoid, [self.pk(bGB)], [SCb.k("sa", r + 1)])
                    self.tt("dve", MM[r], SA[r], self.ps[bPA][:, 0:512], ALU.mult, [SCb.k("sa", r), self.pk(bPA)], [SCb.k("mm", r)])
                    self.tt("dve", MM[r + 1], SA[r + 1], self.ps[bPB][:, 0:512], ALU.mult, [SCb.k("sa", r + 1), self.pk(bPB)], [SCb.k("mm", r + 1)])
                    self.tt("pool", MT[:, oc, t0:t0 + 512], MM[r], MM[r + 1], ALU.add, [SCb.k("mm", r), SCb.k("mm", r + 1)],
                            [MTb.k(b) for b in range(t0 // 128, t0 // 128 + 4)])
        for b_ in (SCb, HTb, YAb, YBb):
            s.retire(b_)
        g["MT"], g["MTb"] = MT, MTb
        if self.taps is not None:
            nm = g["name"]
            for oc in (0, 5):
                self.tap("mt%d_%s" % (oc, nm), MT[:, oc, 0:Tg], [128, Tg], dtype=BF16, reads=[MTb.k(b) for b in range(Tg // 128)])

    def phase_m2(self, g):
        s, I = self.s, self.I
        Tg, v = g["Tg"], g["v"]
        NB = Tg // 128
        MT, MTb = g["MT"], g["MTb"]
        WOb = s.alloc("WO", 96 * KB, 16 * KB)
        WO = self.view(96 * KB, BF16, [128, 8, 1024])
        w_o_v = I["w_o"].rearrange("(k p) c -> p k c", p=128)
        for hf in range(2):
            self.dma("pool", WO[:, :, hf * 512:(hf + 1) * 512], w_o_v[:, :, hf * 512:(hf + 1) * 512], (), [WOb.k(hf)])
        X1b = s.alloc("X1", 0, 64 * KB)
        X1 = self.view(0, F32, [128, 16, D])
        Mb = s.alloc("M2S", 112 * KB, 20 * KB)
        XL = [self.view(112 * KB + i * 4 * KB, F32, [128, D]) for i in range(2)]
        G1row = self.build_grow(self.G1, v, 120 * KB, Mb, 124 * KB, Mb)
        T1 = [[self.view(126 * KB + (i * 2 + hf) * 2 * KB, F32, [128, 512]) for hf in range(2)] for i in range(2)]
        for blk in range(NB):
            r = blk % 2
            kx = Mb.k("xl", r)
            self.dma("sp", XL[r], g["x"][blk * 128:(blk + 1) * 128, :], (), [kx])
            banks = [self.next_bank(), self.next_bank()]
            for hf in range(2):
                for kc in range(8):
                    self.mm(self.ps[banks[hf]][:, 0:512], MT[:, kc, blk * 128:(blk + 1) * 128], WO[:, kc, hf * 512:(hf + 1) * 512], kc == 0, kc == 7,
                            [MTb.k(blk), WOb.k(hf)], [self.pk(banks[hf])])
            self.norm_residual(banks, XL[r], [kx], G1row, Mb.k("grow"), X1[:, blk, :], [X1b.k(blk)], T1[r], [Mb.k("t1", r, 0), Mb.k("t1", r, 1)])
        for b_ in (Mb, MTb, WOb):
            s.retire(b_)
        g["X1"], g["X1b"] = X1, X1b
        if self.taps is not None:
            nm = g["name"]
            self.tap("x1_" + nm, X1[:, 0:NB, :].rearrange("p a b -> p (a b)"), [128, NB * D], reads=[X1b.k(b) for b in range(NB)])

    def phase_ffn(self, g):
        s, I = self.s, self.I
        Tg, v = g["Tg"], g["v"]
        X1, X1b = g["X1"], g["X1b"]
        WPb, WPv = self.wp_views()
        WDb = s.alloc("WD", 64 * KB, 44 * KB)
        WD = self.view(64 * KB, BF16, [128, 22, 1024])
        w_d_v = I["w_down"].rearrange("(k p) c -> p k c", p=128)
        for j0 in range(0, 22, 4):
            j1 = min(j0 + 4, 22)
            self.dma("pool", WD[:, j0:j1, :], w_d_v[:, j0:j1, :], (), [WDb.k(j) for j in range(j0, j1)])
        ATb = s.alloc("ACTT", 108 * KB, 22 * KB)
        ACTT = self.view(108 * KB, BF16, [128, 22, 512])
        H2b = s.alloc("H2T", 130 * KB, 8 * KB)
        H2T = self.view(130 * KB, BF16, [128, 8, 512])
        Fb = s.alloc("FSCR", 138 * KB, 14 * KB)
        XN = self.view(138 * KB, BF16, [128, D])
        YO = [self.view(140 * KB + i * 4 * KB, F32, [128, D]) for i in range(2)]
        Gb = Fb
        G2row = self.build_grow(self.G2, v, 148 * KB, Fb, 138 * KB, Fb, dgkey=Fb.k("xn"))
        SG = [YO[0][:, 0:512], YO[0][:, 512:1024]]
        T1 = SG
        w_up_v = I["w_up"].rearrange("(k p) c -> p k c", p=128)
        A = self.A2[:, :, v]
        B = self.MODT[:, 24:32, v]
        it = 0
        NU = Tg // 512

        def h2t_block(un, bi):
            blk = un * 4 + bi
            self.norm_transpose_block(X1[:, blk, :], [X1b.k(blk)], XN, Fb.k("xn"), H2T[:, :, bi * 128:(bi + 1) * 128], H2b.k(bi), A, B)

        for bi in range(4):
            h2t_block(0, bi)
        def up_load(jp):
            return (self.wp_load([(0, 8, 256, w_up_v[:, :, 256 * jp:256 * jp + 256])]),
                    self.wp_load([(0, 8, 256, w_up_v[:, :, DFF + 256 * jp:DFF + 256 * jp + 256])]))

        q = [up_load(0), up_load(1)]
        for un in range(NU):
            hk = [H2b.k(bi) for bi in range(4)]
            for jp in range(11):
                sg_, su_ = q.pop(0)
                if jp + 2 < 11:
                    q.append(up_load(jp + 2))
                for jj in range(2):
                    j = jp * 2 + jj
                    cs = slice(jj * 128, (jj + 1) * 128)
                    bG, bU = self.next_bank(), self.next_bank()
                    for kc in range(8):
                        self.mm(self.ps[bG][:, 0:512], WPv[sg_][:, kc, cs], H2T[:, kc, :], kc == 0, kc == 7, [WPb.k(sg_)] + hk, [self.pk(bG)])
                    for kc in range(8):
                        self.mm(self.ps[bU][:, 0:512], WPv[su_][:, kc, cs], H2T[:, kc, :], kc == 0, kc == 7, [WPb.k(su_)] + hk, [self.pk(bU)])
                    r = it % 2
                    it += 1
                    self.act(SG[r], self.ps[bG][:, 0:512], AF.Silu, [self.pk(bG)], [Gb.k("sg", r)])
                    self.tt("dve", ACTT[:, j, :], SG[r], self.ps[bU][:, 0:512], ALU.mult, [Gb.k("sg", r), self.pk(bU)], [ATb.k(j)])
            if un + 1 < NU:
                q = [up_load(0), up_load(1)]
            for bi in range(4):
                blk = un * 4 + bi
                if un + 1 < NU:
                    nblk_ = (un + 1) * 4 + bi
                    self.norm_stage_a(X1[:, nblk_, :], [X1b.k(nblk_)], XN, Fb.k("xn"))
                banks = [self.next_bank(), self.next_bank()]
                for hf in range(2):
                    for j in range(22):
                        self.mm(self.ps[banks[hf]][:, 0:512], ACTT[:, j, bi * 128:(bi + 1) * 128], WD[:, j, hf * 512:(hf + 1) * 512], j == 0, j == 21,
                                [ATb.k(j), WDb.k(j)], [self.pk(banks[hf])])
                if un + 1 < NU:
                    self.norm_stage_b(XN, Fb.k("xn"), H2T[:, :, bi * 128:(bi + 1) * 128], H2b.k(bi), A, B)
                yo = YO[1]
                kyo = Fb.k("yo", 1)
                self.norm_residual(banks, X1[:, blk, :], [X1b.k(blk)], G2row, Fb.k("grow"), yo, [kyo], T1, [Gb.k("sg", 0), Gb.k("sg", 1)])
                op = self.dma("sp", g["y"][blk * 128:(blk + 1) * 128, :], yo, [kyo], ())
                self.out_dmas.append(op)
        for b_ in (WDb, ATb, H2b, Fb, X1b):
            s.retire(b_)


_PROG_CACHE = {}


def _get_prog(stop_after=None, taps=False):
    key = (stop_after, taps)
    if key not in _PROG_CACHE:
        p = Prog(stop_after=stop_after, taps=[] if taps else None)
        p.build()
        _PROG_CACHE[key] = p
    return _PROG_CACHE[key]


def _in_maps(inputs):
    f = lambda a: np.ascontiguousarray(np.asarray(a, dtype=np.float32))
    common = {
        "w_mod": f(inputs["w_mod"][0]), "b_mod": f(inputs["b_mod"][0]),
        "g_pre_mix": f(inputs["g_pre_mix"][0]), "g_post_mix": f(inputs["g_post_mix"][0]),
        "g_pre_ffn": f(inputs["g_pre_ffn"][0]), "g_post_ffn": f(inputs["g_post_ffn"][0]),
        "w_in": f(inputs["w_in"][0]), "conv_w": f(inputs["conv_w"][0]),
        "a_log": f(inputs["a_log"][0]).reshape(8), "dt_bias": f(inputs["dt_bias"][0]).reshape(8),
        "g_onorm": f(inputs["g_onorm"][0]), "w_a": f(inputs["w_a_proj"][0]), "w_pool": f(inputs["w_pool"][0]),
        "pool_scale": f(inputs["pool_scale"][0]), "w_b": f(inputs["w_b_proj"][0]), "w_o": f(inputs["w_o"][0]),
        "w_up": f(inputs["w_up"][0]), "w_down": f(inputs["w_down"][0]),
        "constf": CF_ARR, "constb": CB_ARR,
    }
    xp = f(inputs["x_prompt"])
    xs = f(inputs["x_sample"])
    sd = f(inputs["state_delta"])
    c = f(inputs["c"])
    cc = f(inputs["c_ctx"])
    maps = []
    for i in range(8):
        m = dict(common)
        m["xp"] = np.ascontiguousarray(xp[2 * i:2 * i + 2].reshape(512, D))
        m["xs"] = np.ascontiguousarray(xs[i])
        m["s0"] = np.ascontiguousarray(sd[i, 0].reshape(8, 128, 128))
        m["cv"] = np.ascontiguousarray(np.stack([cc, c[i]], axis=0))
        maps.append(m)
    return maps


def kernel(**inputs):
    prog = _get_prog()
    res = run_bass_kernel_spmd(prog.nc, _in_maps(inputs), core_ids=list(range(8)))
    r = res.results
    y_prompt = np.concatenate([r[i]["yp"].reshape(2, 256, D) for i in range(8)], axis=0).astype(np.float32)
    y_sample = np.stack([r[i]["ys"] for i in range(8)], axis=0).astype(np.float32)
    ns = np.concatenate([r[i]["ns"].reshape(2, 1, 2, 4, 128, 128) for i in range(8)], axis=0).astype(np.float32)
    return (y_prompt, y_sample, ns)
```

```python
import numpy as np
from contextlib import ExitStack
import concourse.bass as bass
import concourse.mybir as mybir
from concourse.bass_utils import run_bass_kernel_spmd

F32 = mybir.dt.float32
BF16 = mybir.dt.bfloat16
F32R = mybir.dt.float32r
AF = mybir.ActivationFunctionType
ALU = mybir.AluOpType

D = 1024
NH = 4
DA = 512
DFF = 2816
DIN = 4624
EPS = 1e-6
BIG = 30000.0
NQ = 8

ANNOTATE = False
INV_BF16_FROM = 5
NWP = 6
INV_DT = F32
SAME_ENGINE_RAW = True
POOL_ENG = "dve"
DEBUG_TAPS = None


class Op:
    __slots__ = ("eng", "fn", "deps", "signal", "sigval", "is_dma", "dsem", "dval", "idx", "tag")

    def __init__(self, eng, fn, is_dma, idx):
        self.eng = eng
        self.fn = fn
        self.deps = set()
        self.signal = False
        self.sigval = None
        self.is_dma = is_dma
        self.dsem = None
        self.dval = None
        self.idx = idx


class Buf:
    def __init__(self, name, off=0, size=0):
        self.name = name
        self.off = off
        self.size = size
        self.after = []
        self.keys = set()
        self.psum = off >= 10 ** 7

    def k(self, *sub):
        key = (self, sub)
        self.keys.add(key)
        return key


class Sched:
    def __init__(self, nc):
        self.nc = nc
        self.E = {"pe": nc.tensor, "act": nc.scalar, "dve": nc.vector, "pool": nc.gpsimd, "sp": nc.sync}
        self.ops = []
        self.last_w = {}
        self.readers = {}
        self.retired = []

    def _need(self, prod, eng, dma):
        return prod.is_dma or dma or prod.eng != eng

    def add(self, eng, fn, reads=(), writes=(), sreads=(), dma=False, after=()):
        pr = [k for k in reads if isinstance(k[0], Buf) and k[0].psum]
        if pr:
            reads = [k for k in reads if not (isinstance(k[0], Buf) and k[0].psum)]
            writes = list(writes) + pr
        op = Op(eng, fn, dma, len(self.ops))
        op.tag = None
        if ANNOTATE:
            import sys as _sys
            f = _sys._getframe(2)
            if f.f_code.co_name in ("act", "cp"):
                f = f.f_back
            op.tag = "%s:%d" % (f.f_code.co_name, f.f_lineno)
        deps = op.deps
        for k in reads:
            w = self.last_w.get(k)
            if w is not None and (self._need(w, eng, dma) or (SAME_ENGINE_RAW and eng != "pe")):
                deps.add(w)
        for k in sreads:
            w = self.last_w.get(k)
            if w is not None:
                deps.add(w)
        for k in writes:
            w = self.last_w.get(k)
            if w is not None and self._need(w, eng, dma):
                deps.add(w)
            rd = self.readers.get(k)
            if rd:
                for r in rd.values():
                    if self._need(r, eng, dma):
                        deps.add(r)
            b = k[0]
            if isinstance(b, Buf) and b.after:
                for r in b.after:
                    if self._need(r, eng, dma):
                        deps.add(r)
        for r in after:
            if self._need(r, eng, dma):
                deps.add(r)
        for d in deps:
            if not d.is_dma:
                d.signal = True
        rkey = ("dma", op.idx) if dma else eng
        for k in list(reads) + list(sreads):
            self.readers.setdefault(k, {})[rkey] = op
        for k in writes:
            self.last_w[k] = op
            self.readers[k] = {}
        self.ops.append(op)
        return op

    def collect(self, keys):
        best = {}
        out = []
        for k in keys:
            cands = []
            w = self.last_w.get(k)
            if w is not None:
                cands.append(w)
            rd = self.readers.get(k)
            if rd:
                cands.extend(rd.values())
            for o in cands:
                if o.is_dma:
                    out.append(o)
                else:
                    b = best.get(o.eng)
                    if b is None or o.idx > b.idx:
                        best[o.eng] = o
        return list(set(out)) + list(best.values())

    def alloc(self, name, off, size):
        b = Buf(name, off, size)
        for (o, s, tok) in self.retired:
            if o < off + size and off < o + s:
                b.after.extend(tok)
        return b

    def retire(self, buf):
        self.retired.append((buf.off, buf.size, self.collect(buf.keys)))

    def emit(self, stack):
        nc = self.nc
        sem = {e: stack.enter_context(nc.semaphore("s_" + e)) for e in self.E}
        dsem = {q: [stack.enter_context(nc.semaphore("d_%s%d" % (q, i))) for i in range(NQ)] for q in ("sp", "pool", "act")}
        dval = {q: [0] * NQ for q in dsem}
        dcount = {q: 0 for q in dsem}
        cnt = {e: 0 for e in self.E}
        waited = {}
        for op in self.ops:
            eo = self.E[op.eng]
            for d in sorted(op.deps, key=lambda o: o.idx):
                if d.is_dma:
                    s, v = d.dsem, d.dval
                else:
                    s, v = sem[d.eng], d.sigval
                wk = (op.eng, id(s))
                if waited.get(wk, 0) >= v:
                    continue
                eo.wait_ge(s, v)
                waited[wk] = v
            if op.is_dma:
                q = op.eng
                slot = dcount[q] % NQ
                dcount[q] += 1
                s = dsem[q][slot]
                prev = dval[q][slot]
                wk = (q, id(s))
                if prev > 0 and waited.get(wk, 0) < prev:
                    eo.wait_ge(s, prev)
                    waited[wk] = prev
                ins = op.fn()
                if op.tag:
                    ins.annotate(op.tag)
                ins.then_inc(s, 16)
                op.dsem = s
                op.dval = prev + 16
                dval[q][slot] = prev + 16
            else:
                if op.fn is None:
                    continue
                ins = op.fn()
                if op.tag:
                    ins.annotate(op.tag)
                if op.signal:
                    cnt[op.eng] += 1
                    ins.then_inc(sem[op.eng], 1)
                    op.sigval = cnt[op.eng]
        return cnt


POOL_W = (2, 4, 8, 16)


def _invcnt(n, w):
    t = np.arange(n)
    lo = np.clip(t - w // 2, 0, n)
    hi = np.clip(t + (w - w // 2), 0, n)
    return (1.0 / (hi - lo)).astype(np.float32)


def _build_consts():
    p = np.arange(128)[:, None]
    f = np.arange(128)[None, :]
    same = (p // 64) == (f // 64)
    cf = {}
    cf["IDENT"] = np.eye(128, dtype=np.float32)
    cf["TRI_F"] = (same & (p <= f)).astype(np.float32)
    cf["TRI_B"] = (same & (p >= f)).astype(np.float32)
    cf["STR_F"] = (same & (p > f)).astype(np.float32)
    cf["STR_B"] = (same & (p < f)).astype(np.float32)
    cf["IND0"] = np.broadcast_to((p < 64), (128, 128)).astype(np.float32)
    cf["IND1"] = np.broadcast_to((p >= 64), (128, 128)).astype(np.float32)
    cf["ONES"] = np.ones((128, 128), np.float32)
    cf["NEGONES"] = -np.ones((128, 128), np.float32)
    bm = np.zeros((128, 512), np.float32)
    for h in range(4):
        bm[h, h * 128:(h + 1) * 128] = 1.0
    cf["BM4"] = bm
    cf["NEG05"] = np.full((128, 1), -0.5, np.float32)
    for wi, w in enumerate(POOL_W):
        cf["IC64_%d" % wi] = np.broadcast_to(_invcnt(64, w)[None, :], (128, 64)).copy()
        cf["IC32_%d" % wi] = np.broadcast_to(_invcnt(32, w)[None, :], (128, 32)).copy()
        cf["IC256_%d" % wi] = np.broadcast_to(_invcnt(256, w)[None, :], (128, 256)).copy()
    cb = {}
    cb["IDENTB"] = np.eye(128, dtype=np.float32)
    cb["ONESB"] = np.ones((128, 128), np.float32)
    cb["NEG_SL"] = np.where(same & (p > f), 0.0, -BIG).astype(np.float32)
    cb["NEG_SU"] = np.where(same & (p < f), 0.0, -BIG).astype(np.float32)
    cb["NEG_IU"] = np.where(same & (p <= f), 0.0, -BIG).astype(np.float32)
    cb["NEG_IL"] = np.where(same & (p >= f), 0.0, -BIG).astype(np.float32)
    cb["C_F"] = (BIG * ((~(same & (p > f))).astype(np.float32) + (~(same & (p <= f))).astype(np.float32))).astype(np.float32)
    cb["C_B"] = (BIG * ((~(same & (p < f))).astype(np.float32) + (~(same & (p >= f))).astype(np.float32))).astype(np.float32)

    def pack(dct):
        offs = {}
        cols = []
        o = 0
        for k, v in dct.items():
            offs[k] = (o, v.shape[1])
            cols.append(v)
            o += v.shape[1]
        return offs, np.ascontiguousarray(np.concatenate(cols, axis=1))

    offf, arrf = pack(cf)
    offb, arrb = pack(cb)
    return offf, arrf, offb, arrb


CF_OFF, CF_ARR, CB_OFF, CB_ARR = _build_consts()


KB = 1024


class Prog:
    def __init__(self, stop_after=None, taps=None):
        self.stop_after = stop_after
        self.taps = taps
        self.nc = bass.Bass("TRN2", target_bir_lowering=False)
        self.s = Sched(self.nc)
        self.out_dmas = []

    def mm(self, out, lhsT, rhs, start, stop, reads, writes, after=(), skip=False):
        nc = self.nc
        if skip:
            return self.s.add("pe", lambda: nc.tensor.matmul(out, lhsT, rhs, start=start, stop=stop, skip_group_check=True), reads, writes, after=after)
        return self.s.add("pe", lambda: nc.tensor.matmul(out, lhsT, rhs, start=start, stop=stop), reads, writes, after=after)

    def tr(self, out, in_, ident, reads, writes):
        nc = self.nc
        return self.s.add("pe", lambda: nc.tensor.transpose(out, in_, ident), reads, writes)

    def act(self, out, in_, func, reads, writes, bias=None, scale=None, accum_out=None, sreads=()):
        nc = self.nc
        kw = {}
        if bias is not None:
            kw["bias"] = bias
        if scale is not None:
            kw["scale"] = scale
        if accum_out is not None:
            kw["accum_out"] = accum_out
        return self.s.add("act", lambda: nc.scalar.activation(out, in_, func, **kw), reads, writes, sreads)

    def _ve(self, eng):
        return self.nc.vector if eng == "dve" else self.nc.gpsimd

    def ts(self, eng, out, in0, s1, s2, op0, op1, reads, writes, sreads=()):
        e = self._ve(eng)
        if op1 is None:
            return self.s.add(eng, lambda: e.tensor_scalar(out, in0, s1, None, op0), reads, writes, sreads)
        return self.s.add(eng, lambda: e.tensor_scalar(out, in0, s1, s2, op0, op1), reads, writes, sreads)

    def tt(self, eng, out, in0, in1, op, reads, writes):
        e = self._ve(eng)
        return self.s.add(eng, lambda: e.tensor_tensor(out, in0, in1, op), reads, writes)

    def stt(self, eng, out, in0, scalar, in1, op0, op1, reads, writes, sreads=()):
        e = self._ve(eng)
        return self.s.add(eng, lambda: e.scalar_tensor_tensor(out, in0, scalar, in1, op0, op1), reads, writes, sreads)

    def cp(self, eng, out, in_, reads, writes):
        if eng == "act":
            return self.act(out, in_, AF.Copy, reads, writes)
        e = self._ve(eng)
        return self.s.add(eng, lambda: e.tensor_copy(out, in_), reads, writes)

    def memset(self, eng, out, val, writes):
        e = self._ve(eng)
        return self.s.add(eng, lambda: e.memset(out, val), (), writes)

    def dma(self, q, out, in_, reads, writes, noncontig=False):
        nc = self.nc
        eo = self.s.E[q]
        if noncontig:
            def fn():
                with nc.allow_non_contiguous_dma(reason="small strided"):
                    return eo.dma_start(out=out, in_=in_)
        else:
            def fn():
                return eo.dma_start(out=out, in_=in_)
        return self.s.add(q, fn, reads, writes, dma=True)

    def view(self, off_bytes, dtype, shape):
        nfree = 1
        for d in shape[1:]:
            nfree *= d
        esz = 4 if dtype in (F32, F32R) else 2
        nbytes = nfree * esz
        assert off_bytes % 4 == 0 and nbytes % 4 == 0, (off_bytes, nbytes)
        w0 = off_bytes // 4
        ap = self.arena[0:shape[0], w0:w0 + nbytes // 4]
        if dtype != F32:
            ap = ap.bitcast(dtype)
        if len(shape) == 2:
            return ap
        names = " ".join("d%d" % i for i in range(len(shape) - 1))
        kw = {"d%d" % i: shape[i + 1] for i in range(len(shape) - 1)}
        return ap.rearrange("p (%s) -> p %s" % (names, names), **kw)

    def tap(self, name, ap, shape, dtype=F32, reads=()):
        if self.taps is None or name in self.taps:
            return
        t = self.nc.dram_tensor("tap_" + name, list(shape), dtype, kind="ExternalOutput").ap()
        op = self.dma("sp", t, ap, reads, ())
        self.out_dmas.append(op)
        self.taps.append(name)

    def build(self):
        nc, s = self.nc, self.s
        stack = ExitStack()
        self.stack = stack

        def din(name, shape):
            return nc.dram_tensor(name, list(shape), F32, kind="ExternalInput").ap()

        def dout(name, shape):
            return nc.dram_tensor(name, list(shape), F32, kind="ExternalOutput").ap()

        I = {}
        I["xp"] = din("xp", [512, D])
        I["xs"] = din("xs", [2048, D])
        I["s0"] = din("s0", [8, 128, 128])
        I["cv"] = din("cv", [2, D])
        I["w_mod"] = din("w_mod", [D, 6 * D])
        I["b_mod"] = din("b_mod", [6 * D])
        for n in ("g_pre_mix", "g_post_mix", "g_pre_ffn", "g_post_ffn"):
            I[n] = din(n, [D])
        I["w_in"] = din("w_in", [D, DIN])
        I["conv_w"] = din("conv_w", [5, 1536])
        I["a_log"] = din("a_log", [8])
        I["dt_bias"] = din("dt_bias", [8])
        I["g_onorm"] = din("g_onorm", [128])
        I["w_a"] = din("w_a", [DA, D])
        I["w_pool"] = din("w_pool", [4, 128, 128])
        I["pool_scale"] = din("pool_scale", [512])
        I["w_b"] = din("w_b", [512, D])
        I["w_o"] = din("w_o", [D, D])
        I["w_up"] = din("w_up", [D, 2 * DFF])
        I["w_down"] = din("w_down", [DFF, D])
        I["constf"] = din("constf", list(CF_ARR.shape))
        I["constb"] = din("constb", list(CB_ARR.shape))
        O = {}
        O["yp"] = dout("yp", [512, D])
        O["ys"] = dout("ys", [2048, D])
        O["ns"] = dout("ns", [2, 8, 128, 128])
        self.I, self.O = I, O

        ARENA_KB = 195
        self.arena = stack.enter_context(nc.sbuf_tensor("arena", [128, ARENA_KB * 256], F32))
        self.FR = [stack.enter_context(nc.sbuf_tensor("fr%d" % i, [128, 1536], F32)) for i in range(2)]
        self.ps = [stack.enter_context(nc.psum_tensor("ps%d" % i, [128, 512], F32)) for i in range(8)]
        self.psb = [p[:, :].bitcast(BF16) for p in self.ps]
        self.PSB = [Buf("psbank%d" % i, 10 ** 7 + i, 1) for i in range(8)]
        self.bank_ctr = 0

        CB0 = 176 * KB
        o = CB0
        ncf = CF_ARR.shape[1]
        self.CFv = self.view(o, F32, [128, ncf]); o += ((ncf * 4 + 7) // 8) * 8
        ncb = CB_ARR.shape[1]
        self.CBv = self.view(o, BF16, [128, ncb]); o += ncb * 2
        self.VT = self.view(o, F32, [128, 12, 32]); o += 1536
        self.SCT = self.view(o, F32, [128, 8, 2]); o += 64
        self.MODT = self.view(o, F32, [128, 48, 2]); o += 384
        self.A1 = self.view(o, F32, [128, 8, 2]); o += 64
        self.A2 = self.view(o, F32, [128, 8, 2]); o += 64
        self.G1 = self.view(o, F32, [128, 8, 2]); o += 64
        self.G2 = self.view(o, F32, [128, 8, 2]); o += 64
        self.ALB = self.view(o, F32, [128, 8]); o += 32
        self.DTB = self.view(o, F32, [128, 8]); o += 32
        self.NEGA = self.view(o, F32, [128, 8]); o += 32
        self.WPOOL = self.view(o, BF16, [128, 4, 128]); o += 1024
        self.NWPOOL = self.view(o, BF16, [128, 4, 128]); o += 1024
        self.SS = self.view(o, F32, [128, 64]); o += 256
        assert o <= ARENA_KB * KB, o
        self.CONSTB = Buf("const", CB0, o - CB0)
        self.kC = self.CONSTB.k("c")
        self.ss_ctr = 0

        self.setup()
        groups = [
            dict(name="p", x=I["xp"], y=O["yp"], Tg=512, seqs=[(0, 256), (256, 256)], v=0, grid=False, s0=None, ns=[0, 1]),
            dict(name="s", x=I["xs"], y=O["ys"], Tg=2048, seqs=[(0, 2048)], v=1, grid=True, s0=I["s0"], ns=None),
        ]
        for g in groups:
            self.run_group(g)
            if self.stop_after is not None and self.stop_after[0] == g["name"]:
                break
        s.add("sp", None, after=self.out_dmas)
        s.emit(stack)
        stack.close()
        return nc

    def CF(self, name, rows=128):
        o, n = CF_OFF[name]
        return self.CFv[0:rows, o:o + n]

    def CBc(self, name, rows=128):
        o, n = CB_OFF[name]
        return self.CBv[0:rows, o:o + n]

    def next_bank(self, lo=0, hi=8):
        b = lo + (self.bank_ctr % (hi - lo))
        self.bank_ctr += 1
        return b

    def pk(self, b):
        return self.PSB[b].k("all")

    def ss_col(self):
        c = self.ss_ctr % 64
        self.ss_ctr += 1
        return c, self.CONSTB.k("ss", c)

    def setup(self):
        s, I = self.s, self.I
        kC = self.kC
        kCF, kCB = self.CONSTB.k("cf"), self.CONSTB.k("cb")
        self.dma("sp", self.CFv, I["constf"], (), [kCF])
        self.dma("pool", self.CBv, I["constb"], (), [kCB])
        nc_ = self.nc
        s.add("dve", lambda: nc_.vector.memset(self.SS[:, 63:64], 0.0), [kCF, kCB], [kC])
        self.dma("sp", self.ALB, I["a_log"].partition_broadcast(128), (), [self.CONSTB.k("alb")])
        self.dma("sp", self.DTB, I["dt_bias"].partition_broadcast(128), (), [self.CONSTB.k("dtb")])
        self.dma("pool", self.WPOOL, I["w_pool"].rearrange("g c d -> c g d"), (), [self.CONSTB.k("wpool")])
        self.ts("dve", self.NWPOOL, self.WPOOL, -1.0, None, ALU.mult, None, [self.CONSTB.k("wpool")], [self.CONSTB.k("nwpool")])
        SB = s.alloc("setup", 0, 64 * KB)
        kv = SB.k("vec")
        VEC = self.view(48 * KB, F32, [32, 1536])
        self.memset("dve", VEC, 0.0, [kv])
        rows = [
            (0, 2, 1024, I["cv"]),
            (2, 1, 1024, I["g_pre_mix"].rearrange("(o f) -> o f", o=1)),
            (3, 1, 1024, I["g_post_mix"].rearrange("(o f) -> o f", o=1)),
            (4, 1, 1024, I["g_pre_ffn"].rearrange("(o f) -> o f", o=1)),
            (5, 1, 1024, I["g_post_ffn"].rearrange("(o f) -> o f", o=1)),
            (6, 6, 1024, I["b_mod"].rearrange("(i f) -> i f", i=6)),
            (12, 5, 1536, I["conv_w"]),
            (17, 1, 512, I["pool_scale"].rearrange("(o f) -> o f", o=1)),
            (18, 1, 128, I["g_onorm"].rearrange("(o f) -> o f", o=1)),
        ]
        for (r0, nr, n, src) in rows:
            self.dma("sp", VEC[r0:r0 + nr, 0:n], src, (), [kv])
        b = self.next_bank()
        for c in range(12):
            self.tr(self.ps[b][:, c * 32:(c + 1) * 32], VEC[0:32, c * 128:(c + 1) * 128], self.CF("IDENT", 32)[:, 0:32], [kv, kC], [self.pk(b)])
        kVT = self.CONSTB.k("vt")
        self.kVT = kVT
        self.cp("dve", self.VT, self.ps[b][:, 0:384].rearrange("p (c r) -> p c r", c=12), [self.pk(b)], [kVT])
        kSCT = self.CONSTB.k("sct")
        self.act(self.SCT, self.VT[:, 0:8, 0:2], AF.Silu, [kVT], [kSCT])
        kNEGA = self.CONSTB.k("nega")
        self.act(self.NEGA, self.ALB, AF.Exp, [self.CONSTB.k("alb")], [kNEGA])
        self.ts("dve", self.NEGA, self.NEGA, -1.0, None, ALU.mult, None, [kNEGA], [kNEGA])
        SCB = self.view(54 * KB, BF16, [128, 8, 2])
        kSCB = SB.k("scb")
        self.cp("dve", SCB, self.SCT, [kSCT], [kSCB])
        MROW = self.view(55 * KB, F32, [2, 6 * D])
        kMR = SB.k("mrow")
        w_mod_v = I["w_mod"].rearrange("(k p) c -> p k c", p=128)
        WMP = [self.view(i * 8 * KB, BF16, [128, 8, 512]) for i in range(4)]
        for ct in range(12):
            kw = SB.k("wmp", ct % 4)
            self.dma("pool", WMP[ct % 4], w_mod_v[:, :, ct * 512:(ct + 1) * 512], (), [kw])
            b = self.next_bank()
            for kc in range(8):
                self.mm(self.ps[b][0:2, 0:512], SCB[:, kc, :], WMP[ct % 4][:, kc, :], kc == 0, kc == 7, [kw, kSCB], [self.pk(b)])
            self.cp("act" if ct % 2 else "dve", MROW[:, ct * 512:(ct + 1) * 512], self.ps[b][0:2, 0:512], [self.pk(b)], [kMR])
        bm = self.next_bank()
        for oc in range(48):
            self.tr(self.ps[bm][:, 2 * oc:2 * oc + 2], MROW[0:2, oc * 128:(oc + 1) * 128], self.CF("IDENT", 2)[:, 0:2], [kMR, kC], [self.pk(bm)])
        kMOD = self.CONSTB.k("modt")
        for i in range(6):
            self.tt("dve", self.MODT[:, i * 8:(i + 1) * 8, :], self.ps[bm][:, i * 16:(i + 1) * 16].rearrange("p (k v) -> p k v", v=2),
                    self.VT[:, 0:8, 6 + i:7 + i].to_broadcast([128, 8, 2]), ALU.add, [self.pk(bm), kVT], [kMOD])
        kD = self.CONSTB.k("derived")
        self.kD = kD

        def vcol(j):
            return self.VT[:, 0:8, j:j + 1].to_broadcast([128, 8, 2])

        self.stt("dve", self.A1, self.MODT[:, 8:16, :], 1.0, vcol(2), ALU.add, ALU.mult, [kMOD, kVT], [kD])
        self.stt("dve", self.A2, self.MODT[:, 32:40, :], 1.0, vcol(4), ALU.add, ALU.mult, [kMOD, kVT], [kD])
        self.tt("dve", self.G1, self.MODT[:, 16:24, :], vcol(3), ALU.mult, [kMOD, kVT], [kD])
        self.tt("dve", self.G2, self.MODT[:, 40:48, :], vcol(5), ALU.mult, [kMOD, kVT], [kD])
        s.retire(SB)
        self.tap("modt", self.MODT.rearrange("p a b -> p (a b)"), [128, 96], reads=[kMOD])
        self.tap("vt", self.VT.rearrange("p a b -> p (a b)"), [128, 384], reads=[kVT])

    def run_group(self, g):
        st = self.stop_after[1] if (self.stop_after is not None and self.stop_after[0] == g["name"]) else None
        self.phase_ht(g)
        if st == "ht":
            return
        self.phase_proj1(g)
        if st is not None and (st == "proj1" or st.startswith("p1_")):
            return
        self.phase_dn(g)
        if st == "dn":
            return
        self.phase_m1(g)
        if st == "m1":
            return
        self.phase_m2(g)
        if st == "m2":
            return
        self.phase_ffn(g)

    def rstd_ops(self, c, kss, n_inv, kout):
        SS = self.SS
        self.ts("pool", SS[:, c:c + 1], SS[:, c:c + 1], n_inv, EPS, ALU.mult, ALU.add, [kss], [kss])
        self.tt("pool", SS[:, c:c + 1], SS[:, c:c + 1], self.CF("NEG05"), ALU.pow, [kss, self.kC], [kout])

    def norm_stage_a(self, src, src_keys, XN, kXN):
        c, kss = self.ss_col()
        self.act(XN, src, AF.Square, src_keys, [kXN, kss], accum_out=self.SS[:, c:c + 1])
        self.rstd_ops(c, kss, 1.0 / D, kss)
        self.ts("dve", XN, src, self.SS[:, c:c + 1], None, ALU.mult, None, src_keys, [kXN], sreads=[kss])

    def norm_stage_b(self, XN, kXN, dst, kdst, A, B):
        b = self.next_bank()
        for k in range(8):
            self.tr(self.psb[b][:, k * 128:(k + 1) * 128], XN[:, k * 128:(k + 1) * 128], self.CBc("IDENTB"), [kXN, self.kC], [self.pk(b)])
        for k in range(8):
            if k % 2 == 0:
                self.act(dst[:, k, :], self.psb[b][:, k * 128:(k + 1) * 128], AF.Identity, [self.pk(b)], [kdst],
                         scale=A[:, k:k + 1], bias=B[:, k:k + 1], sreads=[self.kD])
            else:
                self.ts("dve", dst[:, k, :], self.psb[b][:, k * 128:(k + 1) * 128], A[:, k:k + 1], B[:, k:k + 1], ALU.mult, ALU.add,
                        [self.pk(b)], [kdst], sreads=[self.kD])

    def norm_transpose_block(self, src, src_keys, XN, kXN, dst, kdst, A, B, bufafter=()):
        self.norm_stage_a(src, src_keys, XN, kXN)
        self.norm_stage_b(XN, kXN, dst, kdst, A, B)

    def phase_ht(self, g):
        s = self.s
        Tg, v = g["Tg"], g["v"]
        NB = Tg // 128
        HTb = s.alloc("HT", 0, 32 * KB)
        HT = self.view(0, BF16, [128, 8, 2048])
        NS_ = 6
        LA_ = 5
        XLb = s.alloc("XL", 112 * KB, NS_ * 4 * KB)
        XNb = s.alloc("XN", 136 * KB, NS_ * 2 * KB)
        XL = [self.view(112 * KB + i * 4 * KB, F32, [128, D]) for i in range(NS_)]
        XN = [self.view(136 * KB + i * 2 * KB, BF16, [128, D]) for i in range(NS_)]
        A = self.A1[:, :, v]
        B = self.MODT[:, 0:8, v]

        def stage_a(blk):
            sl = blk % NS_
            kx = XLb.k(sl)
            self.dma("sp", XL[sl], g["x"][blk * 128:(blk + 1) * 128, :], (), [kx])
            self.norm_stage_a(XL[sl], [kx], XN[sl], XNb.k(sl))

        for blk in range(min(LA_, NB)):
            stage_a(blk)
        for blk in range(NB):
            if blk + LA_ < NB:
                stage_a(blk + LA_)
            sl = blk % NS_
            self.norm_stage_b(XN[sl], XNb.k(sl), HT[:, :, blk * 128:(blk + 1) * 128], HTb.k(blk), A, B)
        s.retire(XLb)
        s.retire(XNb)
        g["HTb"], g["HT"] = HTb, HT
        if self.taps is not None and g["name"] == "p":
            self.tap("ht_" + g["name"], HT[:, :, 0:Tg].rearrange("p a b -> p (a b)") if Tg == 2048 else HT[:, 0, 0:Tg], [128, 8 * Tg if Tg == 2048 else Tg],
                     dtype=BF16, reads=[HTb.k(b) for b in range(NB)])

    def wp_views(self):
        if not hasattr(self, "WPb"):
            self.WPb = Buf("WP", 152 * KB, 24 * KB)
            self.WPv = [self.view(152 * KB + i * 4 * KB, BF16, [128, 8, 256]) for i in range(NWP)]
            self.wp_ctr = 0
        return self.WPb, self.WPv

    def wp_load(self, parts):
        WPb, WPv = self.wp_views()
        sl = self.wp_ctr % NWP
        self.wp_ctr += 1
        for (kc0, nk, n, src) in parts:
            self.dma("pool", WPv[sl][:, kc0:kc0 + nk, 0:n], src, (), [WPb.k(sl)])
        return sl

    def seg_iter(self, g, t0, n):
        out = []
        for si, (so, T) in enumerate(g["seqs"]):
            a = max(t0, so)
            b = min(t0 + n, so + T)
            if a < b:
                out.append((si, a - so, a - t0, b - a))
        return out

    def phase_proj1(self, g):
        s, I = self.s, self.I
        Tg, v = g["Tg"], g["v"]
        NB = Tg // 128
        NT = Tg // 512
        HT, HTb = g["HT"], g["HTb"]
        seqs = g["seqs"]
        nseq, T = len(seqs), seqs[0][1]
        WPb, WPv = self.wp_views()
        w_in_v = I["w_in"].rearrange("(k p) c -> p k c", p=128)
        panels = [("q0", 0, 256), ("q1", 256, 256), ("k0", 512, 256), ("k1", 768, 256), ("v0", 1024, 256), ("v1", 1280, 256),
                  ("z0", 1536, 256), ("z1", 1792, 256), ("ab", 2048, 16), ("u0", 2064, 256), ("u1", 2320, 256)]
        issued = [0]
        slot_of = {}

        def issue_upto(i):
            while issued[0] <= min(i, len(panels) - 1):
                j = issued[0]
                _, c0, n = panels[j]
                slot_of[j] = self.wp_load([(0, 8, n, w_in_v[:, :, c0:c0 + n])])
                issued[0] += 1

        QKVb = s.alloc("QKV", 64 * KB, 48 * KB)
        QKV = self.view(64 * KB, BF16, [128, 12, 2048])
        ZSb = s.alloc("ZS", 32 * KB, 16 * KB)
        ZS = self.view(32 * KB, BF16, [128, 4, 2048])
        YBb = s.alloc("YB", 48 * KB, 16 * KB)
        YB = self.view(48 * KB, BF16, [128, 4, 2048])
        GTS = self.view(148 * KB, F32, [128, 16, 6, 8])
        EGL = self.view(151 * KB, F32, [128, 16, 16])
        g.update(QKVb=QKVb, QKV=QKV, ZSb=ZSb, ZS=ZS, YBb=YBb, YB=YB, GTS=GTS, EGL=EGL)
        RAWb = s.alloc("RAW", 124 * KB, 10 * KB)
        RAW = [self.view(124 * KB + i * 4608, BF16, [128, nseq, T + 4]) for i in range(2)]
        CDb = s.alloc("CONVD", 148 * KB, 2560)
        CD = [self.view(148 * KB + i * 1280, BF16, [128, 5, 128]) for i in range(2)]
        kC, kVT = self.kC, self.kVT
        for i in range(2):
            self.memset("pool", RAW[i], 0.0, [RAWb.k(i)])
        ht_keys = lambda t0, n: [HTb.k(b) for b in range(t0 // 128, (t0 + n) // 128)]

        def proj_tile(piece0, oc, t0, bank, m=128):
            sl = slot_of[piece0 + oc // 2]
            for kc in range(8):
                self.mm(self.ps[bank][0:m, 0:512], WPv[sl][:, kc, (oc % 2) * 128:(oc % 2) * 128 + m], HT[:, kc, t0:t0 + 512], kc == 0, kc == 7,
                        [WPb.k(sl)] + ht_keys(t0, 512), [self.pk(bank)])

        LOOK = NWP - 2
        issue_upto(LOOK)
        def qkv_proj(ch):
            pi, oc = ch // 4, ch % 4
            issue_upto(2 * pi + oc // 2 + LOOK)
            rs = ch % 2
            for j in range(5):
                self.ts("pool" if j % 2 else "dve", CD[rs][:, j, :], self.CBc("IDENTB"), self.VT[:, ch, 12 + j:13 + j], None, ALU.mult, None,
                        [kC], [CDb.k(rs)], sreads=[kVT])
            for tt in range(NT):
                t0 = tt * 512
                bank = self.next_bank()
                proj_tile(2 * pi, oc, t0, bank)
                for (si, ts_, ot, n) in self.seg_iter(g, t0, 512):
                    self.cp("dve", RAW[rs][:, si, 2 + ts_:2 + ts_ + n], self.ps[bank][:, ot:ot + n],
                            [self.pk(bank)], [RAWb.k(rs)])

        def qkv_conv(ch):
            rs = ch % 2
            for tt in range(NT):
                t0 = tt * 512
                bank = self.next_bank()
                for (si, ts_, ot, n) in self.seg_iter(g, t0, 512):
                    for j in range(5):
                        self.mm(self.ps[bank][:, ot:ot + n], CD[rs][:, j, :], RAW[rs][:, si, ts_ + j:ts_ + j + n], j == 0, j == 4,
                                [CDb.k(rs), RAWb.k(rs)], [self.pk(bank)])
                self.act(QKV[:, ch, t0:t0 + 512], self.ps[bank][:, 0:512], AF.Silu, [self.pk(bank)],
                         [QKVb.k(ch, b) for b in range(t0 // 128, t0 // 128 + 4)])

        SQb = s.alloc("SQ", 112 * KB, 2 * KB)
        SQ = [self.view(112 * KB + i * KB, BF16, [128, 512]) for i in range(2)]
        RIb = s.alloc("RINV", 144 * KB, 4 * KB)
        RI = [self.view(144 * KB + i * 2 * KB, F32, [128, 512]) for i in range(2)]
        import math
        l2it = [0]

        def l2norm_chunk(ch):
            lnsc = math.log(128 ** -0.5) if ch < 4 else 0.0
            for tt in range(NT):
                t0 = tt * 512
                r = l2it[0] % 2
                l2it[0] += 1
                qk = [QKVb.k(ch, b) for b in range(t0 // 128, t0 // 128 + 4)]
                qv = QKV[:, ch, t0:t0 + 512]
                self.tt("dve", SQ[r], qv, qv, ALU.mult, qk, [SQb.k(r)])
                bank = self.next_bank()
                self.mm(self.ps[bank][:, 0:512], self.CBc("ONESB"), SQ[r], True, True, [SQb.k(r), kC], [self.pk(bank)])
                self.act(RI[r], self.ps[bank][:, 0:512], AF.Ln, [self.pk(bank)], [RIb.k(r)], bias=EPS)
                self.act(RI[r], RI[r], AF.Exp, [RIb.k(r)], [RIb.k(r)], scale=-0.5, bias=lnsc)
                self.tt("dve", qv, qv, RI[r], ALU.mult, qk + [RIb.k(r)], qk)

        qkv_proj(0)
        for ch in range(12):
            if ch + 1 < 12:
                qkv_proj(ch + 1)
            qkv_conv(ch)
            if 1 <= ch <= 8:
                l2norm_chunk(ch - 1)
        s.retire(SQb)
        s.retire(RIb)
        s.retire(RAWb)
        s.retire(CDb)
        sub = self.stop_after[1] if (self.stop_after is not None and self.stop_after[0] == g["name"]) else None
        if sub == "p1_qkv":
            return
        for oc in range(4):
            issue_upto(6 + oc // 2 + LOOK)
            for tt in range(NT):
                t0 = tt * 512
                bank = self.next_bank()
                proj_tile(6, oc, t0, bank)
                self.act(ZS[:, oc, t0:t0 + 512], self.ps[bank][:, 0:512], AF.Silu, [self.pk(bank)],
                         [ZSb.k(b) for b in range(t0 // 128, t0 // 128 + 4)])
        if sub == "p1_z":
            return
        issue_upto(10)
        sl = slot_of[8]
        GTSb = s.alloc("GTS", 148 * KB, 4 * KB)
        g["GTSb"] = GTSb
        GTb = s.alloc("GTMP", 114 * KB, 4 * KB)
        GT = self.view(114 * KB, F32, [128, 16, 48])[:, 0:NB, :]
        kalb, kdtb, knega = self.CONSTB.k("alb"), self.CONSTB.k("dtb"), self.CONSTB.k("nega")
        kg = GTb.k("a")
        kp6, kp7 = self.pk(6), self.pk(7)
        allg = [GTSb.k(b) for b in range(NB)]
        alle = [GTSb.k("egl", b) for b in range(NB)]
        for blk in range(NB):
            for kc in range(8):
                self.mm(self.ps[6][:, blk * 16:(blk + 1) * 16], HT[:, kc, blk * 128:(blk + 1) * 128], WPv[sl][:, kc, 0:16], kc == 0, kc == 7,
                        [WPb.k(sl), HTb.k(blk)], [kp6])
        pv = self.ps[6][:, 0:NB * 16].rearrange("p (b c) -> p b c", c=16)
        psA, psB = pv[:, :, 0:8], pv[:, :, 8:16]
        bc = lambda t_: t_.unsqueeze(1).to_broadcast([128, NB, 8])
        GS = lambda slot: GTS[:, 0:NB, slot, :]
        self.tt("dve", GT[:, :, 0:8], psA, bc(self.DTB), ALU.add, [kp6, kdtb], [kg])
        self.act(GT[:, :, 24:32], psB, AF.Exp, [kp6], [kg], scale=-1.0)
        self.act(GT[:, :, 8:16], GT[:, :, 0:8], AF.Exp, [kg], [kg])
        self.act(GT[:, :, 16:24], GT[:, :, 8:16], AF.Ln, [kg], [kg], bias=1.0)
        self.tt("dve", GS(5), GT[:, :, 16:24], bc(self.NEGA), ALU.mult, [kg, knega], allg)
        self.act(GT[:, :, 32:40], GT[:, :, 24:32], AF.Ln, [kg], [kg], bias=1.0)
        self.act(GS(0), GT[:, :, 32:40], AF.Exp, [kg], allg, scale=-1.0)
        for blk in range(NB):
            gtok = GTS[:, blk, 5, :]
            c0 = blk * 32
            for (cc, nm, lo) in ((0, "TRI_F", 0), (4, "TRI_B", 4), (8, "STR_F", 0), (12, "STR_B", 4)):
                self.mm(self.ps[7][:, c0 + cc:c0 + cc + 4], self.CF(nm), gtok[:, lo:lo + 4], True, True, [GTSb.k(blk), kC], [kp7])
            self.mm(self.ps[7][:, c0 + 16:c0 + 24], self.CF("IND0"), gtok, True, True, [GTSb.k(blk), kC], [kp7])
            self.mm(self.ps[7][:, c0 + 24:c0 + 32], self.CF("IND1"), gtok, True, True, [GTSb.k(blk), kC], [kp7])
        cv = self.ps[7][:, 0:NB * 32].rearrange("p (b c) -> p b c", c=32)
        self.tt("dve", GS(1), cv[:, :, 0:8], GT[:, :, 32:40], ALU.subtract, [kp7, kg], allg)
        self.ts("dve", GS(2), cv[:, :, 0:8], -1.0, None, ALU.mult, None, [kp7], allg)
        self.act(GT[:, :, 40:48], cv[:, :, 0:8], AF.Exp, [kp7], [kg])
        self.act(GS(4), cv[:, :, 8:16], AF.Exp, [kp7], allg)
        self.act(EGL[:, 0:NB, :], cv[:, :, 16:32], AF.Exp, [kp7], alle)
        self.tt("dve", GS(3), GT[:, :, 40:48], GS(0), ALU.mult, [kg] + allg, allg)
        s.retire(GTb)
        if sub == "p1_ab":
            return
        PE_ = POOL_ENG
        PAb = s.alloc("PA", 112 * KB, 12 * KB)
        PBb = s.alloc("PB", 124 * KB, 12 * KB)
        UBb = s.alloc("UB", 136 * KB, 4 * KB)
        MBb = s.alloc("MB", 140 * KB, 4 * KB)
        PAf = self.view(112 * KB, F32, [128, 3072])
        PBf = self.view(124 * KB, F32, [128, 3072])
        UB = self.view(136 * KB, BF16, [128, 2048])
        MB = self.view(140 * KB, BF16, [128, 2048])
        if g["grid"]:
            nrows, rowlen = 32, 64
            icn = "IC64_%d"
        else:
            nrows, rowlen = nseq, T
            icn = "IC256_%d"
        kPA, kPB = PAb.k("a"), PBb.k("a")
        for gi in range(4):
            w = POOL_W[gi]
            L1 = nrows * (rowlen + w)
            pa3 = PAf[:, 0:L1].rearrange("p (r c) -> p r c", r=nrows)
            self.memset(PE_, PAf[:, 0:L1], 0.0, [kPA])
            for tt in range(NT):
                t0 = tt * 512
                bank = self.next_bank()
                proj_tile(9, gi, t0, bank)
                if sub == "p1_ua":
                    return
                r0 = t0 // rowlen
                nr = 512 // rowlen
                self.cp("act", pa3[:, r0:r0 + nr, w // 2:w // 2 + rowlen], self.ps[bank][:, 0:512].rearrange("p (r c) -> p r c", r=nr),
                        [self.pk(bank)], [kPA])
                if sub == "p1_ub":
                    return
                self.cp("dve", UB[:, t0:t0 + 512], self.ps[bank][:, 0:512], [self.pk(bank)], [UBb.k(tt)])
                if sub == "p1_uc":
                    return
            cur, ck, oth, ok = PAf, kPA, PBf, kPB
            kk = 1
            while kk < w:
                no = L1 - (2 * kk - 1)
                self.tt(PE_, oth[:, 0:no], cur[:, 0:no], cur[:, kk:kk + no], ALU.add, [ck], [ok])
                cur, ck, oth, ok = oth, ok, cur, ck
                kk *= 2
            if sub == "p1_ud":
                return
            cur3 = cur[:, 0:L1].rearrange("p (r c) -> p r c", r=nrows)
            ic1 = self.CF(icn % gi)
            if not g["grid"]:
                self.tt(PE_, MB[:, 0:Tg].rearrange("p (r c) -> p r c", r=nrows), cur3[:, :, 0:rowlen],
                        ic1.unsqueeze(1).to_broadcast([128, nrows, rowlen]), ALU.mult, [ck, kC], [MBb.k("a")])
            else:
                L2 = (nrows + w) * rowlen
                self.memset(PE_, oth[:, 0:L2], 0.0, [ok])
                o3 = oth[:, 0:L2].rearrange("p (r c) -> p r c", c=rowlen)
                self.tt(PE_, o3[:, w // 2:w // 2 + nrows, :], cur3[:, :, 0:rowlen],
                        ic1.unsqueeze(1).to_broadcast([128, nrows, rowlen]), ALU.mult, [ck, kC], [ok])
                cur, ck, oth, ok = oth, ok, cur, ck
                kk = 1
                while kk < w:
                    sh = kk * rowlen
                    no = L2 - (2 * kk - 1) * rowlen
                    self.tt(PE_, oth[:, 0:no], cur[:, 0:no], cur[:, sh:sh + no], ALU.add, [ck], [ok])
                    cur, ck, oth, ok = oth, ok, cur, ck
                    kk *= 2
                c3 = cur[:, 0:L2].rearrange("p (r c) -> p r c", c=rowlen)
                ic2 = self.CF("IC32_%d" % gi)
                self.tt(PE_, MB[:, 0:Tg].rearrange("p (r c) -> p r c", c=rowlen), c3[:, 0:nrows, :],
                        ic2.unsqueeze(2).to_broadcast([128, nrows, rowlen]), ALU.mult, [ck, kC], [MBb.k("a")])
            if sub == "p1_u1":
                return
            for tt in range(NT):
                t0 = tt * 512
                bank = self.next_bank()
                self.mm(self.ps[bank][:, 0:512], self.WPOOL[:, gi, :], MB[:, t0:t0 + 512], True, False,
                        [MBb.k("a"), self.CONSTB.k("wpool")], [self.pk(bank)])
                self.mm(self.ps[bank][:, 0:512], self.NWPOOL[:, gi, :], UB[:, t0:t0 + 512], False, True,
                        [UBb.k(tt), self.CONSTB.k("nwpool")], [self.pk(bank)])
                self.ts("dve", YB[:, gi, t0:t0 + 512], self.ps[bank][:, 0:512], self.VT[:, gi, 17:18], None, ALU.mult, None,
                        [self.pk(bank)], [YBb.k(b) for b in range(t0 // 128, t0 // 128 + 4)], sreads=[kVT])
        for b_ in (PAb, PBb, UBb, MBb):
            s.retire(b_)
        s.retire(HTb)
        if self.taps is not None:
            nm = g["name"]
            allq = [QKVb.k(ch, b) for ch in range(12) for b in range(NB)]
            for ch in (0, 5, 9):
                self.tap("qkv%d_%s" % (ch, nm), QKV[:, ch, 0:Tg], [128, Tg], dtype=BF16, reads=allq)
            self.tap("zs1_" + nm, ZS[:, 1, 0:Tg], [128, Tg], dtype=BF16, reads=[ZSb.k(b) for b in range(NB)])
            for gi in (0, 3):
                self.tap("yb%d_%s" % (gi, nm), YB[:, gi, 0:Tg], [128, Tg], dtype=BF16, reads=[YBb.k(b) for b in range(NB)])
            self.tap("gts_" + nm, GTS[:, 0:NB, :, :].rearrange("p a b c -> p (a b c)"), [128, NB * 48], reads=[GTSb.k(b) for b in range(NB)])
            self.tap("egl_" + nm, EGL[:, 0:NB, :].rearrange("p a b -> p (a b)"), [128, NB * 16], reads=[GTSb.k("egl", b) for b in range(NB)])

    def phase_dn(self, g):
        s, I = self.s, self.I
        Tg = g["Tg"]
        QKV, QKVb, ZS, ZSb, GTS, GTSb, EGL = g["QKV"], g["QKVb"], g["ZS"], g["ZSb"], g["GTS"], g["GTSb"], g["EGL"]
        kC, kVT = self.kC, self.kVT
        OTb = s.alloc("OTOK", 0, 32 * KB)
        OT = self.view(0, F32, [128, 16, 512])
        WPb, _ = self.wp_views()
        DNb = s.alloc("DN", 112 * KB, 36 * KB)
        DN2b = s.alloc("DN2", 152 * KB, 19 * KB)
        DN2b.after.extend(s.collect(WPb.keys))
        dbase = [112 * KB, 152 * KB]
        xbase = [131 * KB, 140 * KB]
        dbuf = [DNb, DN2b]

        def dv(d, off, dtype, shape=(128, 4, 128)):
            return self.view(dbase[d] + off, dtype, list(shape))

        def frv(d, w0, n, dtype, pat, **kw):
            ap = self.FR[d][:, w0:w0 + n]
            if dtype != F32:
                ap = ap.bitcast(dtype)
            return ap.rearrange(pat, **kw)

        T_ = {}
        for d in range(2):
            t = dict(
                vb32=dv(d, 0, F32), E1=dv(d, 0, BF16), E3=dv(d, 1024, BF16),
                L=frv(d, 0, 512, F32, "p (h c) -> p h c", h=4), Lr=frv(d, 0, 512, F32R, "p (h c) -> p h c", h=4),
                RP=frv(d, 512, 1024, F32, "p (h a c) -> p h a c", h=4, a=2), RPr=frv(d, 512, 1024, F32R, "p (h a c) -> p h a c", h=4, a=2),
                EG=self.view(xbase[d] + 2048, BF16, [128, 4, 128]), BS=self.view(xbase[d], F32, [128, 4, 128]),
                kbg32=dv(d, 2048, F32), vnew=dv(d, 4096, BF16),
            )
            for par in range(2):
                o = 5120 + par * 7168
                t["wT32", par] = dv(d, o, F32)
                t["u", par] = dv(d, o + 2048, F32)
                t["IT", par] = dv(d, o + 4096, BF16)
                t["qdT", par] = dv(d, o + 5120, BF16)
                t["kdec", par] = dv(d, o + 6144, BF16)
            T_[d] = t
        S = self.view(134 * KB, F32, [128, 8, 128])
        Sb = self.view(138 * KB, BF16, [128, 8, 128])
        IDB = self.CBc("IDENTB")
        IDF = self.CF("IDENT")
        ID4 = self.CBc("IDENTB").unsqueeze(1).to_broadcast([128, 4, 128])
        BM4 = self.CF("BM4", 4).rearrange("p (h c) -> p h c", h=4)
        visited = set()

        def K(d, name, par=None):
            if name in ("BS", "EG"):
                return None
            return dbuf[d].k(name, par)

        def KS(d, name):
            return [dbuf[d].k(name, None)]

        def bcs(gb, slot, hd0):
            return GTS[:, gb, slot, hd0:hd0 + 4].unsqueeze(2).to_broadcast([128, 4, 128])

        def p4(bank):
            return self.ps[bank][:, 0:512].rearrange("p (h c) -> p h c", h=4)

        def pb4(bank, half=0):
            return self.psb[bank][:, half * 512:half * 512 + 512].rearrange("p (h c) -> p h c", h=4)

        def pre(d, gb, par):
            t = T_[d]
            hd0 = 4 * d
            X1, X2, X3 = 4 * d + 1, 4 * d + 2, 4 * d + 3
            t0 = gb * 128
            kgs = GTSb.k(gb)
            qk = lambda ch: QKVb.k(ch, gb)
            for h in range(4):
                self.tr(self.psb[X1][:, h * 128:(h + 1) * 128], QKV[:, 4 + h, t0:t0 + 128], IDB, [qk(4 + h), kC], [self.pk(X1)])
            yield
            self.tt("dve", t["kbg32"], pb4(X1, 0), bcs(gb, 3, hd0), ALU.mult, [self.pk(X1), kgs], [K(d, "kbg")])
            self.tt("dve", t["kdec", par], pb4(X1, 0), bcs(gb, 4, hd0), ALU.mult, [self.pk(X1), kgs], [K(d, "kdec", par)])
            yield
            for h in range(4):
                kT = QKV[:, 4 + h, t0:t0 + 128]
                self.mm(self.ps[X2][:, h * 128:(h + 1) * 128], kT, kT, True, True, [qk(4 + h)], [self.pk(X2)])
            for h in range(4):
                self.mm(self.ps[X3][:, h * 128:(h + 1) * 128], QKV[:, 4 + h, t0:t0 + 128], QKV[:, h, t0:t0 + 128], True, True,
                        [qk(4 + h), qk(h)], [self.pk(X3)])
            yield
            self.mm(self.ps[X1][0:4, 0:128], GTS[:, gb, 5, hd0:hd0 + 4], self.CF("TRI_F" if d == 0 else "TRI_B"), True, True,
                    [kgs, kC], [self.pk(X1)])
            self.tt("dve", t["BS"][0:4], self.ps[X1][0:4, 0:128].unsqueeze(1).to_broadcast([4, 4, 128]), BM4, ALU.mult,
                    [self.pk(X1), kC], KS(d, "BS"))
            bs2 = t["BS"][0:4].rearrange("p h c -> p (h c)")
            mc = self.CBc("C_F" if d == 0 else "C_B").unsqueeze(1).to_broadcast([128, 4, 128])
            m3 = self.CBc("NEG_IU" if d == 0 else "NEG_IL").unsqueeze(1).to_broadcast([128, 4, 128])
            yield
            self.mm(self.ps[X1][:, 0:512], self.CF("ONES", 4), bs2, True, False, KS(d, "BS") + [kC], [self.pk(X1)], skip=True)
            self.act(t["EG"], p4(X1), AF.Exp, [self.pk(X1)], KS(d, "EG"))
            yield
            self.mm(p4(X1), IDB, m3, False, False, [kC], [self.pk(X1)], skip=True)
            for h in range(4):
                self.act(t["E3"][:, h, :], self.ps[X1][:, h * 128:(h + 1) * 128], AF.Exp, [self.pk(X1)], [K(d, "E3")],
                         bias=GTS[:, gb, 2, hd0 + h:hd0 + h + 1], sreads=[kgs])
            yield
            self.mm(p4(X1), IDB, mc, False, True, [kC], [self.pk(X1)], skip=True)
            for h in range(4):
                self.act(t["E1"][:, h, :], self.ps[X1][:, h * 128:(h + 1) * 128], AF.Exp, [self.pk(X1)], [K(d, "E1")],
                         scale=-1.0, bias=GTS[:, gb, 1, hd0 + h:hd0 + h + 1], sreads=[kgs])
            yield
            self.tt("dve", t["Lr"], p4(X2), t["E1"], ALU.mult, [self.pk(X2), K(d, "E1")], [K(d, "L")])
            self.tt("dve", t["IT", par], p4(X3), t["E3"], ALU.mult, [self.pk(X3), K(d, "E3")], [K(d, "IT", par)])
            self.tt("pool", t["qdT", par], QKV[:, 0:4, t0:t0 + 128], t["EG"], ALU.mult, [qk(0), qk(1), qk(2), qk(3)] + KS(d, "EG"), [K(d, "qdT", par)])
            yield
            for h in range(4):
                self.tr(self.ps[X2][:, h * 128:(h + 1) * 128], t["L"][:, h, :], IDF, [K(d, "L"), kC], [self.pk(X2)])
            yield
            self.cp("act", t["RPr"][:, :, 0, :], p4(X2), [self.pk(X2)], [K(d, "R")])
            self.tt("dve", t["RPr"][:, :, 1, :], ID4, p4(X2), ALU.subtract, [self.pk(X2), kC], [K(d, "P")])
            yield
            Lr, RPr = t["Lr"], t["RPr"]
            XA = (X1, X2)
            for lev in range(6):
                for h in range(4):
                    bk = XA[h // 2]
                    c0 = (h % 2) * 256
                    self.mm(self.ps[bk][:, c0:c0 + 256], Lr[:, h, :], RPr[:, h, :, :].rearrange("p a c -> p (a c)"), True, True,
                            [K(d, "L"), K(d, "R"), K(d, "P")], [self.pk(bk)])
                if lev < 5:
                    for h in range(3):
                        self.mm(self.ps[X3][:, h * 128:h * 128 + 256], RPr[:, h, 0, :], Lr[:, h:h + 2, :].rearrange("p a c -> p (a c)"), True, True,
                                [K(d, "R"), K(d, "L")], [self.pk(X3)])
                    self.mm(self.ps[X3][:, 384:512], RPr[:, 3, 0, :], Lr[:, 3, :], True, True, [K(d, "R"), K(d, "L")], [self.pk(X3)])
                yield
                if lev < 5:
                    self.cp("act", Lr, p4(X3), [self.pk(X3)], [K(d, "L")])
                for i, bk in enumerate(XA):
                    pv = self.ps[bk][:, 0:512].rearrange("p (h a c) -> p h a c", h=2, a=2)
                    if lev < 5:
                        self.cp("act" if i else "dve", RPr[:, 2 * i:2 * i + 2, 0, :], pv[:, :, 0, :], [self.pk(bk)], [K(d, "R")])
                    if lev >= 1:
                        self.tt("dve", RPr[:, 2 * i:2 * i + 2, 1, :], t["RP"][:, 2 * i:2 * i + 2, 1, :], pv[:, :, 1, :], ALU.add,
                                [K(d, "P"), self.pk(bk)], [K(d, "P")])
                    yield
            for h in range(4):
                self.tr(self.psb[X2][:, h * 128:(h + 1) * 128], QKV[:, 8 + h, t0:t0 + 128], IDB, [qk(8 + h), kC], [self.pk(X2)])
            yield
            self.tt("dve", t["vb32"], pb4(X2, 0), bcs(gb, 0, hd0), ALU.mult, [self.pk(X2), kgs], [K(d, "E1"), K(d, "E3")])
            yield
            for h in range(4):
                self.mm(self.ps[X3][:, h * 128:(h + 1) * 128], t["RP"][:, h, 1, :], t["vb32"][:, h, :], True, True,
                        [K(d, "P"), K(d, "E1"), K(d, "E3")], [self.pk(X3)])
            for h in range(4):
                self.mm(self.ps[X1][:, h * 128:(h + 1) * 128], t["kbg32"][:, h, :], t["RP"][:, h, 1, :], True, True,
                        [K(d, "P"), K(d, "kbg")], [self.pk(X1)])
            yield
            self.cp("act", t["u", par], p4(X3), [self.pk(X3)], [K(d, "u", par)])
            self.cp("dve", t["wT32", par], p4(X1), [self.pk(X1)], [K(d, "wT32", par)])
            yield

        def scan(d, gb, par, first_o, tail=()):
            t = T_[d]
            hd0 = 4 * d
            X0 = 4 * d
            kS = DNb.k("S", d)
            kSb = DNb.k("Sb", d)
            wT, u, IT, qdT, kdec = t["wT32", par], t["u", par], t["IT", par], t["qdT", par], t["kdec", par]
            for c in ((0, 1) if d == 0 else (1, 0)):
                r0, r1 = c * 64, (c + 1) * 64
                for h in range(4):
                    self.mm(self.ps[X0][r0:r1, h * 128:(h + 1) * 128], wT[:, h, r0:r1], S[:, hd0 + h, :], True, True,
                            [K(d, "wT32", par), kS], [self.pk(X0)])
                yield
                self.tt("dve", t["vnew"][r0:r1].rearrange("p h c -> p (h c)"), u[r0:r1].rearrange("p h c -> p (h c)"),
                        self.ps[X0][r0:r1, 0:512], ALU.subtract, [K(d, "u", par), self.pk(X0)], [K(d, "vnew")])
                yield
                for h in range(4):
                    self.mm(self.ps[X0][r0:r1, h * 128:(h + 1) * 128], qdT[:, h, r0:r1], Sb[:, hd0 + h, :], h == 0, False,
                            [K(d, "qdT", par), kSb], [self.pk(X0)], skip=True)
                for h in range(4):
                    self.mm(self.ps[X0][r0:r1, h * 128:(h + 1) * 128], IT[r0:r1, h, r0:r1], t["vnew"][r0:r1, h, :], False, True,
                            [K(d, "IT", par), K(d, "vnew")], [self.pk(X0)], skip=True)
                yield
                ko = OTb.k(gb, c)
                if first_o:
                    self.cp("act", OT[r0:r1, gb, :], self.ps[X0][r0:r1, 0:512], [self.pk(X0)], [ko])
                else:
                    self.tt("dve", OT[r0:r1, gb, :], OT[r0:r1, gb, :], self.ps[X0][r0:r1, 0:512], ALU.add, [ko, self.pk(X0)], [ko])
                yield
                for h in range(4):
                    self.mm(self.ps[X0][:, h * 128:(h + 1) * 128], kdec[r0:r1, h, :], t["vnew"][r0:r1, h, :], True, True,
                            [K(d, "kdec", par), K(d, "vnew")], [self.pk(X0)])
                yield
                for h in range(4):
                    hd = hd0 + h
                    self.stt("dve", S[:, hd, :], S[:, hd, :], EGL[:, gb, c * 8 + hd:c * 8 + hd + 1], self.ps[X0][:, h * 128:(h + 1) * 128],
                             ALU.mult, ALU.add, [kS, self.pk(X0)], [kS], sreads=[GTSb.k("egl", gb)])
                self.cp("act", Sb[:, hd0:hd0 + 4, :], S[:, hd0:hd0 + 4, :], [kS], [kSb])
                yield
            for (pgb, slot) in tail:
                post_b(pgb, slot, X0)
                yield

        import itertools

        def run_il(gens):
            for _ in itertools.zip_longest(*gens):
                pass

        steps = []
        for si, (so, T) in enumerate(g["seqs"]):
            nblk = T // 128
            for n in range(nblk):
                steps.append((si, so // 128, nblk, n))
        blk_of = lambda st, d: st[1] + (st[3] if d == 0 else st[2] - 1 - st[3])

        def init_state(si):
            for d in range(2):
                kS, kSb = DNb.k("S", d), DNb.k("Sb", d)
                if g["s0"] is None:
                    self.memset("pool", S[:, 4 * d:4 * d + 4, :], 0.0, [kS])
                    self.memset("pool", Sb[:, 4 * d:4 * d + 4, :], 0.0, [kSb])
                else:
                    self.dma("sp", S[:, 4 * d:4 * d + 4, :], g["s0"][4 * d:4 * d + 4].rearrange("h k v -> k h v"), (), [kS])
                    self.cp("act", Sb[:, 4 * d:4 * d + 4, :], S[:, 4 * d:4 * d + 4, :], [kS], [kSb])

        ONb = s.alloc("ON", 143 * KB, 3 * KB)
        ON = [self.view(143 * KB + i * 1024, BF16, [128, 4, 128]) for i in range(2)]
        JK = self.view(145 * KB, BF16, [128, 128])

        def post_a(gb, slot):
            ko = [OTb.k(gb, 0), OTb.k(gb, 1)]
            c0 = (self.ss_ctr // 4 * 4 + 4) % 64
            self.ss_ctr = c0 + 4
            kss = self.CONSTB.k("ss4", c0)
            for h in range(4):
                self.act(JK, OT[:, gb, h * 128:(h + 1) * 128], AF.Square, ko, [ONb.k("jk"), kss], accum_out=self.SS[:, c0 + h:c0 + h + 1])
            yield
            self.ts("pool", self.SS[:, c0:c0 + 4], self.SS[:, c0:c0 + 4], 1.0 / 128, EPS, ALU.mult, ALU.add, [kss], [kss])
            self.tt("pool", self.SS[:, c0:c0 + 4], self.SS[:, c0:c0 + 4], self.CF("NEG05").to_broadcast([128, 4]), ALU.pow, [kss, kC], [kss])
            yield
            self.tt("dve", ON[slot], OT[:, gb, :].rearrange("p (h c) -> p h c", h=4),
                    self.SS[:, c0:c0 + 4].unsqueeze(2).to_broadcast([128, 4, 128]), ALU.mult, ko + [kss], [ONb.k(slot)])
            yield

        def post_b(gb, slot, bank):
            t0 = gb * 128
            for h in range(4):
                self.tr(self.psb[bank][:, h * 128:(h + 1) * 128], ON[slot][:, h, :], IDB, [ONb.k(slot), kC], [self.pk(bank)])
            self.stt("dve", ZS[:, 0:4, t0:t0 + 128], pb4(bank, 0), self.VT[:, 0, 18:19], ZS[:, 0:4, t0:t0 + 128], ALU.mult, ALU.mult,
                     [self.pk(bank), ZSb.k(gb)], [ZSb.k(gb)], sreads=[kVT])

        vcount = {}
        pending = []
        run_il([pre(d, blk_of(steps[0], d), 0) for d in range(2)])
        for i, st in enumerate(steps):
            si, gb0, nblk, n = st
            if n == 0:
                init_state(si)
            gl = []
            done_now = []
            for d in range(2):
                gb = blk_of(st, d)
                tail = [(pending[d], d)] if d < len(pending) else []
                gl.append(scan(d, gb, i % 2, gb not in visited, tail))
                visited.add(gb)
                vcount[gb] = vcount.get(gb, 0) + 1
                if vcount[gb] == 2:
                    done_now.append(gb)
            if i + 1 < len(steps):
                gl += [pre(d, blk_of(steps[i + 1], d), (i + 1) % 2) for d in range(2)]
            gl += [post_a(pgb, j) for j, pgb in enumerate(pending)]
            pending = done_now
            run_il(gl)
            if n == nblk - 1 and g["ns"] is not None:
                op = self.dma("sp", self.O["ns"][g["ns"][si]].rearrange("h k v -> k h v"), S, [DNb.k("S", 0), DNb.k("S", 1)], ())
                self.out_dmas.append(op)
        run_il([post_a(pgb, j) for j, pgb in enumerate(pending)])
        for j, pgb in enumerate(pending):
            post_b(pgb, j, 4 * j)
        s.retire(ONb)
        s.retire(DNb)
        WPb.after = s.collect(DN2b.keys)
        base = 112 * KB
        s.retire(OTb)
        s.retire(QKVb)
        s.retire(GTSb)
        g["YA"], g["YAb"] = ZS, ZSb
        if self.taps is not None:
            nm = g["name"]
            for h in (0, 2):
                self.tap("ya%d_%s" % (h, nm), ZS[:, h, 0:Tg], [128, Tg], dtype=BF16, reads=[ZSb.k(b) for b in range(Tg // 128)])

    def build_grow(self, Gt, v, dst_off, buf, tmp_off, tmpbuf, dgkey=None):
        dst = self.view(dst_off, F32, [128, D])
        DG = [self.view(tmp_off + i * 512, F32, [128, 128]) for i in range(2)]
        banks = [self.next_bank(), self.next_bank()]
        for k in range(8):
            r = k % 2
            kdg = dgkey if dgkey is not None else tmpbuf.k("dg", r)
            self.ts("dve", DG[r], self.CF("IDENT"), Gt[:, k, v:v + 1], None, ALU.mult, None, [self.kC], [kdg], sreads=[self.kD])
            b = banks[k // 4]
            self.mm(self.ps[b][:, (k % 4) * 128:(k % 4 + 1) * 128], self.CF("ONES"), DG[r], True, True, [kdg, self.kC], [self.pk(b)])
        for hf in range(2):
            self.cp("act" if hf else "dve", dst[:, hf * 512:(hf + 1) * 512], self.ps[banks[hf]][:, 0:512], [self.pk(banks[hf])], [buf.k("grow")])
        return dst

    def norm_residual(self, banks, resid, resid_keys, grow, kgrow, dst, dst_keys, T1, kT1):
        c0 = (self.ss_ctr // 2 * 2 + 2) % 64
        self.ss_ctr = c0 + 2
        kss = self.CONSTB.k("ss2", c0)
        SS = self.SS
        for hf in range(2):
            self.act(T1[hf], self.ps[banks[hf]][:, 0:512], AF.Square, [self.pk(banks[hf])], [kT1[hf], kss], accum_out=SS[:, c0 + hf:c0 + hf + 1])
        self.tt("pool", SS[:, c0:c0 + 1], SS[:, c0:c0 + 1], SS[:, c0 + 1:c0 + 2], ALU.add, [kss], [kss])
        self.rstd_ops(c0, kss, 1.0 / D, kss)
        for hf in range(2):
            self.stt("dve", T1[hf], self.ps[banks[hf]][:, 0:512], SS[:, c0:c0 + 1], grow[:, hf * 512:(hf + 1) * 512], ALU.mult, ALU.mult,
                     [self.pk(banks[hf]), kgrow], [kT1[hf]], sreads=[kss])
            self.tt("pool", dst[:, hf * 512:(hf + 1) * 512], T1[hf], resid[:, hf * 512:(hf + 1) * 512], ALU.add,
                    [kT1[hf]] + list(resid_keys), list(dst_keys))

    def phase_m1(self, g):
        s, I = self.s, self.I
        Tg = g["Tg"]
        NT = Tg // 512
        self.phase_ht(g)
        HT, HTb, YA, YAb, YB, YBb = g["HT"], g["HTb"], g["YA"], g["YAb"], g["YB"], g["YBb"]
        MTb = s.alloc("MT", 64 * KB, 32 * KB)
        MT = self.view(64 * KB, BF16, [128, 8, 2048])
        SCb = s.alloc("M1S", 96 * KB, 12 * KB)
        SA = [self.view(96 * KB + i * KB, BF16, [128, 512]) for i in range(4)]
        MM = [self.view(100 * KB + i * 2 * KB, F32, [128, 512]) for i in range(4)]
        WPb, WPv = self.wp_views()
        w_in_v = I["w_in"].rearrange("(k p) c -> p k c", p=128)
        w_a_v = I["w_a"].rearrange("(k p) c -> p k c", p=128)
        w_b_v = I["w_b"].rearrange("(k p) c -> p k c", p=128)
        it = 0

        def m1_load(pp):
            c = 256 * pp
            return (self.wp_load([(0, 8, 256, w_in_v[:, :, 2576 + c:2576 + c + 256])]),
                    self.wp_load([(0, 8, 256, w_in_v[:, :, 3600 + c:3600 + c + 256])]),
                    self.wp_load([(0, 4, 256, w_a_v[:, :, c:c + 256]), (4, 4, 256, w_b_v[:, :, c:c + 256])]))

        nxt = m1_load(0)
        for pp in range(4):
            sl = nxt
            if pp + 1 < 4:
                nxt = m1_load(pp + 1)
            for tt in range(NT):
                t0 = tt * 512
                hk = [HTb.k(b) for b in range(t0 // 128, t0 // 128 + 4)]
                yak = [YAb.k(b) for b in range(t0 // 128, t0 // 128 + 4)]
                ybk = [YBb.k(b) for b in range(t0 // 128, t0 // 128 + 4)]
                for oc2 in range(2):
                    oc = pp * 2 + oc2
                    cs = slice(oc2 * 128, (oc2 + 1) * 128)
                    bGA, bGB, bPA, bPB = [self.next_bank() for _ in range(4)]
                    for kc in range(8):
                        self.mm(self.ps[bGA][:, 0:512], WPv[sl[0]][:, kc, cs], HT[:, kc, t0:t0 + 512], kc == 0, kc == 7, [WPb.k(sl[0])] + hk, [self.pk(bGA)])
                    for kc in range(8):
                        self.mm(self.ps[bGB][:, 0:512], WPv[sl[1]][:, kc, cs], HT[:, kc, t0:t0 + 512], kc == 0, kc == 7, [WPb.k(sl[1])] + hk, [self.pk(bGB)])
                    for kc in range(4):
                        self.mm(self.ps[bPA][:, 0:512], WPv[sl[2]][:, kc, cs], YA[:, kc, t0:t0 + 512], kc == 0, kc == 3, [WPb.k(sl[2])] + yak, [self.pk(bPA)])
                    for kc in range(4):
                        self.mm(self.ps[bPB][:, 0:512], WPv[sl[2]][:, 4 + kc, cs], YB[:, kc, t0:t0 + 512], kc == 0, kc == 3, [WPb.k(sl[2])] + ybk, [self.pk(bPB)])
                    r = (it % 2) * 2
                    it += 1
                    self.act(SA[r], self.ps[bGA][:, 0:512], AF.Sigmoid, [self.pk(bGA)], [SCb.k("sa", r)])
                    self.act(SA[r + 1], self.ps[bGB][:, 0:512], AF.Sigmoid, [self.pk(bGB)], [SCb.k("sa", r + 1)])
                    self.tt("dve", MM[r], SA[r], self.ps[bPA][:, 0:512], ALU.mult, [SCb.k("sa", r), self.pk(bPA)], [SCb.k("mm", r)])
                    self.tt("dve", MM[r + 1], SA[r + 1], self.ps[bPB][:, 0:512], ALU.mult, [SCb.k("sa", r + 1), self.pk(bPB)], [SCb.k("mm", r + 1)])
                    self.tt("pool", MT[:, oc, t0:t0 + 512], MM[r], MM[r + 1], ALU.add, [SCb.k("mm", r), SCb.k("mm", r + 1)],
                            [MTb.k(b) for b in range(t0 // 128, t0 // 128 + 4)])
        for b_ in (SCb, HTb, YAb, YBb):
            s.retire(b_)
        g["MT"], g["MTb"] = MT, MTb
        if self.taps is not None:
            nm = g["name"]
            for oc in (0, 5):
                self.tap("mt%d_%s" % (oc, nm), MT[:, oc, 0:Tg], [128, Tg], dtype=BF16, reads=[MTb.k(b) for b in range(Tg // 128)])

    def phase_m2(self, g):
        s, I = self.s, self.I
        Tg, v = g["Tg"], g["v"]
        NB = Tg // 128
        MT, MTb = g["MT"], g["MTb"]
        WOb = s.alloc("WO", 96 * KB, 16 * KB)
        WO = self.view(96 * KB, BF16, [128, 8, 1024])
        w_o_v = I["w_o"].rearrange("(k p) c -> p k c", p=128)
        for hf in range(2):
            self.dma("pool", WO[:, :, hf * 512:(hf + 1) * 512], w_o_v[:, :, hf * 512:(hf + 1) * 512], (), [WOb.k(hf)])
        X1b = s.alloc("X1", 0, 64 * KB)
        X1 = self.view(0, F32, [128, 16, D])
        Mb = s.alloc("M2S", 112 * KB, 20 * KB)
        XL = [self.view(112 * KB + i * 4 * KB, F32, [128, D]) for i in range(2)]
        G1row = self.build_grow(self.G1, v, 120 * KB, Mb, 124 * KB, Mb)
        T1 = [[self.view(126 * KB + (i * 2 + hf) * 2 * KB, F32, [128, 512]) for hf in range(2)] for i in range(2)]
        for blk in range(NB):
            r = blk % 2
            kx = Mb.k("xl", r)
            self.dma("sp", XL[r], g["x"][blk * 128:(blk + 1) * 128, :], (), [kx])
            banks = [self.next_bank(), self.next_bank()]
            for hf in range(2):
                for kc in range(8):
                    self.mm(self.ps[banks[hf]][:, 0:512], MT[:, kc, blk * 128:(blk + 1) * 128], WO[:, kc, hf * 512:(hf + 1) * 512], kc == 0, kc == 7,
                            [MTb.k(blk), WOb.k(hf)], [self.pk(banks[hf])])
            self.norm_residual(banks, XL[r], [kx], G1row, Mb.k("grow"), X1[:, blk, :], [X1b.k(blk)], T1[r], [Mb.k("t1", r, 0), Mb.k("t1", r, 1)])
        for b_ in (Mb, MTb, WOb):
            s.retire(b_)
        g["X1"], g["X1b"] = X1, X1b
        if self.taps is not None:
            nm = g["name"]
            self.tap("x1_" + nm, X1[:, 0:NB, :].rearrange("p a b -> p (a b)"), [128, NB * D], reads=[X1b.k(b) for b in range(NB)])

    def phase_ffn(self, g):
        s, I = self.s, self.I
        Tg, v = g["Tg"], g["v"]
        X1, X1b = g["X1"], g["X1b"]
        WPb, WPv = self.wp_views()
        WDb = s.alloc("WD", 64 * KB, 44 * KB)
        WD = self.view(64 * KB, BF16, [128, 22, 1024])
        w_d_v = I["w_down"].rearrange("(k p) c -> p k c", p=128)
        for j0 in range(0, 22, 4):
            j1 = min(j0 + 4, 22)
            self.dma("pool", WD[:, j0:j1, :], w_d_v[:, j0:j1, :], (), [WDb.k(j) for j in range(j0, j1)])
        ATb = s.alloc("ACTT", 108 * KB, 22 * KB)
        ACTT = self.view(108 * KB, BF16, [128, 22, 512])
        H2b = s.alloc("H2T", 130 * KB, 8 * KB)
        H2T = self.view(130 * KB, BF16, [128, 8, 512])
        Fb = s.alloc("FSCR", 138 * KB, 14 * KB)
        XN = self.view(138 * KB, BF16, [128, D])
        YO = [self.view(140 * KB + i * 4 * KB, F32, [128, D]) for i in range(2)]
        Gb = Fb
        G2row = self.build_grow(self.G2, v, 148 * KB, Fb, 138 * KB, Fb, dgkey=Fb.k("xn"))
        SG = [YO[0][:, 0:512], YO[0][:, 512:1024]]
        T1 = SG
        w_up_v = I["w_up"].rearrange("(k p) c -> p k c", p=128)
        A = self.A2[:, :, v]
        B = self.MODT[:, 24:32, v]
        it = 0
        NU = Tg // 512

        def h2t_block(un, bi):
            blk = un * 4 + bi
            self.norm_transpose_block(X1[:, blk, :], [X1b.k(blk)], XN, Fb.k("xn"), H2T[:, :, bi * 128:(bi + 1) * 128], H2b.k(bi), A, B)

        for bi in range(4):
            h2t_block(0, bi)
        def up_load(jp):
            return (self.wp_load([(0, 8, 256, w_up_v[:, :, 256 * jp:256 * jp + 256])]),
                    self.wp_load([(0, 8, 256, w_up_v[:, :, DFF + 256 * jp:DFF + 256 * jp + 256])]))

        q = [up_load(0), up_load(1)]
        for un in range(NU):
            hk = [H2b.k(bi) for bi in range(4)]
            for jp in range(11):
                sg_, su_ = q.pop(0)
                if jp + 2 < 11:
                    q.append(up_load(jp + 2))
                for jj in range(2):
                    j = jp * 2 + jj
                    cs = slice(jj * 128, (jj + 1) * 128)
                    bG, bU = self.next_bank(), self.next_bank()
                    for kc in range(8):
                        self.mm(self.ps[bG][:, 0:512], WPv[sg_][:, kc, cs], H2T[:, kc, :], kc == 0, kc == 7, [WPb.k(sg_)] + hk, [self.pk(bG)])
                    for kc in range(8):
                        self.mm(self.ps[bU][:, 0:512], WPv[su_][:, kc, cs], H2T[:, kc, :], kc == 0, kc == 7, [WPb.k(su_)] + hk, [self.pk(bU)])
                    r = it % 2
                    it += 1
                    self.act(SG[r], self.ps[bG][:, 0:512], AF.Silu, [self.pk(bG)], [Gb.k("sg", r)])
                    self.tt("dve", ACTT[:, j, :], SG[r], self.ps[bU][:, 0:512], ALU.mult, [Gb.k("sg", r), self.pk(bU)], [ATb.k(j)])
            if un + 1 < NU:
                q = [up_load(0), up_load(1)]
            for bi in range(4):
                blk = un * 4 + bi
                if un + 1 < NU:
                    nblk_ = (un + 1) * 4 + bi
                    self.norm_stage_a(X1[:, nblk_, :], [X1b.k(nblk_)], XN, Fb.k("xn"))
                banks = [self.next_bank(), self.next_bank()]
                for hf in range(2):
                    for j in range(22):
                        self.mm(self.ps[banks[hf]][:, 0:512], ACTT[:, j, bi * 128:(bi + 1) * 128], WD[:, j, hf * 512:(hf + 1) * 512], j == 0, j == 21,
                                [ATb.k(j), WDb.k(j)], [self.pk(banks[hf])])
                if un + 1 < NU:
                    self.norm_stage_b(XN, Fb.k("xn"), H2T[:, :, bi * 128:(bi + 1) * 128], H2b.k(bi), A, B)
                yo = YO[1]
                kyo = Fb.k("yo", 1)
                self.norm_residual(banks, X1[:, blk, :], [X1b.k(blk)], G2row, Fb.k("grow"), yo, [kyo], T1, [Gb.k("sg", 0), Gb.k("sg", 1)])
                op = self.dma("sp", g["y"][blk * 128:(blk + 1) * 128, :], yo, [kyo], ())
                self.out_dmas.append(op)
        for b_ in (WDb, ATb, H2b, Fb, X1b):
            s.retire(b_)


_PROG_CACHE = {}


def _get_prog(stop_after=None, taps=False):
    key = (stop_after, taps)
    if key not in _PROG_CACHE:
        p = Prog(stop_after=stop_after, taps=[] if taps else None)
        p.build()
        _PROG_CACHE[key] = p
    return _PROG_CACHE[key]


def _in_maps(inputs):
    f = lambda a: np.ascontiguousarray(np.asarray(a, dtype=np.float32))
    common = {
        "w_mod": f(inputs["w_mod"][0]), "b_mod": f(inputs["b_mod"][0]),
        "g_pre_mix": f(inputs["g_pre_mix"][0]), "g_post_mix": f(inputs["g_post_mix"][0]),
        "g_pre_ffn": f(inputs["g_pre_ffn"][0]), "g_post_ffn": f(inputs["g_post_ffn"][0]),
        "w_in": f(inputs["w_in"][0]), "conv_w": f(inputs["conv_w"][0]),
        "a_log": f(inputs["a_log"][0]).reshape(8), "dt_bias": f(inputs["dt_bias"][0]).reshape(8),
        "g_onorm": f(inputs["g_onorm"][0]), "w_a": f(inputs["w_a_proj"][0]), "w_pool": f(inputs["w_pool"][0]),
        "pool_scale": f(inputs["pool_scale"][0]), "w_b": f(inputs["w_b_proj"][0]), "w_o": f(inputs["w_o"][0]),
        "w_up": f(inputs["w_up"][0]), "w_down": f(inputs["w_down"][0]),
        "constf": CF_ARR, "constb": CB_ARR,
    }
    xp = f(inputs["x_prompt"])
    xs = f(inputs["x_sample"])
    sd = f(inputs["state_delta"])
    c = f(inputs["c"])
    cc = f(inputs["c_ctx"])
    maps = []
    for i in range(8):
        m = dict(common)
        m["xp"] = np.ascontiguousarray(xp[2 * i:2 * i + 2].reshape(512, D))
        m["xs"] = np.ascontiguousarray(xs[i])
        m["s0"] = np.ascontiguousarray(sd[i, 0].reshape(8, 128, 128))
        m["cv"] = np.ascontiguousarray(np.stack([cc, c[i]], axis=0))
        maps.append(m)
    return maps


def kernel(**inputs):
    prog = _get_prog()
    res = run_bass_kernel_spmd(prog.nc, _in_maps(inputs), core_ids=list(range(8)))
    r = res.results
    y_prompt = np.concatenate([r[i]["yp"].reshape(2, 256, D) for i in range(8)], axis=0).astype(np.float32)
    y_sample = np.stack([r[i]["ys"] for i in range(8)], axis=0).astype(np.float32)
    ns = np.concatenate([r[i]["ns"].reshape(2, 1, 2, 4, 128, 128) for i in range(8)], axis=0).astype(np.float32)
    return (y_prompt, y_sample, ns)
```
